# Optimizing a Trainium2 kernel written in Bass

```python
import math
import jax, jax.numpy as jnp
from jax import lax
import numpy as np

D_MODEL = 2048
BATCH = 4
SEQ = 4096
DEPTH = 1

CHUNK = 128
A_WIDTH = D_MODEL
A_GROUPS = 8
A_GROUP_DIM = A_WIDTH // A_GROUPS
R_HEADS = 8
R_QK_DIM = D_MODEL // (2 * R_HEADS)
R_V_DIM = D_MODEL // R_HEADS
R_QK_WIDTH = R_HEADS * R_QK_DIM
R_V_WIDTH = R_HEADS * R_V_DIM
ROPE_BASE = 10000.0
LN_EPS = 1e-5
DEEPNORM_ALPHA = (2 * DEPTH) ** 0.25
DEEPNORM_BETA = (8 * DEPTH) ** -0.25

IN_WIDTHS = (A_WIDTH, A_WIDTH, A_WIDTH,
             R_QK_WIDTH, R_QK_WIDTH, R_V_WIDTH, R_V_WIDTH,
             D_MODEL, D_MODEL)
IN_WIDTH = sum(IN_WIDTHS)
IN_SPLITS = tuple(int(s) for s in np.cumsum(IN_WIDTHS)[:-1])

kernel_name = "hybrid_gmlp_retention_gated_deepnorm"


def standardize(x):
    xf = x.astype(jnp.float32)
    mu = jnp.mean(xf, axis=-1, keepdims=True)
    var = jnp.mean(jnp.square(xf - mu), axis=-1, keepdims=True)
    return ((xf - mu) * lax.rsqrt(var + LN_EPS)).astype(x.dtype)


def layer_norm(x, g, b):
    return standardize(x) * g + b


def rotary(x, positions):
    d = x.shape[-1]
    freqs = ROPE_BASE ** (-jnp.arange(0, d, 2, dtype=jnp.float32) / d)
    ang = positions.astype(jnp.float32)[:, None] * freqs[None, :]
    cos = jnp.cos(ang).astype(x.dtype)[None, :, None, :]
    sin = jnp.sin(ang).astype(x.dtype)[None, :, None, :]
    x1, x2 = x[..., : d // 2], x[..., d // 2:]
    return jnp.concatenate([x1 * cos - x2 * sin, x1 * sin + x2 * cos], axis=-1)


def spatial_gating(u, v, ln_g, ln_b, w_s, b_s):
    bsz, s, _ = v.shape
    n = s // CHUNK
    v = layer_norm(v, ln_g, ln_b).reshape(bsz, n, CHUNK, A_GROUPS, A_GROUP_DIM)
    causal = jnp.tril(jnp.ones((CHUNK, CHUNK), dtype=w_s.dtype))
    ws = w_s * causal[None]
    sv = jnp.einsum('gts,bnsgc->bntgc', ws, v) + b_s.T[None, None, :, :, None]
    return u * sv.reshape(bsz, s, A_WIDTH)


def retention(q, k, v):
    bsz, s, h, dk = q.shape
    dv = v.shape[-1]
    n = s // CHUNK
    dt = q.dtype
    log_gamma = jnp.log1p(-jnp.exp2(-5.0 - jnp.arange(h, dtype=jnp.float32)))
    idx = jnp.arange(CHUNK, dtype=jnp.float32)
    q = q.reshape(bsz, n, CHUNK, h, dk)
    k = (k * (dk ** -0.5)).reshape(bsz, n, CHUNK, h, dk)
    v = v.reshape(bsz, n, CHUNK, h, dv)
    rel = idx[:, None] - idx[None, :]
    decay = jnp.where(rel[None] >= 0,
                      jnp.exp(log_gamma[:, None, None] * jnp.maximum(rel, 0.0)[None]), 0.0).astype(dt)
    scores = jnp.einsum('bnthd,bnshd->bnhts', q, k) * decay[None, None]
    inner = jnp.einsum('bnhts,bnshe->bnthe', scores, v)
    w_state = jnp.exp(log_gamma[None, :] * (CHUNK - 1 - idx)[:, None]).astype(dt)
    kv = jnp.einsum('bnshd,bnshe->bnhde', k * w_state[None, None, :, :, None], v)
    chunk_decay = jnp.exp(log_gamma * CHUNK).astype(kv.dtype)[None, :, None, None]

    def step(state, kv_i):
        return state * chunk_decay + kv_i, state

    init = jnp.zeros((bsz, h, dk, dv), dtype=kv.dtype)
    _, r_prev = lax.scan(step, init, jnp.moveaxis(kv, 1, 0))
    r_prev = jnp.moveaxis(r_prev, 0, 1)
    cross_decay = jnp.exp(log_gamma[None, :] * (idx + 1.0)[:, None]).astype(dt)
    cross = jnp.einsum('bnthd,bnhde->bnthe', q, r_prev) * cross_decay[None, None, :, :, None]
    out = standardize(inner + cross)
    return out.reshape(bsz, s, h * dv)


def setup_inputs(seed: int = 0) -> dict:
    key = jax.random.key(seed)
    ks = jax.random.split(key, 13)
    f32 = jnp.float32
    nrm = lambda k, shape: jax.random.normal(k, shape, dtype=f32)
    x = nrm(ks[0], (BATCH, SEQ, D_MODEL))
    w_in = nrm(ks[1], (DEPTH, D_MODEL, IN_WIDTH)) * D_MODEL ** -0.5
    b_gate = 0.01 * nrm(ks[2], (DEPTH, 2 * D_MODEL))
    ln_v_g = 1.0 + 0.02 * nrm(ks[3], (DEPTH, A_WIDTH))
    ln_v_b = 0.02 * nrm(ks[4], (DEPTH, A_WIDTH))
    w_s = nrm(ks[5], (DEPTH, A_GROUPS, CHUNK, CHUNK)) * CHUNK ** -0.5
    b_s = 1.0 + 0.02 * nrm(ks[6], (DEPTH, A_GROUPS, CHUNK))
    w_oa = nrm(ks[7], (DEPTH, A_WIDTH, D_MODEL)) * (A_WIDTH ** -0.5) * DEEPNORM_BETA
    w_ob = nrm(ks[8], (DEPTH, R_V_WIDTH, D_MODEL)) * (R_V_WIDTH ** -0.5) * DEEPNORM_BETA
    w_out = nrm(ks[9], (DEPTH, D_MODEL, D_MODEL)) * (D_MODEL ** -0.5) * DEEPNORM_BETA
    ln_g = 1.0 + 0.02 * nrm(ks[10], (DEPTH, D_MODEL))
    ln_b = 0.02 * nrm(ks[11], (DEPTH, D_MODEL))
    return {"x": x, "w_in": w_in, "b_gate": b_gate, "ln_v_g": ln_v_g, "ln_v_b": ln_v_b,
            "w_s": w_s, "b_s": b_s, "w_oa": w_oa, "w_ob": w_ob, "w_out": w_out,
            "ln_g": ln_g, "ln_b": ln_b}


def reference(x, w_in, b_gate, ln_v_g, ln_v_b, w_s, b_s, w_oa, w_ob, w_out, ln_g, ln_b):
    bsz, s, _ = x.shape
    positions = jnp.arange(s, dtype=jnp.int32)
    for l in range(DEPTH):
        h = x @ w_in[l]
        a_u, a_v, a_z, r_q, r_k, r_v, r_z, g_a, g_b = jnp.split(h, IN_SPLITS, axis=-1)
        ya = spatial_gating(jax.nn.gelu(a_u), jax.nn.gelu(a_v), ln_v_g[l], ln_v_b[l], w_s[l], b_s[l])
        ya = (ya * jax.nn.silu(a_z)) @ w_oa[l]
        q = rotary(r_q.reshape(bsz, s, R_HEADS, R_QK_DIM), positions)
        k = rotary(r_k.reshape(bsz, s, R_HEADS, R_QK_DIM), positions)
        ret = retention(q, k, r_v.reshape(bsz, s, R_HEADS, R_V_DIM))
        yb = (ret * jax.nn.silu(r_z)) @ w_ob[l]
        ga = jax.nn.sigmoid(g_a + b_gate[l, :D_MODEL])
        gb = jax.nn.sigmoid(g_b + b_gate[l, D_MODEL:])
        out = (ga * ya + gb * yb) @ w_out[l]
        x = layer_norm(DEEPNORM_ALPHA * x + out, ln_g[l], ln_b[l])
    return x
```

```python
import math
from contextlib import ExitStack

import numpy as np
import concourse.bass as bass
import concourse.mybir as mybir
from concourse.bass_utils import run_bass_kernel_spmd

F32 = mybir.dt.float32
BF16 = mybir.dt.bfloat16
AF = mybir.ActivationFunctionType
ALU = mybir.AluOpType

D = 2048
NCORES = 8
TOK = 2048
NCH = 4
T = 512
NT = TOK // T
KT = 16
NH = 8
ALPHA = float(2.0 ** 0.25)
EPS = 1e-5
NSLOT = 3
MAXP = 12
C_U, C_V, C_Z, C_Q, C_K, C_RV, C_RZ, C_GA, C_GB = 0, 2048, 4096, 6144, 7168, 8192, 10240, 12288, 14336

ENGS = ("pe", "act", "dve", "pool", "sp")


class _Op:
    __slots__ = ("idx", "eng", "fn", "deps", "dma_key", "signal", "count", "sem", "grp", "gcount")

    def __init__(self, idx, eng, fn, deps, dma_key, grp=None):
        self.idx = idx
        self.eng = eng
        self.fn = fn
        self.deps = deps
        self.dma_key = dma_key
        self.signal = False
        self.count = 0
        self.sem = None
        self.grp = grp if grp is not None else ("_", idx)
        self.gcount = 0


class Sched:
    def __init__(self, nc):
        self.nc = nc
        self.ops = []
        self.lastw = {}
        self.readers = {}

    def op(self, eng, fn, reads=(), writes=(), dma_key=None, dma_group=None):
        deps = set()
        for k in reads:
            w = self.lastw.get(k)
            if w is not None:
                deps.add(w)
            if k.startswith("PF"):
                for r in self.readers.get(k, ()):
                    if self.ops[r].eng != eng:
                        deps.add(r)
        for k in writes:
            w = self.lastw.get(k)
            if w is not None:
                deps.add(w)
            for r in self.readers.get(k, ()):
                deps.add(r)
        idx = len(self.ops)
        o = _Op(idx, eng, fn, deps, dma_key, dma_group)
        self.ops.append(o)
        for k in reads:
            self.readers.setdefault(k, []).append(idx)
        for k in writes:
            self.lastw[k] = idx
            self.readers[k] = []
        return idx

    def emit(self):
        nc = self.nc
        ops = self.ops
        for o in ops:
            if o.eng == "pe" and o.dma_key is None:
                o.deps = {d for d in o.deps
                          if not (ops[d].eng == "pe" and ops[d].dma_key is None)}
        for o in ops:
            for d in o.deps:
                ops[d].signal = True
        with ExitStack() as es:
            esem = {e: es.enter_context(nc.semaphore("s_" + e)) for e in ENGS}
            dsem = {}
            ecount = {e: 0 for e in ENGS}
            dcount = {}
            for o in ops:
                if o.dma_key is not None:
                    if o.dma_key not in dsem:
                        dsem[o.dma_key] = es.enter_context(nc.semaphore("d_%d" % len(dsem)))
                        dcount[o.dma_key] = 0
                    dcount[o.dma_key] += 16
                    o.sem = dsem[o.dma_key]
                    o.count = dcount[o.dma_key]
                    o.signal = True
                elif o.signal:
                    ecount[o.eng] += 1
                    o.sem = esem[o.eng]
                    o.count = ecount[o.eng]
            self.stats = dict(ecount)
            self.stats["n_dma_sems"] = len(dsem)
            self.stats["n_ops"] = len(ops)
            streams = {e: [o for o in ops if o.eng == e] for e in ENGS}
            gend = {}
            for o in ops:
                if o.dma_key is not None:
                    k = (o.dma_key, o.grp)
                    gend[k] = max(gend.get(k, 0), o.count)
            for o in ops:
                if o.dma_key is not None:
                    o.gcount = gend[(o.dma_key, o.grp)]

            def dma_count(p, cons_idx):
                return p.gcount

            def run(eng_name, e):
                waited = {}
                for o in streams[eng_name]:
                    need = {}
                    for d in o.deps:
                        p = ops[d]
                        sid = id(p.sem)
                        pc = p.count if p.dma_key is None else dma_count(p, o.idx)
                        if waited.get(sid, 0) < pc:
                            if sid not in need or need[sid][1] < pc:
                                need[sid] = (p.sem, pc)
                    for sid, (sem, cnt) in need.items():
                        e.wait_ge(sem, cnt)
                        waited[sid] = cnt
                    if o.fn is None:
                        continue
                    ins = o.fn(e)
                    if o.signal:
                        ins.then_inc(o.sem, 16 if o.dma_key is not None else 1)

            with nc.Block() as block:
                @block.tensor
                def _(e):
                    run("pe", e)

                @block.scalar
                def _(e):
                    run("act", e)

                @block.vector
                def _(e):
                    run("dve", e)

                @block.gpsimd
                def _(e):
                    run("pool", e)

                @block.sync
                def _(e):
                    run("sp", e)


def build_program(taps=None):
    nc = bass.Bass("TRN2", target_bir_lowering=False)

    def din(name, shape):
        return nc.dram_tensor(name, list(shape), F32, kind="ExternalInput").ap()

    x_d = din("x", [TOK, D])
    xp_d = din("xp", [TOK, D])
    w_in = din("w_in", [D, 16384])
    w_oa = din("w_oa", [D, D])
    w_ob = din("w_ob", [D, D])
    w_out = din("w_out", [D, D])
    b_gate = din("b_gate", [4096])
    ln_v_g = din("ln_v_g", [D])
    ln_v_b = din("ln_v_b", [D])
    w_s = din("w_s", [8, 128, 128])
    b_s = din("b_s", [8, 128])
    ln_g = din("ln_g", [D])
    ln_b = din("ln_b", [D])
    cs_main_d = din("cs_main", [128, 16, 128])
    cs_pre_d = din("cs_pre", [128, 16, 128])
    dec_d = din("dec", [128, 16])
    dkp_d = din("dkp", [128, 16, 8])
    maskT_d = din("maskT", [128, 128])
    maskts_d = din("mask_ts", [128, 128])
    out_d = nc.dram_tensor("out", [TOK, D], F32, kind="ExternalOutput").ap()

    gam = [1.0 - 2.0 ** (-5.0 - h) for h in range(NH)]
    gC = [g ** 128 for g in gam]
    gC1 = [g ** 127 for g in gam]

    with ExitStack() as es:
        es.enter_context(nc.allow_non_contiguous_dma(reason="small param column loads"))

        def sb(name, shape, dt):
            return es.enter_context(nc.sbuf_tensor("sb_" + name, list(shape), dt))

        xT = sb("xT", [128, KT, T], BF16)
        yaT = sb("yaT", [128, KT, T], BF16)
        X = sb("X", [128, 16384], BF16)
        vln = X[:, 0:8192].rearrange("p (c n) -> p c n", c=NCH)
        ybT = X[:, 0:8192].rearrange("p (k n) -> p k n", k=KT)
        preT = X[:, 8192:16384].rearrange("p (k n) -> p k n", k=KT)
        X32 = X[:].bitcast(F32)
        res = X32.rearrange("p (c n) -> p c n", c=NCH)
        wr = [sb("wr%d" % i, [128, KT, 512], BF16) for i in range(NSLOT)]
        lncg = sb("lncg", [128, D], F32)
        lncb = sb("lncb", [128, D], F32)
        xb = [sb("xb%d" % i, [128, D], BF16) for i in range(4)]
        R = sb("R", [128, NH, 256], F32)
        Rg = sb("Rg", [128, NH, 256], BF16)
        wsT = sb("wsT", [128, 8, 128], BF16)
        bias2 = sb("bias2", [128, 16, 128], F32)
        cs = sb("cs", [128, NCH, 128], F32)
        dec = sb("dec", [128, 16], F32)
        dkp = sb("dkp", [128, 16, 8], F32)
        maskT = sb("maskT", [128, 128], F32)
        ident = sb("ident", [128, 128], BF16)
        ones = sb("ones", [128, 128], BF16)
        lnvg = sb("lnvg", [128, 16], F32)
        lnvb = sb("lnvb", [128, 16], F32)
        bgate = sb("bgate", [128, 32], F32)
        scratch = sb("scratch", [128, 2], F32)
        tmpF = [sb("tmpF%d" % i, [128, 512], F32) for i in range(4)]
        qk32 = [sb("qk32_%d" % i, [128, 512], F32) for i in range(2)]
        rta = [sb("rta%d" % i, [128, 256], F32) for i in range(2)]
        rtb = [sb("rtb%d" % i, [128, 256], F32) for i in range(2)]
        qkt = [sb("qkt%d" % i, [128, NCH, 2, 128], BF16) for i in range(2)]
        qkT = [sb("qkT%d" % i, [128, 2, T], BF16) for i in range(2)]
        vh = [sb("vh%d" % i, [128, NCH, 256], BF16) for i in range(2)]
        ST = [sb("ST%d" % i, [128, NCH, 128], BF16) for i in range(2)]
        on32 = [sb("on32_%d" % i, [128, NCH, 256], BF16) for i in range(2)]
        ybp = [sb("ybp0", [128, NCH, 256], BF16)] * 2
        stv = sb("stv", [128, NCH, 4, 6], F32)
        st4 = [sb("st4_%d" % i, [128, 4, 6], F32) for i in range(2)]
        sm = [sb("sm%d" % i, [128, 8], F32) for i in range(8)]

        PS = es.enter_context(nc.psum_tensor("ps_all", [128, 8, 512], F32))

        def bank(i):
            return PS[:, i, :]

        def bank16(i):
            return PS[:, i, :].bitcast(BF16)

        def bk(i):
            return ["PF%d.a" % i, "PF%d.b" % i]

        def bkh(i, half):
            return ["PF%d.a" % i, "PF%d.b" % i]

        S = Sched(nc)
        cnt = {"tmp": 0, "sm": 0, "st4": 0, "pf": 0, "ev": 0}

        def next_tmp():
            i = cnt["tmp"] % 4
            cnt["tmp"] += 1
            return tmpF[i], "tmpF%d" % i

        def next_sm():
            i = cnt["sm"] % 8
            cnt["sm"] += 1
            return sm[i], "sm%d" % i

        def next_pf():
            i = cnt["pf"] % 6
            cnt["pf"] += 1
            return bank(i), bk(i)

        def ev_eng():
            cnt["ev"] += 1
            return "act" if cnt["ev"] % 2 else "dve"

        def copy_op(eng, out, in_, reads, writes):
            if eng == "act":
                S.op("act", lambda e: e.activation(out=out, in_=in_, func=AF.Identity), reads=reads, writes=writes)
            else:
                S.op("dve", lambda e: e.tensor_copy(out=out, in_=in_), reads=reads, writes=writes)

        def alias(old, new):
            S.op("dve", lambda e: e.memset(scratch[:, 0:1], 0.0), writes=list(old) + list(new) + ["scratch"])

        def wkeys(slot):
            return ["wr%d.%d" % (slot, j) for j in range(MAXP)]

        def mm_group(out, pairs, reads, writes):
            def fn(e):
                n = len(pairs)
                ins = None
                for i, (l, r) in enumerate(pairs):
                    ins = e.matmul(out, lhsT=l, rhs=r, start=(i == 0), stop=(i == n - 1))
                return ins
            S.op("pe", fn, reads=reads, writes=writes)

        def mm_each(items, reads, writes):
            def fn(e):
                ins = None
                for (o, l, r) in items:
                    ins = e.matmul(o, lhsT=l, rhs=r, start=True, stop=True)
                return ins
            S.op("pe", fn, reads=reads, writes=writes)

        def tr_group(items, reads, writes):
            def fn(e):
                ins = None
                for (o, i) in items:
                    ins = e.transpose(out=o, in_=i, identity=ident[:])
                return ins
            S.op("pe", fn, reads=reads + ["ident"], writes=writes)

        bsb = X32[:, 0:1024].rearrange("p (g t) -> p g t", g=8)
        Wb = X32[:, 1024:2048].rearrange("p (g t) -> p g t", g=8)
        ws32 = X32[:, 2048:3072].rearrange("p (g s) -> p g s", g=8)
        mts = X32[:, 3072:3200]
        wsm = X[:, 8192:9216].rearrange("p (g s) -> p g s", g=8)

        S.op("dve", lambda e: e.memset(tmpF[0][:, 0:128], 0.0), writes=["tmpF0"])
        S.op("pool", lambda e: e.affine_select(out=tmpF[0][:, 0:128], in_=tmpF[0][:, 0:128], pattern=[[-1, 128]],
                                                 compare_op=ALU.not_equal, fill=1.0, base=0, channel_multiplier=1),
             reads=["tmpF0"], writes=["tmpF0"])
        S.op("dve", lambda e: e.tensor_copy(out=ident[:], in_=tmpF[0][:, 0:128]), reads=["tmpF0"], writes=["ident"])
        S.op("dve", lambda e: e.memset(ones[:], 1.0), writes=["ones"])
        S.op("dve", lambda e: e.memset(R[:], 0.0), writes=["R%d" % h for h in range(NH)])

        def ld(dst, src, key):
            S.op("sp", lambda e: e.dma_start(out=dst, in_=src), writes=[key], dma_key=key)

        ld(dec[:], dec_d, "dec")
        ld(dkp[:], dkp_d, "dkp")
        ld(maskT[:], maskT_d, "maskT")
        ld(lnvg[:], ln_v_g.rearrange("(c p) -> p c", p=128), "lnvg")
        ld(lnvb[:], ln_v_b.rearrange("(c p) -> p c", p=128), "lnvb")
        ld(bgate[:], b_gate.rearrange("(c p) -> p c", p=128), "bgate")
        ld(lncg[:], ln_g.partition_broadcast(128), "lncg")
        ld(lncb[:], ln_b.partition_broadcast(128), "lncb")
        S.op("sp", lambda e: e.dma_start(out=X32[:, 0:1024], in_=b_s.rearrange("g t -> (g t)").partition_broadcast(128)),
             writes=["X.bsb"], dma_key="X.bsb")
        S.op("sp", lambda e: e.dma_start(out=ws32, in_=w_s.rearrange("g t s -> t g s")), writes=["X.ws32"], dma_key="X.ws32")
        S.op("sp", lambda e: e.dma_start(out=mts, in_=maskts_d), writes=["X.mts"], dma_key="X.mts")
        S.op("dve", lambda e: e.tensor_tensor(out=wsm, in0=ws32, in1=mts.unsqueeze(1).to_broadcast([128, 8, 128]), op=ALU.mult),
             reads=["X.ws32", "X.mts"], writes=["X.wsm"])
        tr_group([(bank16(6)[:, g * 128:(g + 1) * 128], wsm[:, g, :]) for g in range(8)], ["X.wsm"], bk(6))
        S.op("dve", lambda e: e.tensor_copy(out=wsT[:].rearrange("p g t -> p (g t)"), in_=bank16(6)), reads=bk(6), writes=["wsT"])
        for hlf in range(2):
            mm_group(bank(hlf), [(ones[:], wsT[:, hlf * 4:(hlf + 1) * 4, :].rearrange("p g t -> p (g t)"))],
                     ["ones", "wsT"], bk(hlf))
            S.op("dve", (lambda hlf: lambda e: e.tensor_copy(out=X32[:, 1024 + hlf * 512:1024 + (hlf + 1) * 512], in_=bank(hlf)))(hlf),
                 reads=bk(hlf), writes=["X.Wb%d" % hlf])
        for ct in range(16):
            g = ct // 2
            S.op("dve", (lambda ct, g: lambda e: e.scalar_tensor_tensor(
                out=bias2[:, ct, :], in0=Wb[:, g, :], scalar=lnvb[:, ct:ct + 1], in1=bsb[:, g, :],
                op0=ALU.mult, op1=ALU.add))(ct, g),
                reads=["X.Wb%d" % (g // 4), "lnvb", "X.bsb"], writes=["bias2"])
        VLN_K = ["vln.%d" % c for c in range(NCH)]
        PRE_K = ["preT.%d" % k for k in range(KT)]
        YBT_K = ["ybT.%d" % k for k in range(KT)]
        RES_K = ["res.%d" % c for c in range(NCH)]
        XT_K = ["xT.c%d.%d" % (c, hh) for c in range(NCH) for hh in range(2)]
        YAT_K = ["yaT.%d" % k for k in range(KT)]
        alias(["X.bsb", "X.ws32", "X.mts", "X.wsm", "X.Wb0", "X.Wb1"], VLN_K + PRE_K)

        xdone = set()

        def x_dma(src, tag, tile, c):
            if (tag, tile, c) in xdone:
                return
            xdone.add((tag, tile, c))
            b = c
            r0 = (tile * NCH + c) * 128
            S.op("pool", lambda e: e.dma_start(out=xb[b][:], in_=src[r0:r0 + 128, :]), writes=["xb%d" % b], dma_key="xb%d" % b)

        def build_xT(src, tag, tile):
            for c in range(NCH):
                x_dma(src, tag, tile, c)
                b = c
                for hlf in range(2):
                    bi_ = (2 * c + hlf) % 8
                    pb = bank16(bi_)
                    tr_group([(pb[:, j * 128:(j + 1) * 128], xb[b][:, (hlf * 8 + j) * 128:(hlf * 8 + j + 1) * 128])
                              for j in range(8)], ["xb%d" % b], bk(bi_))
                    copy_op("act" if tag == "m" else ev_eng(), xT[:, hlf * 8:(hlf + 1) * 8, c * 128:(c + 1) * 128],
                            pb.rearrange("p (k t) -> p k t", k=8), bk(bi_), ["xT.c%d.%d" % (c, hlf)])

        def load_cs(src_d, tile):
            S.op("sp", lambda e: e.dma_start(out=cs[:], in_=src_d[:, tile * NCH:(tile + 1) * NCH, :]), writes=["cs"], dma_key="cs")

        def rotary(x1, x2, cosv, sinv, d1, d2, ta, tb, srck, dstk, par):
            ka, kb = "rta%d" % par, "rtb%d" % par
            S.op("dve", lambda e: e.tensor_tensor(out=ta, in0=x1, in1=cosv, op=ALU.mult), reads=[srck, "cs"], writes=[ka])
            S.op("dve", lambda e: e.tensor_tensor(out=tb, in0=x2, in1=sinv, op=ALU.mult), reads=[srck, "cs"], writes=[kb])
            S.op("dve", lambda e: e.tensor_tensor(out=d1, in0=ta, in1=tb, op=ALU.subtract), reads=[ka, kb], writes=list(dstk))
            S.op("dve", lambda e: e.tensor_tensor(out=ta, in0=x1, in1=sinv, op=ALU.mult), reads=[srck, "cs"], writes=[ka])
            S.op("dve", lambda e: e.tensor_tensor(out=tb, in0=x2, in1=cosv, op=ALU.mult), reads=[srck, "cs"], writes=[kb])
            S.op("dve", lambda e: e.tensor_tensor(out=d2, in0=ta, in1=tb, op=ALU.add), reads=[ka, kb], writes=list(dstk))

        def prefix_proj(slot, h, ptile):
            hb = h % 2
            kb = 3 * hb
            for c in range(NCH):
                mm_group(bank(kb)[:, c * 128:(c + 1) * 128],
                         [(xT[:, kt, c * 128:(c + 1) * 128], wr[slot][:, kt, 0:128]) for kt in range(KT)],
                         ["xT.c%d.0" % c, "xT.c%d.1" % c] + wkeys(slot), bkh(kb, c // 2))
                vb = kb + 1 + c // 2
                mm_group(bank(vb)[:, (c % 2) * 256:(c % 2 + 1) * 256],
                         [(xT[:, kt, c * 128:(c + 1) * 128], wr[slot][:, kt, 128:384]) for kt in range(KT)],
                         ["xT.c%d.0" % c, "xT.c%d.1" % c] + wkeys(slot), bkh(vb, c % 2))
            k32 = qk32[hb][:].rearrange("p (c d) -> p c d", c=NCH)
            k32k = "qk32_%d" % hb
            for c in range(NCH):
                jj = ptile * NCH + c
                S.op("act", (lambda c, jj: lambda e: e.activation(out=k32[:, c, :], in_=bank(kb)[:, c * 128:(c + 1) * 128],
                                                                   func=AF.Identity, scale=dkp[:, jj, h:h + 1]))(c, jj),
                     reads=bkh(kb, c // 2) + ["dkp"], writes=[k32k])
            S.op("act", lambda e: e.activation(out=vh[hb][:], in_=PS[:, kb + 1:kb + 3, :].rearrange("p b (c e) -> p (b c) e", c=2),
                                               func=AF.Identity),
                 reads=bk(kb + 1) + bk(kb + 2), writes=["vh%d.%d" % (hb, c) for c in range(NCH)])
            ta = rta[hb][:].rearrange("p (c f) -> p c f", c=NCH)
            tb = rtb[hb][:].rearrange("p (c f) -> p c f", c=NCH)
            rotary(k32[:, :, 0:64], k32[:, :, 64:128], cs[:, :, 0:64], cs[:, :, 64:128],
                   qkt[hb][:, :, 0, 0:64], qkt[hb][:, :, 0, 64:128], ta, tb, k32k, ["qkt%d.0" % hb, "qkt%d.1" % hb], hb)

        def prefix_state(h):
            hb = h % 2
            for c in range(NCH):
                def fn(e, c=c):
                    return e.matmul(bank(6)[:, 0:256], lhsT=qkt[hb][:, c, 0, :], rhs=vh[hb][:, c, :],
                                    start=(c == 0), stop=(c == NCH - 1))
                S.op("pe", fn, reads=["qkt%d.0" % hb, "qkt%d.1" % hb, "vh%d.%d" % (hb, c)], writes=bkh(6, 0))
            S.op("dve", lambda e: e.tensor_tensor(out=R[:, h, :], in0=R[:, h, :], in1=bank(6)[:, 0:256], op=ALU.add),
                 reads=bkh(6, 0) + ["R%d" % h], writes=["R%d" % h])

        def prefix_step(slot, h, ptile):
            if h == 2:
                for c in range(NCH):
                    if ptile + 1 < NT:
                        x_dma(xp_d, "p", ptile + 1, c)
                    else:
                        x_dma(x_d, "m", 0, c)
            prefix_proj(slot, h, ptile)
            if h > 0:
                prefix_state(h - 1)

        def rstd_batch(stat_views, stat_keys, want_nb=False):
            n = len(stat_views)
            m, mk = next_sm()
            r, rk = next_sm()
            for c in range(n):
                S.op("dve", (lambda c: lambda e: e.bn_aggr(out=m[:, 2 * c:2 * c + 2], in_=stat_views[c]))(c),
                     reads=[stat_keys[c]], writes=[mk])
            var = m[:, 0:2 * n].rearrange("p (c t) -> p c t", t=2)[:, :, 1]
            mean = m[:, 0:2 * n].rearrange("p (c t) -> p c t", t=2)[:, :, 0]
            S.op("dve", lambda e: e.tensor_scalar(out=r[:, 0:n], in0=var, scalar1=EPS, scalar2=None, op0=ALU.add),
                 reads=[mk], writes=[rk])
            S.op("act", lambda e: e.activation(out=r[:, 0:n], in_=r[:, 0:n], func=AF.Sqrt), reads=[rk], writes=[rk])
            S.op("dve", lambda e: e.reciprocal(out=r[:, 0:n], in_=r[:, 0:n]), reads=[rk], writes=[rk])
            if want_nb:
                S.op("dve", lambda e: e.scalar_tensor_tensor(out=r[:, 4:4 + n], in0=mean, scalar=-1.0, in1=r[:, 0:n],
                                                             op0=ALU.mult, op1=ALU.mult), reads=[mk, rk], writes=[rk])
            return m, mk, r, rk

        def a1(slot, nb):
            for c in range(NCH):
                pf, pk = next_pf()
                mm_group(pf, [(xT[:, kt, c * 128:(c + 1) * 128], wr[slot][:, kt, :]) for kt in range(KT)],
                         ["xT.c%d.0" % c, "xT.c%d.1" % c] + wkeys(slot), pk)
                tf, tk = next_tmp()
                S.op("act", (lambda pf, tf: lambda e: e.activation(out=tf[:], in_=pf, func=AF.Gelu_apprx_tanh))(pf, tf),
                     reads=pk, writes=[tk])
                S.op("dve", (lambda tf, c: lambda e: e.bn_stats(out=stv[:, c, nb, :], in_=tf[:]))(tf, c),
                     reads=[tk], writes=["stv.%d" % c])
                S.op("dve", (lambda tf, c: lambda e: e.tensor_copy(out=vln[:, c, nb * 512:(nb + 1) * 512], in_=tf[:]))(tf, c),
                     reads=[tk], writes=["vln.%d" % c])

        def a2():
            m, mk, r, rk = rstd_batch([stv[:, c, :, :].rearrange("p a b -> p (a b)") for c in range(NCH)],
                                      ["stv.%d" % c for c in range(NCH)])
            for c in range(NCH):
                S.op("dve", (lambda c: lambda e: e.tensor_scalar(out=vln[:, c, :], in0=vln[:, c, :], scalar1=m[:, 2 * c:2 * c + 1],
                                                                  scalar2=r[:, c:c + 1], op0=ALU.subtract, op1=ALU.mult))(c),
                     reads=[mk, rk, "vln.%d" % c], writes=["vln.%d" % c])

        def a3(slot, j):
            for l in range(2):
                ct = 2 * j + l
                g = ct // 2
                pu, puk = next_pf()
                psv, psvk = next_pf()
                pz, pzk = next_pf()
                mm_group(pu, [(wr[slot][:, kt, l * 128:(l + 1) * 128], xT[:, kt, :]) for kt in range(KT)],
                         XT_K + wkeys(slot), puk)
                mm_each([(psv[:, c * 128:(c + 1) * 128], vln[:, c, ct * 128:(ct + 1) * 128], wsT[:, g, :]) for c in range(NCH)],
                        VLN_K + ["wsT"], psvk)
                mm_group(pz, [(wr[slot][:, kt, 256 + l * 128:256 + (l + 1) * 128], xT[:, kt, :]) for kt in range(KT)],
                         XT_K + wkeys(slot), pzk)
                tu, tuk = next_tmp()
                t2, t2k = next_tmp()
                S.op("act", (lambda pu, tu: lambda e: e.activation(out=tu[:], in_=pu, func=AF.Gelu_apprx_tanh))(pu, tu),
                     reads=puk, writes=[tuk])
                S.op("dve", (lambda psv, t2, ct: lambda e: e.scalar_tensor_tensor(
                    out=t2[:].rearrange("p (c t) -> p c t", c=NCH), in0=psv.rearrange("p (c t) -> p c t", c=NCH),
                    scalar=lnvg[:, ct:ct + 1], in1=bias2[:, ct, :].unsqueeze(1).to_broadcast([128, NCH, 128]),
                    op0=ALU.mult, op1=ALU.add))(psv, t2, ct),
                    reads=psvk + ["lnvg", "bias2"], writes=[t2k])
                S.op("dve", (lambda t2, tu: lambda e: e.tensor_tensor(out=t2[:], in0=t2[:], in1=tu[:], op=ALU.mult))(t2, tu),
                     reads=[t2k, tuk], writes=[t2k])
                S.op("act", (lambda pz, tu: lambda e: e.activation(out=tu[:], in_=pz, func=AF.Silu))(pz, tu),
                     reads=pzk, writes=[tuk])
                S.op("dve", (lambda t2, tu, ct: lambda e: e.tensor_tensor(out=preT[:, ct, :], in0=t2[:], in1=tu[:], op=ALU.mult))(t2, tu, ct),
                     reads=[t2k, tuk], writes=["preT.%d" % ct])

        def proj_fm(slot, j, srcT, src_keys, dstT, dst_prefix):
            for l in range(4):
                dt_ = 4 * j + l
                pf, pk = next_pf()
                mm_group(pf, [(wr[slot][:, kt, l * 128:(l + 1) * 128], srcT[:, kt, :]) for kt in range(KT)],
                         src_keys + wkeys(slot), pk)
                copy_op(ev_eng(), dstT[:, dt_, :], pf, pk, ["%s.%d" % (dst_prefix, dt_)])

        def qbank(h, c):
            return (4 * h + c) % 3

        def P_gemm(h, c, sq, mid=None, split=12):
            qb = qbank(h, c)
            pairs = [(xT[:, kt, c * 128:(c + 1) * 128], wr[sq][:, kt, :]) for kt in range(KT)]
            rk_ = ["xT.c%d.0" % c, "xT.c%d.1" % c] + wkeys(sq)
            if mid is None:
                mm_group(bank(qb), pairs, rk_, bk(qb))
                return

            def part(lo, hi):
                def fn(e):
                    ins = None
                    for i in range(lo, hi):
                        ins = e.matmul(bank(qb), lhsT=pairs[i][0], rhs=pairs[i][1], start=(i == 0), stop=(i == KT - 1))
                    return ins
                S.op("pe", fn, reads=rk_, writes=bk(qb))
            part(0, split)
            mid()
            part(split, KT)

        def P_evac(h, c):
            hb = h % 2
            half = c // 2
            qb = qbank(h, c)
            q32 = qk32[half][:].rearrange("p (c j d) -> p c j d", c=2, j=2)
            q32k = "qk32_%d" % half
            S.op("act", lambda e: e.activation(out=q32[:, c % 2, 0, :], in_=bank(qb)[:, 0:128], func=AF.Identity, scale=dec[:, h:h + 1]),
                 reads=bk(qb) + ["dec"], writes=[q32k])
            S.op("act", lambda e: e.activation(out=q32[:, c % 2, 1, :], in_=bank(qb)[:, 128:256], func=AF.Identity, scale=dec[:, 8 + h:9 + h]),
                 reads=bk(qb) + ["dec"], writes=[q32k])
            S.op("act", lambda e: e.activation(out=vh[hb][:, c, :], in_=bank(qb)[:, 256:512], func=AF.Identity),
                 reads=bk(qb), writes=["vh%d.%d" % (hb, c)])

        def P_post(h, half):
            hb = h % 2
            c0 = 2 * half
            q32 = qk32[half][:].rearrange("p (c j d) -> p c j d", c=2, j=2)
            q32k = "qk32_%d" % half
            ta = rta[half][:].rearrange("p (c j f) -> p c j f", c=2, j=2)
            tb = rtb[half][:].rearrange("p (c j f) -> p c j f", c=2, j=2)
            cosv = cs[:, c0:c0 + 2, 0:64].unsqueeze(2).to_broadcast([128, 2, 2, 64])
            sinv = cs[:, c0:c0 + 2, 64:128].unsqueeze(2).to_broadcast([128, 2, 2, 64])
            rotary(q32[:, :, :, 0:64], q32[:, :, :, 64:128], cosv, sinv,
                   qkt[hb][:, c0:c0 + 2, :, 0:64], qkt[hb][:, c0:c0 + 2, :, 64:128], ta, tb, q32k, ["qkt%d.%d" % (hb, half)], half)

        def T_a(h, half):
            hb = h % 2
            c0 = 2 * half
            pb = bank16(6)
            tr_group([(pb[:, half * 512 + (ci * 2 + j) * 128:half * 512 + (ci * 2 + j + 1) * 128], qkt[hb][:, c0 + ci, j, :])
                      for ci in range(2) for j in range(2)], ["qkt%d.%d" % (hb, half)], bkh(6, half))
            copy_op("act", qkT[hb][:, :, c0 * 128:(c0 + 2) * 128].rearrange("p j (c t) -> p c j t", c=2),
                    pb[:, half * 512:(half + 1) * 512].rearrange("p (c j t) -> p c j t", c=2, j=2),
                    bkh(6, half), ["qkT%d.%d" % (hb, half)])

        def T_b(h, half):
            hb = h % 2
            c0 = 2 * half
            mm_each([(bank(4)[:, c * 128:(c + 1) * 128], qkT[hb][:, 1, c * 128:(c + 1) * 128], qkT[hb][:, 0, c * 128:(c + 1) * 128])
                     for c in (c0, c0 + 1)], ["qkT%d.%d" % (hb, half)], bkh(4, half))
            S.op("dve", lambda e: e.tensor_tensor(out=ST[hb][:, c0:c0 + 2, :],
                                                  in0=bank(4)[:, c0 * 128:(c0 + 2) * 128].rearrange("p (c t) -> p c t", c=2),
                                                  in1=maskT[:].unsqueeze(1).to_broadcast([128, 2, 128]), op=ALU.mult),
                 reads=bkh(4, half) + ["maskT"], writes=["ST%d.%d" % (hb, half)])

        def R_stage(h, c):
            hb = h % 2
            half = c // 2

            def fn_o(e):
                e.matmul(bank(5)[:, 0:256], lhsT=ST[hb][:, c, :], rhs=vh[hb][:, c, :], start=True, stop=False)
                return e.matmul(bank(5)[:, 0:256], lhsT=qkT[hb][:, 0, c * 128:(c + 1) * 128], rhs=Rg[:, h, :], start=False, stop=True)
            S.op("pe", fn_o, reads=["ST%d.%d" % (hb, half), "vh%d.%d" % (hb, c), "qkT%d.%d" % (hb, half), "Rg%d" % h], writes=bk(5))
            mm_group(bank(3)[:, 0:256], [(qkt[hb][:, c, 1, :], vh[hb][:, c, :])],
                     ["qkt%d.%d" % (hb, half), "vh%d.%d" % (hb, c)], bk(3))
            s_c = float(gC1[h] / (gC[h] ** (c + 1)))
            S.op("dve", lambda e: e.scalar_tensor_tensor(out=R[:, h, :], in0=bank(3)[:, 0:256], scalar=s_c, in1=R[:, h, :],
                                                         op0=ALU.mult, op1=ALU.add),
                 reads=bk(3) + ["R%d" % h], writes=["R%d" % h])
            S.op("dve", lambda e: e.tensor_scalar(out=Rg[:, h, :], in0=R[:, h, :], scalar1=float(gam[h] * gC[h] ** (c + 1)),
                                                  scalar2=None, op0=ALU.mult),
                 reads=["R%d" % h], writes=["Rg%d" % h])
            if c == NCH - 1:
                S.op("dve", lambda e: e.tensor_scalar(out=R[:, h, :], in0=R[:, h, :], scalar1=float(gC[h] ** NCH),
                                                      scalar2=None, op0=ALU.mult),
                     reads=["R%d" % h], writes=["R%d" % h])
            S.op("act", lambda e: e.activation(out=on32[hb][:, c, :], in_=bank(5)[:, 0:256], func=AF.Identity),
                 reads=bk(5), writes=["on32_%d.%d" % (hb, c)])

        nst = {}

        def N1(h):
            hb = h % 2
            s4, s4k = st4[cnt["st4"] % 2], "st4_%d" % (cnt["st4"] % 2)
            cnt["st4"] += 1
            for c in range(NCH):
                S.op("dve", (lambda c: lambda e: e.bn_stats(out=s4[:, c, :], in_=on32[hb][:, c, :]))(c),
                     reads=["on32_%d.%d" % (hb, c)], writes=[s4k])
            m, mk = next_sm()
            r, rk = next_sm()
            for c in range(NCH):
                S.op("dve", (lambda c: lambda e: e.bn_aggr(out=m[:, 2 * c:2 * c + 2], in_=s4[:, c, :]))(c), reads=[s4k], writes=[mk])
            var = m[:, 0:8].rearrange("p (c t) -> p c t", t=2)[:, :, 1]
            S.op("dve", lambda e: e.tensor_scalar(out=r[:, 0:4], in0=var, scalar1=EPS, scalar2=None, op0=ALU.add), reads=[mk], writes=[rk])
            nst[h] = (m, mk, r, rk)

        def N2(h):
            m, mk, r, rk = nst[h]
            mean = m[:, 0:8].rearrange("p (c t) -> p c t", t=2)[:, :, 0]
            S.op("act", lambda e: e.activation(out=r[:, 0:4], in_=r[:, 0:4], func=AF.Sqrt), reads=[rk], writes=[rk])
            S.op("dve", lambda e: e.reciprocal(out=r[:, 0:4], in_=r[:, 0:4]), reads=[rk], writes=[rk])
            S.op("dve", lambda e: e.scalar_tensor_tensor(out=r[:, 4:8], in0=mean, scalar=-1.0, in1=r[:, 0:4],
                                                         op0=ALU.mult, op1=ALU.mult), reads=[mk, rk], writes=[rk])

        def N3(h):
            hb = h % 2
            m, mk, r, rk = nst[h]
            for c in range(NCH):
                S.op("act", (lambda c: lambda e: e.activation(out=ybp[hb][:, c, :], in_=on32[hb][:, c, :], func=AF.Identity,
                                                               bias=r[:, 4 + c:5 + c], scale=r[:, c:c + 1]))(c),
                     reads=[rk, "on32_%d.%d" % (hb, c)], writes=["ybp.%d" % c])

        def Y_stage(h):
            hb = h % 2
            pb = bank16(7)
            tr_group([(pb[:, (c * 2 + j) * 128:(c * 2 + j + 1) * 128], ybp[hb][:, c, j * 128:(j + 1) * 128])
                      for c in range(NCH) for j in range(2)], ["ybp.%d" % c for c in range(NCH)], bk(7))
            copy_op("act", preT[:, 2 * h:2 * h + 2, :].rearrange("p j (c t) -> p c j t", c=NCH),
                    pb.rearrange("p (c j t) -> p c j t", c=NCH, j=2), bk(7), ["preT.%d" % (2 * h), "preT.%d" % (2 * h + 1)])

        def bstep(slots, h, tile=None):
            sq = slots[0] if h < NH else None
            hr = h - 1
            if h == 1 and tile is not None and tile + 1 < NT:
                for c in range(NCH):
                    x_dma(x_d, "m", tile + 1, c)
            rv = 0 <= hr < NH
            for c in range(NCH):
                if h < NH:
                    if c == 2 and rv:
                        P_gemm(h, c, sq, mid=lambda: T_b(hr, 1))
                    else:
                        P_gemm(h, c, sq)
                        if c == 3:
                            T_b(h, 0)
                    P_evac(h, c)
                    if c % 2 == 1:
                        P_post(h, c // 2)
                else:
                    if c == 2 and rv:
                        T_b(hr, 1)
                if rv:
                    R_stage(hr, c)
                    if c == 1:
                        T_a(hr, 1)
                if c == 2 and h < NH:
                    T_a(h, 0)
                if h - 2 >= 0:
                    if c == 0:
                        N1(h - 2)
                    elif c == 1:
                        N2(h - 2)
                    elif c == 2:
                        N3(h - 2)
            if 0 <= h - 2 < NH:
                Y_stage(h - 2)

        def zstep(slot, j, extra=None, banks=None):
            for l in range(4):
                ct = 4 * j + l
                if banks is None:
                    pf, pk = next_pf()
                else:
                    bi_ = banks[l % len(banks)]
                    pf, pk = bank(bi_), bk(bi_)
                mm_group(pf, [(wr[slot][:, kt, l * 128:(l + 1) * 128], xT[:, kt, :]) for kt in range(KT)],
                         XT_K + wkeys(slot), pk)
                tz, tzk = next_tmp()
                S.op("act", (lambda pf, tz: lambda e: e.activation(out=tz[:], in_=pf, func=AF.Silu))(pf, tz), reads=pk, writes=[tzk])
                S.op("dve", (lambda tz, ct: lambda e: e.tensor_tensor(out=preT[:, ct, :], in0=preT[:, ct, :], in1=tz[:], op=ALU.mult))(tz, ct),
                     reads=[tzk, "preT.%d" % ct], writes=["preT.%d" % ct])
                if extra is not None:
                    extra(l)

        def drain_a(l):
            if l == 2:
                T_b(NH - 1, 1)
            R_stage(NH - 1, l)
            if l == 1:
                T_a(NH - 1, 1)
            if l == 0:
                N1(NH - 2)
            elif l == 1:
                N2(NH - 2)
            elif l == 2:
                N3(NH - 2)
            elif l == 3:
                Y_stage(NH - 2)

        def drain_b(l):
            if l == 0:
                N1(NH - 1)
            elif l == 1:
                N2(NH - 1)
            elif l == 2:
                N3(NH - 1)
            elif l == 3:
                Y_stage(NH - 1)

        def drain_c(l):
            pass

        def gstep(slot, j):
            for l in range(2):
                dt_ = 2 * j + l
                pa, pak = next_pf()
                pb_, pbk = next_pf()
                mm_group(pa, [(wr[slot][:, kt, l * 128:(l + 1) * 128], xT[:, kt, :]) for kt in range(KT)],
                         XT_K + wkeys(slot), pak)
                mm_group(pb_, [(wr[slot][:, kt, 256 + l * 128:256 + (l + 1) * 128], xT[:, kt, :]) for kt in range(KT)],
                         XT_K + wkeys(slot), pbk)
                ta, tak = next_tmp()
                tb, tbk = next_tmp()
                S.op("act", (lambda pa, ta, dt_: lambda e: e.activation(out=ta[:], in_=pa, func=AF.Sigmoid,
                                                                         bias=bgate[:, dt_:dt_ + 1], scale=1.0))(pa, ta, dt_),
                     reads=pak + ["bgate"], writes=[tak])
                S.op("act", (lambda pb_, tb, dt_: lambda e: e.activation(out=tb[:], in_=pb_, func=AF.Sigmoid,
                                                                          bias=bgate[:, 16 + dt_:17 + dt_], scale=1.0))(pb_, tb, dt_),
                     reads=pbk + ["bgate"], writes=[tbk])
                S.op("dve", (lambda ta, dt_: lambda e: e.tensor_tensor(out=ta[:], in0=ta[:], in1=yaT[:, dt_, :], op=ALU.mult))(ta, dt_),
                     reads=[tak, "yaT.%d" % dt_], writes=[tak])
                S.op("dve", (lambda tb, dt_: lambda e: e.tensor_tensor(out=tb[:], in0=tb[:], in1=ybT[:, dt_, :], op=ALU.mult))(tb, dt_),
                     reads=[tbk, "ybT.%d" % dt_], writes=[tbk])
                S.op("dve", (lambda ta, tb, dt_: lambda e: e.tensor_tensor(out=yaT[:, dt_, :], in0=ta[:], in1=tb[:], op=ALU.add))(ta, tb, dt_),
                     reads=[tak, tbk], writes=["yaT.%d" % dt_])

        def final_ln_chunk(tile, c):
            m, mk, r, rk = rstd_batch([stv[:, c, :, :].rearrange("p a b -> p (a b)")], ["stv.%d" % c], want_nb=True)
            S.op("act", lambda e: e.activation(out=res[:, c, :], in_=res[:, c, :], func=AF.Identity, bias=r[:, 4:5], scale=r[:, 0:1]),
                 reads=[rk, "res.%d" % c], writes=["res.%d" % c])
            S.op("dve", lambda e: e.tensor_tensor(out=res[:, c, :], in0=res[:, c, :], in1=lncg[:], op=ALU.mult),
                 reads=["res.%d" % c, "lncg"], writes=["res.%d" % c])
            S.op("dve", lambda e: e.tensor_tensor(out=res[:, c, :], in0=res[:, c, :], in1=lncb[:], op=ALU.add),
                 reads=["res.%d" % c, "lncb"], writes=["res.%d" % c])
            r0 = (tile * NCH + c) * 128
            S.op("sp", lambda e: e.dma_start(out=out_d[r0:r0 + 128, :], in_=res[:, c, :]),
                 reads=["res.%d" % c], writes=["out.%d.%d" % (tile, c)], dma_key="out")

        def ostep(slot, nb, tile):
            for c in range(NCH):
                pf, pk = next_pf()
                mm_group(pf, [(yaT[:, kt, c * 128:(c + 1) * 128], wr[slot][:, kt, :]) for kt in range(KT)],
                         YAT_K + wkeys(slot), pk)
                S.op("dve", (lambda pf, c: lambda e: e.scalar_tensor_tensor(
                    out=res[:, c, nb * 512:(nb + 1) * 512], in0=res[:, c, nb * 512:(nb + 1) * 512], scalar=ALPHA, in1=pf,
                    op0=ALU.mult, op1=ALU.add))(pf, c),
                    reads=pk + ["res.%d" % c], writes=["res.%d" % c])
                S.op("dve", (lambda c: lambda e: e.bn_stats(out=stv[:, c, nb, :], in_=res[:, c, nb * 512:(nb + 1) * 512]))(c),
                     reads=["res.%d" % c], writes=["stv.%d" % c])
                if nb == 3:
                    final_ln_chunk(tile, c)

        steps = []
        wmeta = []
        wctx = {"kind": "p", "tile": 0, "nid": 0}

        def W(pieces_list, fn):
            steps.append((pieces_list, fn))
            for _ in pieces_list:
                li = wctx["nid"]
                if wctx["kind"] == "p":
                    ctile = li % 2
                else:
                    ctile = 0 if ((li - NH) % 5) < 3 else 1
                t = wctx["tile"]
                mode = "load" if t < ctile else ("load_store" if t == ctile else "cached")
                wmeta.append((wctx["kind"], li, mode))
                wctx["nid"] += 1

        def Nw(fn):
            steps.append(([], fn))

        for ptile in range(NT):
            wctx.update(kind="p", tile=ptile, nid=0)
            Nw((lambda ptile: lambda s: (build_xT(xp_d, "p", ptile), load_cs(cs_pre_d, ptile)))(ptile))
            for h in range(NH):
                W([[(0, w_in[:, C_K + h * 128:C_K + (h + 1) * 128]), (128, w_in[:, C_RV + h * 256:C_RV + (h + 1) * 256])]],
                  (lambda h, ptile: lambda s: prefix_step(s[0], h, ptile))(h, ptile))
            Nw(lambda s: prefix_state(NH - 1))

        def after_prefix(s):
            for h in range(NH):
                S.op("act", (lambda h: lambda e: e.activation(out=Rg[:, h, :], in_=R[:, h, :], func=AF.Identity, scale=float(gam[h])))(h),
                     reads=["R%d" % h], writes=["Rg%d" % h])
        Nw(after_prefix)

        for tile in range(NT):
            wctx.update(kind="m", tile=tile, nid=NH)
            Nw((lambda tile: lambda s: (build_xT(x_d, "m", tile), load_cs(cs_main_d, tile)))(tile))
            for nb in range(4):
                W([[(0, w_in[:, C_V + nb * 512:C_V + (nb + 1) * 512])]], (lambda nb: lambda s: a1(s[0], nb))(nb))
            Nw(lambda s: a2())
            for j in range(8):
                W([[(0, w_in[:, C_U + j * 256:C_U + (j + 1) * 256]), (256, w_in[:, C_Z + j * 256:C_Z + (j + 1) * 256])]],
                  (lambda j: lambda s: a3(s[0], j))(j))
            Nw(lambda s: alias(VLN_K, YBT_K))
            for j in range(4):
                W([[(0, w_oa[:, j * 512:(j + 1) * 512])]], (lambda j: lambda s: proj_fm(s[0], j, preT, PRE_K, yaT, "yaT"))(j))
            for h in range(NH):
                W([[(0, w_in[:, C_Q + h * 128:C_Q + (h + 1) * 128]), (128, w_in[:, C_K + h * 128:C_K + (h + 1) * 128]),
                    (256, w_in[:, C_RV + h * 256:C_RV + (h + 1) * 256])]],
                  (lambda h, tile: lambda s: bstep(s, h, tile))(h, tile))
            zx = [(drain_a, [0, 1, 2]), (drain_b, [0, 1, 2]), (drain_c, [0, 1, 2]), (None, None)]
            for j in range(4):
                W([[(0, w_in[:, C_RZ + j * 512:C_RZ + (j + 1) * 512])]],
                  (lambda j: lambda s: zstep(s[0], j, extra=zx[j][0], banks=zx[j][1]))(j))
            for j in range(4):
                W([[(0, w_ob[:, j * 512:(j + 1) * 512])]], (lambda j: lambda s: proj_fm(s[0], j, preT, PRE_K, ybT, "ybT"))(j))
            for j in range(8):
                W([[(0, w_in[:, C_GA + j * 256:C_GA + (j + 1) * 256]), (256, w_in[:, C_GB + j * 256:C_GB + (j + 1) * 256])]],
                  (lambda j: lambda s: gstep(s[0], j))(j))

            def pre_o(s, tile=tile):
                alias(YBT_K + PRE_K, RES_K)
                for c in range(NCH):
                    r0 = (tile * NCH + c) * 128
                    S.op("sp", (lambda c, r0: lambda e: e.dma_start(out=res[:, c, :], in_=x_d[r0:r0 + 128, :]))(c, r0),
                         writes=["res.%d" % c], dma_key="res.%d" % c)
                if tile + 1 < NT:
                    for c in range(NCH):
                        x_dma(x_d, "m", tile + 1, c)
            Nw(pre_o)
            for nb in range(4):
                W([[(0, w_out[:, nb * 512:(nb + 1) * 512])]], (lambda nb, tile: lambda s: ostep(s[0], nb, tile))(nb, tile))
            if tile + 1 < NT:
                Nw(lambda s: alias(RES_K, VLN_K + PRE_K))

        wblocks = []
        for (pl, fn) in steps:
            for pieces in pl:
                wblocks.append(pieces)

        NCACHE = NH + 44
        wsc = nc.dram_tensor("wsc", [NCACHE, 128, KT * 512], BF16).ap()

        def issue_load(bi):
            slot = bi % NSLOT
            kind, cid, mode = wmeta[bi]
            assert cid < NCACHE
            flat = wr[slot][:].rearrange("p k n -> p (k n)")
            if mode == "cached":
                S.op("pool", lambda e: e.dma_start(out=flat, in_=wsc[cid]), reads=["wsc.%d" % cid], writes=wkeys(slot),
                     dma_key="wr%d" % slot, dma_group=bi)
                return
            pi = 0
            for (off, src) in wblocks[bi]:
                n = src.shape[1]
                srcv = src.rearrange("(kt p) n -> p kt n", p=128)
                nparts = 4 if n >= 256 else 2
                kper = KT // nparts
                for q in range(nparts):
                    S.op("pool", (lambda slot, off, n, srcv, q, kper: lambda e: e.dma_start(
                        out=wr[slot][:, q * kper:(q + 1) * kper, off:off + n], in_=srcv[:, q * kper:(q + 1) * kper, :]))(slot, off, n, srcv, q, kper),
                        writes=["wr%d.%d" % (slot, pi)], dma_key="wr%d" % slot, dma_group=bi)
                    pi += 1
            assert pi <= MAXP
            if mode != "load_store":
                return
            S.op("sp", lambda e: e.dma_start(out=wsc[cid], in_=flat), reads=wkeys(slot), writes=["wsc.%d" % cid],
                 dma_key="wsc_st%d" % slot)

        issued = 0
        bi = 0
        for (pl, fn) in steps:
            limit = min(len(wblocks), bi + NSLOT)
            while issued < limit:
                issue_load(issued)
                issued += 1
            slots = [(bi + k) % NSLOT for k in range(len(pl))]
            fn(slots)
            bi += len(pl)

        S.op("sp", None, reads=["out.%d.%d" % (t_, c_) for t_ in range(NT) for c_ in range(NCH)])
        build_program.sbuf_left = nc.sbuf_bytes_remaining
        S.emit()
        build_program.stats = S.stats
    return nc


def _tables(hf):
    lg = np.log1p(-np.exp2(-5.0 - np.arange(NH, dtype=np.float64)))
    freqs = (10000.0 ** (-np.arange(0, 128, 2, dtype=np.float32) / np.float32(128))).astype(np.float32)
    t = np.arange(128, dtype=np.float64)

    def cs_table(base):
        pos = (base + np.arange(TOK)).astype(np.float32)
        ang = (pos[:, None] * freqs[None, :]).astype(np.float32)
        c = np.cos(ang.astype(np.float64)).astype(np.float32)
        s = np.sin(ang.astype(np.float64)).astype(np.float32)
        tab = np.concatenate([c, s], axis=1).reshape(16, 128, 128).transpose(1, 0, 2)
        return np.ascontiguousarray(tab, dtype=np.float32)

    cs_main = cs_table(hf * TOK)
    cs_pre = cs_table(0)
    dec = np.zeros((128, 16), np.float32)
    dec[:, 0:8] = np.exp(lg[None, :] * t[:, None])
    dec[:, 8:16] = np.exp(-lg[None, :] * t[:, None]) * (128.0 ** -0.5)
    s_glob = np.arange(TOK, dtype=np.float64)
    dkp = (np.exp(lg[None, :] * (TOK - 1 - s_glob)[:, None]) * (128.0 ** -0.5)).astype(np.float32)
    dkp = np.ascontiguousarray(dkp.reshape(16, 128, 8).transpose(1, 0, 2))
    maskT = (np.arange(128)[None, :] >= np.arange(128)[:, None]).astype(np.float32)
    mask_ts = np.ascontiguousarray(maskT.T)
    return dict(cs_main=cs_main, cs_pre=cs_pre, dec=dec, dkp=dkp, maskT=maskT, mask_ts=mask_ts)


_NC_CACHE = {}


def kernel(x, w_in, b_gate, ln_v_g, ln_v_b, w_s, b_s, w_oa, w_ob, w_out, ln_g, ln_b):
    x = np.asarray(x, dtype=np.float32)
    B, SEQ, _ = x.shape
    if "nc" not in _NC_CACHE:
        _NC_CACHE["nc"] = build_program()
    nc = _NC_CACHE["nc"]
    shared = dict(
        w_in=np.ascontiguousarray(np.asarray(w_in, np.float32)[0]),
        w_oa=np.ascontiguousarray(np.asarray(w_oa, np.float32)[0]),
        w_ob=np.ascontiguousarray(np.asarray(w_ob, np.float32)[0]),
        w_out=np.ascontiguousarray(np.asarray(w_out, np.float32)[0]),
        b_gate=np.ascontiguousarray(np.asarray(b_gate, np.float32)[0]),
        ln_v_g=np.ascontiguousarray(np.asarray(ln_v_g, np.float32)[0]),
        ln_v_b=np.ascontiguousarray(np.asarray(ln_v_b, np.float32)[0]),
        w_s=np.ascontiguousarray(np.asarray(w_s, np.float32)[0]),
        b_s=np.ascontiguousarray(np.asarray(b_s, np.float32)[0]),
        ln_g=np.ascontiguousarray(np.asarray(ln_g, np.float32)[0]),
        ln_b=np.ascontiguousarray(np.asarray(ln_b, np.float32)[0]),
    )
    tabs = [_tables(0), _tables(1)]
    zeros = np.zeros((TOK, D), np.float32)
    in_maps = []
    for c in range(NCORES):
        b, hf = c // 2, c % 2
        m = dict(shared)
        m["x"] = np.ascontiguousarray(x[b, hf * TOK:(hf + 1) * TOK, :])
        m["xp"] = np.ascontiguousarray(x[b, 0:TOK, :]) if hf == 1 else zeros
        m.update(tabs[hf])
        in_maps.append(m)
    res = run_bass_kernel_spmd(nc, in_maps, core_ids=list(range(NCORES)))
    out = np.empty((B, SEQ, D), np.float32)
    for c in range(NCORES):
        b, hf = c // 2, c % 2
        out[b, hf * TOK:(hf + 1) * TOK, :] = res.results[c]["out"]
    return out
```

```python
import math
from contextlib import ExitStack

import numpy as np
import concourse.bass as bass
import concourse.mybir as mybir
from concourse.bass_utils import run_bass_kernel_spmd

F32 = mybir.dt.float32
BF16 = mybir.dt.bfloat16
AF = mybir.ActivationFunctionType
ALU = mybir.AluOpType

D = 2048
NCORES = 8
TOK = 2048
NCH = 4
T = 512
NT = TOK // T
KT = 16
NH = 8
ALPHA = float(2.0 ** 0.25)
EPS = 1e-5
NSLOT = 3
MAXP = 12
C_U, C_V, C_Z, C_Q, C_K, C_RV, C_RZ, C_GA, C_GB = 0, 2048, 4096, 6144, 7168, 8192, 10240, 12288, 14336

ENGS = ("pe", "act", "dve", "pool", "sp")


class _Op:
    __slots__ = ("idx", "eng", "fn", "deps", "dma_key", "signal", "count", "sem", "grp", "gcount")

    def __init__(self, idx, eng, fn, deps, dma_key, grp=None):
        self.idx = idx
        self.eng = eng
        self.fn = fn
        self.deps = deps
        self.dma_key = dma_key
        self.signal = False
        self.count = 0
        self.sem = None
        self.grp = grp if grp is not None else ("_", idx)
        self.gcount = 0


class Sched:
    def __init__(self, nc):
        self.nc = nc
        self.ops = []
        self.lastw = {}
        self.readers = {}

    def op(self, eng, fn, reads=(), writes=(), dma_key=None, dma_group=None):
        deps = set()
        for k in reads:
            w = self.lastw.get(k)
            if w is not None:
                deps.add(w)
            if k.startswith("PF"):
                for r in self.readers.get(k, ()):
                    if self.ops[r].eng != eng:
                        deps.add(r)
        for k in writes:
            w = self.lastw.get(k)
            if w is not None:
                deps.add(w)
            for r in self.readers.get(k, ()):
                deps.add(r)
        idx = len(self.ops)
        o = _Op(idx, eng, fn, deps, dma_key, dma_group)
        self.ops.append(o)
        for k in reads:
            self.readers.setdefault(k, []).append(idx)
        for k in writes:
            self.lastw[k] = idx
            self.readers[k] = []
        return idx

    def emit(self):
        nc = self.nc
        ops = self.ops
        for o in ops:
            if o.eng == "pe" and o.dma_key is None:
                o.deps = {d for d in o.deps
                          if not (ops[d].eng == "pe" and ops[d].dma_key is None)}
        for o in ops:
            for d in o.deps:
                ops[d].signal = True
        with ExitStack() as es:
            esem = {e: es.enter_context(nc.semaphore("s_" + e)) for e in ENGS}
            dsem = {}
            ecount = {e: 0 for e in ENGS}
            dcount = {}
            for o in ops:
                if o.dma_key is not None:
                    if o.dma_key not in dsem:
                        dsem[o.dma_key] = es.enter_context(nc.semaphore("d_%d" % len(dsem)))
                        dcount[o.dma_key] = 0
                    dcount[o.dma_key] += 16
                    o.sem = dsem[o.dma_key]
                    o.count = dcount[o.dma_key]
                    o.signal = True
                elif o.signal:
                    ecount[o.eng] += 1
                    o.sem = esem[o.eng]
                    o.count = ecount[o.eng]
            self.stats = dict(ecount)
            self.stats["n_dma_sems"] = len(dsem)
            self.stats["n_ops"] = len(ops)
            streams = {e: [o for o in ops if o.eng == e] for e in ENGS}
            gend = {}
            for o in ops:
                if o.dma_key is not None:
                    k = (o.dma_key, o.grp)
                    gend[k] = max(gend.get(k, 0), o.count)
            for o in ops:
                if o.dma_key is not None:
                    o.gcount = gend[(o.dma_key, o.grp)]

            def dma_count(p, cons_idx):
                return p.gcount

            def run(eng_name, e):
                waited = {}
                for o in streams[eng_name]:
                    need = {}
                    for d in o.deps:
                        p = ops[d]
                        sid = id(p.sem)
                        pc = p.count if p.dma_key is None else dma_count(p, o.idx)
                        if waited.get(sid, 0) < pc:
                            if sid not in need or need[sid][1] < pc:
                                need[sid] = (p.sem, pc)
                    for sid, (sem, cnt) in need.items():
                        e.wait_ge(sem, cnt)
                        waited[sid] = cnt
                    if o.fn is None:
                        continue
                    ins = o.fn(e)
                    if o.signal:
                        ins.then_inc(o.sem, 16 if o.dma_key is not None else 1)

            with nc.Block() as block:
                @block.tensor
                def _(e):
                    run("pe", e)

                @block.scalar
                def _(e):
                    run("act", e)

                @block.vector
                def _(e):
                    run("dve", e)

                @block.gpsimd
                def _(e):
                    run("pool", e)

                @block.sync
                def _(e):
                    run("sp", e)


def build_program(taps=None):
    nc = bass.Bass("TRN2", target_bir_lowering=False)

    def din(name, shape):
        return nc.dram_tensor(name, list(shape), F32, kind="ExternalInput").ap()

    x_d = din("x", [TOK, D])
    xp_d = din("xp", [TOK, D])
    w_in = din("w_in", [D, 16384])
    w_oa = din("w_oa", [D, D])
    w_ob = din("w_ob", [D, D])
    w_out = din("w_out", [D, D])
    b_gate = din("b_gate", [4096])
    ln_v_g = din("ln_v_g", [D])
    ln_v_b = din("ln_v_b", [D])
    w_s = din("w_s", [8, 128, 128])
    b_s = din("b_s", [8, 128])
    ln_g = din("ln_g", [D])
    ln_b = din("ln_b", [D])
    cs_main_d = din("cs_main", [128, 16, 128])
    cs_pre_d = din("cs_pre", [128, 16, 128])
    dec_d = din("dec", [128, 16])
    dkp_d = din("dkp", [128, 16, 8])
    maskT_d = din("maskT", [128, 128])
    maskts_d = din("mask_ts", [128, 128])
    out_d = nc.dram_tensor("out", [TOK, D], F32, kind="ExternalOutput").ap()

    gam = [1.0 - 2.0 ** (-5.0 - h) for h in range(NH)]
    gC = [g ** 128 for g in gam]
    gC1 = [g ** 127 for g in gam]

    with ExitStack() as es:
        es.enter_context(nc.allow_non_contiguous_dma(reason="small param column loads"))

        def sb(name, shape, dt):
            return es.enter_context(nc.sbuf_tensor("sb_" + name, list(shape), dt))

        xT = sb("xT", [128, KT, T], BF16)
        yaT = sb("yaT", [128, KT, T], BF16)
        X = sb("X", [128, 16384], BF16)
        vln = X[:, 0:8192].rearrange("p (c n) -> p c n", c=NCH)
        ybT = X[:, 0:8192].rearrange("p (k n) -> p k n", k=KT)
        preT = X[:, 8192:16384].rearrange("p (k n) -> p k n", k=KT)
        X32 = X[:].bitcast(F32)
        res = X32.rearrange("p (c n) -> p c n", c=NCH)
        wr = [sb("wr%d" % i, [128, KT, 512], BF16) for i in range(NSLOT)]
        lncg = sb("lncg", [128, D], F32)
        lncb = sb("lncb", [128, D], F32)
        xb = [sb("xb%d" % i, [128, D], BF16) for i in range(4)]
        R = sb("R", [128, NH, 256], F32)
        Rg = sb("Rg", [128, NH, 256], BF16)
        wsT = sb("wsT", [128, 8, 128], BF16)
        bias2 = sb("bias2", [128, 16, 128], F32)
        cs = sb("cs", [128, NCH, 128], F32)
        dec = sb("dec", [128, 16], F32)
        dkp = sb("dkp", [128, 16, 8], F32)
        maskT = sb("maskT", [128, 128], F32)
        ident = sb("ident", [128, 128], BF16)
        ones = sb("ones", [128, 128], BF16)
        lnvg = sb("lnvg", [128, 16], F32)
        lnvb = sb("lnvb", [128, 16], F32)
        bgate = sb("bgate", [128, 32], F32)
        scratch = sb("scratch", [128, 2], F32)
        tmpF = [sb("tmpF%d" % i, [128, 512], F32) for i in range(4)]
        qk32 = [sb("qk32_%d" % i, [128, 512], F32) for i in range(2)]
        rta = [sb("rta%d" % i, [128, 256], F32) for i in range(2)]
        rtb = [sb("rtb%d" % i, [128, 256], F32) for i in range(2)]
        qkt = [sb("qkt%d" % i, [128, NCH, 2, 128], BF16) for i in range(2)]
        qkT = [sb("qkT%d" % i, [128, 2, T], BF16) for i in range(2)]
        vh = [sb("vh%d" % i, [128, NCH, 256], BF16) for i in range(2)]
        ST = [sb("ST%d" % i, [128, NCH, 128], BF16) for i in range(2)]
        on32 = [sb("on32_%d" % i, [128, NCH, 256], BF16) for i in range(2)]
        ybp = [sb("ybp0", [128, NCH, 256], BF16)] * 2
        stv = sb("stv", [128, NCH, 4, 6], F32)
        st4 = [sb("st4_%d" % i, [128, 4, 6], F32) for i in range(2)]
        sm = [sb("sm%d" % i, [128, 8], F32) for i in range(8)]

        PS = es.enter_context(nc.psum_tensor("ps_all", [128, 8, 512], F32))

        def bank(i):
            return PS[:, i, :]

        def bank16(i):
            return PS[:, i, :].bitcast(BF16)

        def bk(i):
            return ["PF%d.a" % i, "PF%d.b" % i]

        def bkh(i, half):
            return ["PF%d.a" % i, "PF%d.b" % i]

        S = Sched(nc)
        cnt = {"tmp": 0, "sm": 0, "st4": 0, "pf": 0, "ev": 0}

        def next_tmp():
            i = cnt["tmp"] % 4
            cnt["tmp"] += 1
            return tmpF[i], "tmpF%d" % i

        def next_sm():
            i = cnt["sm"] % 8
            cnt["sm"] += 1
            return sm[i], "sm%d" % i

        def next_pf():
            i = cnt["pf"] % 6
            cnt["pf"] += 1
            return bank(i), bk(i)

        def ev_eng():
            cnt["ev"] += 1
            return "act" if cnt["ev"] % 2 else "dve"

        def copy_op(eng, out, in_, reads, writes):
            if eng == "act":
                S.op("act", lambda e: e.activation(out=out, in_=in_, func=AF.Identity), reads=reads, writes=writes)
            else:
                S.op("dve", lambda e: e.tensor_copy(out=out, in_=in_), reads=reads, writes=writes)

        def alias(old, new):
            S.op("dve", lambda e: e.memset(scratch[:, 0:1], 0.0), writes=list(old) + list(new) + ["scratch"])

        def wkeys(slot):
            return ["wr%d.%d" % (slot, j) for j in range(MAXP)]

        def mm_group(out, pairs, reads, writes):
            def fn(e):
                n = len(pairs)
                ins = None
                for i, (l, r) in enumerate(pairs):
                    ins = e.matmul(out, lhsT=l, rhs=r, start=(i == 0), stop=(i == n - 1))
                return ins
            S.op("pe", fn, reads=reads, writes=writes)

        def mm_each(items, reads, writes):
            def fn(e):
                ins = None
                for (o, l, r) in items:
                    ins = e.matmul(o, lhsT=l, rhs=r, start=True, stop=True)
                return ins
            S.op("pe", fn, reads=reads, writes=writes)

        def tr_group(items, reads, writes):
            def fn(e):
                ins = None
                for (o, i) in items:
                    ins = e.transpose(out=o, in_=i, identity=ident[:])
                return ins
            S.op("pe", fn, reads=reads + ["ident"], writes=writes)

        bsb = X32[:, 0:1024].rearrange("p (g t) -> p g t", g=8)
        Wb = X32[:, 1024:2048].rearrange("p (g t) -> p g t", g=8)
        ws32 = X32[:, 2048:3072].rearrange("p (g s) -> p g s", g=8)
        mts = X32[:, 3072:3200]
        wsm = X[:, 8192:9216].rearrange("p (g s) -> p g s", g=8)

        S.op("dve", lambda e: e.memset(tmpF[0][:, 0:128], 0.0), writes=["tmpF0"])
        S.op("pool", lambda e: e.affine_select(out=tmpF[0][:, 0:128], in_=tmpF[0][:, 0:128], pattern=[[-1, 128]],
                                                 compare_op=ALU.not_equal, fill=1.0, base=0, channel_multiplier=1),
             reads=["tmpF0"], writes=["tmpF0"])
        S.op("dve", lambda e: e.tensor_copy(out=ident[:], in_=tmpF[0][:, 0:128]), reads=["tmpF0"], writes=["ident"])
        S.op("dve", lambda e: e.memset(ones[:], 1.0), writes=["ones"])
        S.op("dve", lambda e: e.memset(R[:], 0.0), writes=["R%d" % h for h in range(NH)])

        def ld(dst, src, key):
            S.op("sp", lambda e: e.dma_start(out=dst, in_=src), writes=[key], dma_key=key)

        ld(dec[:], dec_d, "dec")
        ld(dkp[:], dkp_d, "dkp")
        ld(maskT[:], maskT_d, "maskT")
        ld(lnvg[:], ln_v_g.rearrange("(c p) -> p c", p=128), "lnvg")
        ld(lnvb[:], ln_v_b.rearrange("(c p) -> p c", p=128), "lnvb")
        ld(bgate[:], b_gate.rearrange("(c p) -> p c", p=128), "bgate")
        ld(lncg[:], ln_g.partition_broadcast(128), "lncg")
        ld(lncb[:], ln_b.partition_broadcast(128), "lncb")
        S.op("sp", lambda e: e.dma_start(out=X32[:, 0:1024], in_=b_s.rearrange("g t -> (g t)").partition_broadcast(128)),
             writes=["X.bsb"], dma_key="X.bsb")
        S.op("sp", lambda e: e.dma_start(out=ws32, in_=w_s.rearrange("g t s -> t g s")), writes=["X.ws32"], dma_key="X.ws32")
        S.op("sp", lambda e: e.dma_start(out=mts, in_=maskts_d), writes=["X.mts"], dma_key="X.mts")
        S.op("dve", lambda e: e.tensor_tensor(out=wsm, in0=ws32, in1=mts.unsqueeze(1).to_broadcast([128, 8, 128]), op=ALU.mult),
             reads=["X.ws32", "X.mts"], writes=["X.wsm"])
        tr_group([(bank16(6)[:, g * 128:(g + 1) * 128], wsm[:, g, :]) for g in range(8)], ["X.wsm"], bk(6))
        S.op("dve", lambda e: e.tensor_copy(out=wsT[:].rearrange("p g t -> p (g t)"), in_=bank16(6)), reads=bk(6), writes=["wsT"])
        for hlf in range(2):
            mm_group(bank(hlf), [(ones[:], wsT[:, hlf * 4:(hlf + 1) * 4, :].rearrange("p g t -> p (g t)"))],
                     ["ones", "wsT"], bk(hlf))
            S.op("dve", (lambda hlf: lambda e: e.tensor_copy(out=X32[:, 1024 + hlf * 512:1024 + (hlf + 1) * 512], in_=bank(hlf)))(hlf),
                 reads=bk(hlf), writes=["X.Wb%d" % hlf])
        for ct in range(16):
            g = ct // 2
            S.op("dve", (lambda ct, g: lambda e: e.scalar_tensor_tensor(
                out=bias2[:, ct, :], in0=Wb[:, g, :], scalar=lnvb[:, ct:ct + 1], in1=bsb[:, g, :],
                op0=ALU.mult, op1=ALU.add))(ct, g),
                reads=["X.Wb%d" % (g // 4), "lnvb", "X.bsb"], writes=["bias2"])
        VLN_K = ["vln.%d" % c for c in range(NCH)]
        PRE_K = ["preT.%d" % k for k in range(KT)]
        YBT_K = ["ybT.%d" % k for k in range(KT)]
        RES_K = ["res.%d" % c for c in range(NCH)]
        XT_K = ["xT.c%d.%d" % (c, hh) for c in range(NCH) for hh in range(2)]
        YAT_K = ["yaT.%d" % k for k in range(KT)]
        alias(["X.bsb", "X.ws32", "X.mts", "X.wsm", "X.Wb0", "X.Wb1"], VLN_K + PRE_K)

        xdone = set()

        def x_dma(src, tag, tile, c):
            if (tag, tile, c) in xdone:
                return
            xdone.add((tag, tile, c))
            b = c
            r0 = (tile * NCH + c) * 128
            S.op("pool", lambda e: e.dma_start(out=xb[b][:], in_=src[r0:r0 + 128, :]), writes=["xb%d" % b], dma_key="xb%d" % b)

        def build_xT(src, tag, tile):
            for c in range(NCH):
                x_dma(src, tag, tile, c)
                b = c
                for hlf in range(2):
                    bi_ = (2 * c + hlf) % 8
                    pb = bank16(bi_)
                    tr_group([(pb[:, j * 128:(j + 1) * 128], xb[b][:, (hlf * 8 + j) * 128:(hlf * 8 + j + 1) * 128])
                              for j in range(8)], ["xb%d" % b], bk(bi_))
                    copy_op("act" if tag == "m" else ev_eng(), xT[:, hlf * 8:(hlf + 1) * 8, c * 128:(c + 1) * 128],
                            pb.rearrange("p (k t) -> p k t", k=8), bk(bi_), ["xT.c%d.%d" % (c, hlf)])

        def load_cs(src_d, tile):
            S.op("sp", lambda e: e.dma_start(out=cs[:], in_=src_d[:, tile * NCH:(tile + 1) * NCH, :]), writes=["cs"], dma_key="cs")

        def rotary(x1, x2, cosv, sinv, d1, d2, ta, tb, srck, dstk, par):
            ka, kb = "rta%d" % par, "rtb%d" % par
            S.op("dve", lambda e: e.tensor_tensor(out=ta, in0=x1, in1=cosv, op=ALU.mult), reads=[srck, "cs"], writes=[ka])
            S.op("dve", lambda e: e.tensor_tensor(out=tb, in0=x2, in1=sinv, op=ALU.mult), reads=[srck, "cs"], writes=[kb])
            S.op("dve", lambda e: e.tensor_tensor(out=d1, in0=ta, in1=tb, op=ALU.subtract), reads=[ka, kb], writes=list(dstk))
            S.op("dve", lambda e: e.tensor_tensor(out=ta, in0=x1, in1=sinv, op=ALU.mult), reads=[srck, "cs"], writes=[ka])
            S.op("dve", lambda e: e.tensor_tensor(out=tb, in0=x2, in1=cosv, op=ALU.mult), reads=[srck, "cs"], writes=[kb])
            S.op("dve", lambda e: e.tensor_tensor(out=d2, in0=ta, in1=tb, op=ALU.add), reads=[ka, kb], writes=list(dstk))

        def prefix_proj(slot, h, ptile):
            hb = h % 2
            kb = 3 * hb
            for c in range(NCH):
                mm_group(bank(kb)[:, c * 128:(c + 1) * 128],
                         [(xT[:, kt, c * 128:(c + 1) * 128], wr[slot][:, kt, 0:128]) for kt in range(KT)],
                         ["xT.c%d.0" % c, "xT.c%d.1" % c] + wkeys(slot), bkh(kb, c // 2))
                vb = kb + 1 + c // 2
                mm_group(bank(vb)[:, (c % 2) * 256:(c % 2 + 1) * 256],
                         [(xT[:, kt, c * 128:(c + 1) * 128], wr[slot][:, kt, 128:384]) for kt in range(KT)],
                         ["xT.c%d.0" % c, "xT.c%d.1" % c] + wkeys(slot), bkh(vb, c % 2))
            k32 = qk32[hb][:].rearrange("p (c d) -> p c d", c=NCH)
            k32k = "qk32_%d" % hb
            for c in range(NCH):
                jj = ptile * NCH + c
                S.op("act", (lambda c, jj: lambda e: e.activation(out=k32[:, c, :], in_=bank(kb)[:, c * 128:(c + 1) * 128],
                                                                   func=AF.Identity, scale=dkp[:, jj, h:h + 1]))(c, jj),
                     reads=bkh(kb, c // 2) + ["dkp"], writes=[k32k])
            S.op("act", lambda e: e.activation(out=vh[hb][:], in_=PS[:, kb + 1:kb + 3, :].rearrange("p b (c e) -> p (b c) e", c=2),
                                               func=AF.Identity),
                 reads=bk(kb + 1) + bk(kb + 2), writes=["vh%d.%d" % (hb, c) for c in range(NCH)])
            ta = rta[hb][:].rearrange("p (c f) -> p c f", c=NCH)
            tb = rtb[hb][:].rearrange("p (c f) -> p c f", c=NCH)
            rotary(k32[:, :, 0:64], k32[:, :, 64:128], cs[:, :, 0:64], cs[:, :, 64:128],
                   qkt[hb][:, :, 0, 0:64], qkt[hb][:, :, 0, 64:128], ta, tb, k32k, ["qkt%d.0" % hb, "qkt%d.1" % hb], hb)

        def prefix_state(h):
            hb = h % 2
            for c in range(NCH):
                def fn(e, c=c):
                    return e.matmul(bank(6)[:, 0:256], lhsT=qkt[hb][:, c, 0, :], rhs=vh[hb][:, c, :],
                                    start=(c == 0), stop=(c == NCH - 1))
                S.op("pe", fn, reads=["qkt%d.0" % hb, "qkt%d.1" % hb, "vh%d.%d" % (hb, c)], writes=bkh(6, 0))
            S.op("dve", lambda e: e.tensor_tensor(out=R[:, h, :], in0=R[:, h, :], in1=bank(6)[:, 0:256], op=ALU.add),
                 reads=bkh(6, 0) + ["R%d" % h], writes=["R%d" % h])

        def prefix_step(slot, h, ptile):
            if h == 2:
                for c in range(NCH):
                    if ptile + 1 < NT:
                        x_dma(xp_d, "p", ptile + 1, c)
                    else:
                        x_dma(x_d, "m", 0, c)
            prefix_proj(slot, h, ptile)
            if h > 0:
                prefix_state(h - 1)

        def rstd_batch(stat_views, stat_keys, want_nb=False):
            n = len(stat_views)
            m, mk = next_sm()
            r, rk = next_sm()
            for c in range(n):
                S.op("dve", (lambda c: lambda e: e.bn_aggr(out=m[:, 2 * c:2 * c + 2], in_=stat_views[c]))(c),
                     reads=[stat_keys[c]], writes=[mk])
            var = m[:, 0:2 * n].rearrange("p (c t) -> p c t", t=2)[:, :, 1]
            mean = m[:, 0:2 * n].rearrange("p (c t) -> p c t", t=2)[:, :, 0]
            S.op("dve", lambda e: e.tensor_scalar(out=r[:, 0:n], in0=var, scalar1=EPS, scalar2=None, op0=ALU.add),
                 reads=[mk], writes=[rk])
            S.op("act", lambda e: e.activation(out=r[:, 0:n], in_=r[:, 0:n], func=AF.Sqrt), reads=[rk], writes=[rk])
            S.op("dve", lambda e: e.reciprocal(out=r[:, 0:n], in_=r[:, 0:n]), reads=[rk], writes=[rk])
            if want_nb:
                S.op("dve", lambda e: e.scalar_tensor_tensor(out=r[:, 4:4 + n], in0=mean, scalar=-1.0, in1=r[:, 0:n],
                                                             op0=ALU.mult, op1=ALU.mult), reads=[mk, rk], writes=[rk])
            return m, mk, r, rk

        def a1(slot, nb):
            for c in range(NCH):
                pf, pk = next_pf()
                mm_group(pf, [(xT[:, kt, c * 128:(c + 1) * 128], wr[slot][:, kt, :]) for kt in range(KT)],
                         ["xT.c%d.0" % c, "xT.c%d.1" % c] + wkeys(slot), pk)
                tf, tk = next_tmp()
                S.op("act", (lambda pf, tf: lambda e: e.activation(out=tf[:], in_=pf, func=AF.Gelu_apprx_tanh))(pf, tf),
                     reads=pk, writes=[tk])
                S.op("dve", (lambda tf, c: lambda e: e.bn_stats(out=stv[:, c, nb, :], in_=tf[:]))(tf, c),
                     reads=[tk], writes=["stv.%d" % c])
                S.op("dve", (lambda tf, c: lambda e: e.tensor_copy(out=vln[:, c, nb * 512:(nb + 1) * 512], in_=tf[:]))(tf, c),
                     reads=[tk], writes=["vln.%d" % c])

        def a2():
            m, mk, r, rk = rstd_batch([stv[:, c, :, :].rearrange("p a b -> p (a b)") for c in range(NCH)],
                                      ["stv.%d" % c for c in range(NCH)])
            for c in range(NCH):
                S.op("dve", (lambda c: lambda e: e.tensor_scalar(out=vln[:, c, :], in0=vln[:, c, :], scalar1=m[:, 2 * c:2 * c + 1],
                                                                  scalar2=r[:, c:c + 1], op0=ALU.subtract, op1=ALU.mult))(c),
                     reads=[mk, rk, "vln.%d" % c], writes=["vln.%d" % c])

        def a3(slot, j):
            for l in range(2):
                ct = 2 * j + l
                g = ct // 2
                pu, puk = next_pf()
                psv, psvk = next_pf()
                pz, pzk = next_pf()
                mm_group(pu, [(wr[slot][:, kt, l * 128:(l + 1) * 128], xT[:, kt, :]) for kt in range(KT)],
                         XT_K + wkeys(slot), puk)
                mm_each([(psv[:, c * 128:(c + 1) * 128], vln[:, c, ct * 128:(ct + 1) * 128], wsT[:, g, :]) for c in range(NCH)],
                        VLN_K + ["wsT"], psvk)
                mm_group(pz, [(wr[slot][:, kt, 256 + l * 128:256 + (l + 1) * 128], xT[:, kt, :]) for kt in range(KT)],
                         XT_K + wkeys(slot), pzk)
                tu, tuk = next_tmp()
                t2, t2k = next_tmp()
                S.op("act", (lambda pu, tu: lambda e: e.activation(out=tu[:], in_=pu, func=AF.Gelu_apprx_tanh))(pu, tu),
                     reads=puk, writes=[tuk])
                S.op("dve", (lambda psv, t2, ct: lambda e: e.scalar_tensor_tensor(
                    out=t2[:].rearrange("p (c t) -> p c t", c=NCH), in0=psv.rearrange("p (c t) -> p c t", c=NCH),
                    scalar=lnvg[:, ct:ct + 1], in1=bias2[:, ct, :].unsqueeze(1).to_broadcast([128, NCH, 128]),
                    op0=ALU.mult, op1=ALU.add))(psv, t2, ct),
                    reads=psvk + ["lnvg", "bias2"], writes=[t2k])
                S.op("dve", (lambda t2, tu: lambda e: e.tensor_tensor(out=t2[:], in0=t2[:], in1=tu[:], op=ALU.mult))(t2, tu),
                     reads=[t2k, tuk], writes=[t2k])
                S.op("act", (lambda pz, tu: lambda e: e.activation(out=tu[:], in_=pz, func=AF.Silu))(pz, tu),
                     reads=pzk, writes=[tuk])
                S.op("dve", (lambda t2, tu, ct: lambda e: e.tensor_tensor(out=preT[:, ct, :], in0=t2[:], in1=tu[:], op=ALU.mult))(t2, tu, ct),
                     reads=[t2k, tuk], writes=["preT.%d" % ct])

        def proj_fm(slot, j, srcT, src_keys, dstT, dst_prefix):
            for l in range(4):
                dt_ = 4 * j + l
                pf, pk = next_pf()
                mm_group(pf, [(wr[slot][:, kt, l * 128:(l + 1) * 128], srcT[:, kt, :]) for kt in range(KT)],
                         src_keys + wkeys(slot), pk)
                copy_op(ev_eng(), dstT[:, dt_, :], pf, pk, ["%s.%d" % (dst_prefix, dt_)])

        def qbank(h, c):
            return (4 * h + c) % 3

        def P_gemm(h, c, sq, mid=None, split=12):
            qb = qbank(h, c)
            pairs = [(xT[:, kt, c * 128:(c + 1) * 128], wr[sq][:, kt, :]) for kt in range(KT)]
            rk_ = ["xT.c%d.0" % c, "xT.c%d.1" % c] + wkeys(sq)
            if mid is None:
                mm_group(bank(qb), pairs, rk_, bk(qb))
                return

            def part(lo, hi):
                def fn(e):
                    ins = None
                    for i in range(lo, hi):
                        ins = e.matmul(bank(qb), lhsT=pairs[i][0], rhs=pairs[i][1], start=(i == 0), stop=(i == KT - 1))
                    return ins
                S.op("pe", fn, reads=rk_, writes=bk(qb))
            part(0, split)
            mid()
            part(split, KT)

        def P_evac(h, c):
            hb = h % 2
            half = c // 2
            qb = qbank(h, c)
            q32 = qk32[half][:].rearrange("p (c j d) -> p c j d", c=2, j=2)
            q32k = "qk32_%d" % half
            S.op("act", lambda e: e.activation(out=q32[:, c % 2, 0, :], in_=bank(qb)[:, 0:128], func=AF.Identity, scale=dec[:, h:h + 1]),
                 reads=bk(qb) + ["dec"], writes=[q32k])
            S.op("act", lambda e: e.activation(out=q32[:, c % 2, 1, :], in_=bank(qb)[:, 128:256], func=AF.Identity, scale=dec[:, 8 + h:9 + h]),
                 reads=bk(qb) + ["dec"], writes=[q32k])
            S.op("act", lambda e: e.activation(out=vh[hb][:, c, :], in_=bank(qb)[:, 256:512], func=AF.Identity),
                 reads=bk(qb), writes=["vh%d.%d" % (hb, c)])

        def P_post(h, half):
            hb = h % 2
            c0 = 2 * half
            q32 = qk32[half][:].rearrange("p (c j d) -> p c j d", c=2, j=2)
            q32k = "qk32_%d" % half
            ta = rta[half][:].rearrange("p (c j f) -> p c j f", c=2, j=2)
            tb = rtb[half][:].rearrange("p (c j f) -> p c j f", c=2, j=2)
            cosv = cs[:, c0:c0 + 2, 0:64].unsqueeze(2).to_broadcast([128, 2, 2, 64])
            sinv = cs[:, c0:c0 + 2, 64:128].unsqueeze(2).to_broadcast([128, 2, 2, 64])
            rotary(q32[:, :, :, 0:64], q32[:, :, :, 64:128], cosv, sinv,
                   qkt[hb][:, c0:c0 + 2, :, 0:64], qkt[hb][:, c0:c0 + 2, :, 64:128], ta, tb, q32k, ["qkt%d.%d" % (hb, half)], half)

        def T_a(h, half):
            hb = h % 2
            c0 = 2 * half
            pb = bank16(6)
            tr_group([(pb[:, half * 512 + (ci * 2 + j) * 128:half * 512 + (ci * 2 + j + 1) * 128], qkt[hb][:, c0 + ci, j, :])
                      for ci in range(2) for j in range(2)], ["qkt%d.%d" % (hb, half)], bkh(6, half))
            copy_op("act", qkT[hb][:, :, c0 * 128:(c0 + 2) * 128].rearrange("p j (c t) -> p c j t", c=2),
                    pb[:, half * 512:(half + 1) * 512].rearrange("p (c j t) -> p c j t", c=2, j=2),
                    bkh(6, half), ["qkT%d.%d" % (hb, half)])

        def T_b(h, half):
            hb = h % 2
            c0 = 2 * half
            mm_each([(bank(4)[:, c * 128:(c + 1) * 128], qkT[hb][:, 1, c * 128:(c + 1) * 128], qkT[hb][:, 0, c * 128:(c + 1) * 128])
                     for c in (c0, c0 + 1)], ["qkT%d.%d" % (hb, half)], bkh(4, half))
            S.op("dve", lambda e: e.tensor_tensor(out=ST[hb][:, c0:c0 + 2, :],
                                                  in0=bank(4)[:, c0 * 128:(c0 + 2) * 128].rearrange("p (c t) -> p c t", c=2),
                                                  in1=maskT[:].unsqueeze(1).to_broadcast([128, 2, 128]), op=ALU.mult),
                 reads=bkh(4, half) + ["maskT"], writes=["ST%d.%d" % (hb, half)])

        def R_stage(h, c):
            hb = h % 2
            half = c // 2

            def fn_o(e):
                e.matmul(bank(5)[:, 0:256], lhsT=ST[hb][:, c, :], rhs=vh[hb][:, c, :], start=True, stop=False)
                return e.matmul(bank(5)[:, 0:256], lhsT=qkT[hb][:, 0, c * 128:(c + 1) * 128], rhs=Rg[:, h, :], start=False, stop=True)
            S.op("pe", fn_o, reads=["ST%d.%d" % (hb, half), "vh%d.%d" % (hb, c), "qkT%d.%d" % (hb, half), "Rg%d" % h], writes=bk(5))
            mm_group(bank(3)[:, 0:256], [(qkt[hb][:, c, 1, :], vh[hb][:, c, :])],
                     ["qkt%d.%d" % (hb, half), "vh%d.%d" % (hb, c)], bk(3))
            s_c = float(gC1[h] / (gC[h] ** (c + 1)))
            S.op("dve", lambda e: e.scalar_tensor_tensor(out=R[:, h, :], in0=bank(3)[:, 0:256], scalar=s_c, in1=R[:, h, :],
                                                         op0=ALU.mult, op1=ALU.add),
                 reads=bk(3) + ["R%d" % h], writes=["R%d" % h])
            S.op("dve", lambda e: e.tensor_scalar(out=Rg[:, h, :], in0=R[:, h, :], scalar1=float(gam[h] * gC[h] ** (c + 1)),
                                                  scalar2=None, op0=ALU.mult),
                 reads=["R%d" % h], writes=["Rg%d" % h])
            if c == NCH - 1:
                S.op("dve", lambda e: e.tensor_scalar(out=R[:, h, :], in0=R[:, h, :], scalar1=float(gC[h] ** NCH),
                                                      scalar2=None, op0=ALU.mult),
                     reads=["R%d" % h], writes=["R%d" % h])
            S.op("act", lambda e: e.activation(out=on32[hb][:, c, :], in_=bank(5)[:, 0:256], func=AF.Identity),
                 reads=bk(5), writes=["on32_%d.%d" % (hb, c)])

        nst = {}

        def N1(h):
            hb = h % 2
            s4, s4k = st4[cnt["st4"] % 2], "st4_%d" % (cnt["st4"] % 2)
            cnt["st4"] += 1
            for c in range(NCH):
                S.op("dve", (lambda c: lambda e: e.bn_stats(out=s4[:, c, :], in_=on32[hb][:, c, :]))(c),
                     reads=["on32_%d.%d" % (hb, c)], writes=[s4k])
            m, mk = next_sm()
            r, rk = next_sm()
            for c in range(NCH):
                S.op("dve", (lambda c: lambda e: e.bn_aggr(out=m[:, 2 * c:2 * c + 2], in_=s4[:, c, :]))(c), reads=[s4k], writes=[mk])
            var = m[:, 0:8].rearrange("p (c t) -> p c t", t=2)[:, :, 1]
            S.op("dve", lambda e: e.tensor_scalar(out=r[:, 0:4], in0=var, scalar1=EPS, scalar2=None, op0=ALU.add), reads=[mk], writes=[rk])
            nst[h] = (m, mk, r, rk)

        def N2(h):
            m, mk, r, rk = nst[h]
            mean = m[:, 0:8].rearrange("p (c t) -> p c t", t=2)[:, :, 0]
            S.op("act", lambda e: e.activation(out=r[:, 0:4], in_=r[:, 0:4], func=AF.Sqrt), reads=[rk], writes=[rk])
            S.op("dve", lambda e: e.reciprocal(out=r[:, 0:4], in_=r[:, 0:4]), reads=[rk], writes=[rk])
            S.op("dve", lambda e: e.scalar_tensor_tensor(out=r[:, 4:8], in0=mean, scalar=-1.0, in1=r[:, 0:4],
                                                         op0=ALU.mult, op1=ALU.mult), reads=[mk, rk], writes=[rk])

        def N3(h):
            hb = h % 2
            m, mk, r, rk = nst[h]
            for c in range(NCH):
                S.op("act", (lambda c: lambda e: e.activation(out=ybp[hb][:, c, :], in_=on32[hb][:, c, :], func=AF.Identity,
                                                               bias=r[:, 4 + c:5 + c], scale=r[:, c:c + 1]))(c),
                     reads=[rk, "on32_%d.%d" % (hb, c)], writes=["ybp.%d" % c])

        def Y_stage(h):
            hb = h % 2
            pb = bank16(7)
            tr_group([(pb[:, (c * 2 + j) * 128:(c * 2 + j + 1) * 128], ybp[hb][:, c, j * 128:(j + 1) * 128])
                      for c in range(NCH) for j in range(2)], ["ybp.%d" % c for c in range(NCH)], bk(7))
            copy_op("act", preT[:, 2 * h:2 * h + 2, :].rearrange("p j (c t) -> p c j t", c=NCH),
                    pb.rearrange("p (c j t) -> p c j t", c=NCH, j=2), bk(7), ["preT.%d" % (2 * h), "preT.%d" % (2 * h + 1)])

        def bstep(slots, h, tile=None):
            sq = slots[0] if h < NH else None
            hr = h - 1
            if h == 1 and tile is not None and tile + 1 < NT:
                for c in range(NCH):
                    x_dma(x_d, "m", tile + 1, c)
            rv = 0 <= hr < NH
            for c in range(NCH):
                if h < NH:
                    if c == 1 and rv:
                        P_gemm(h, c, sq, mid=lambda: T_b(hr, 1), split=8)
                    else:
                        P_gemm(h, c, sq)
                        if c == 3:
                            T_b(h, 0)
                    P_evac(h, c)
                    if c % 2 == 1:
                        P_post(h, c // 2)
                else:
                    if c == 1 and rv:
                        T_b(hr, 1)
                if rv:
                    R_stage(hr, c)
                    if c == 0:
                        T_a(hr, 1)
                if c == 2 and h < NH:
                    T_a(h, 0)
                if h - 2 >= 0:
                    if c == 0:
                        N1(h - 2)
                    elif c == 1:
                        N2(h - 2)
                    elif c == 2:
                        N3(h - 2)
            if 0 <= h - 2 < NH:
                Y_stage(h - 2)

        def zstep(slot, j, extra=None, banks=None):
            for l in range(4):
                ct = 4 * j + l
                if banks is None:
                    pf, pk = next_pf()
                else:
                    bi_ = banks[l % len(banks)]
                    pf, pk = bank(bi_), bk(bi_)
                mm_group(pf, [(wr[slot][:, kt, l * 128:(l + 1) * 128], xT[:, kt, :]) for kt in range(KT)],
                         XT_K + wkeys(slot), pk)
                tz, tzk = next_tmp()
                S.op("act", (lambda pf, tz: lambda e: e.activation(out=tz[:], in_=pf, func=AF.Silu))(pf, tz), reads=pk, writes=[tzk])
                S.op("dve", (lambda tz, ct: lambda e: e.tensor_tensor(out=preT[:, ct, :], in0=preT[:, ct, :], in1=tz[:], op=ALU.mult))(tz, ct),
                     reads=[tzk, "preT.%d" % ct], writes=["preT.%d" % ct])
                if extra is not None:
                    extra(l)

        def drain_a(l):
            if l == 1:
                T_b(NH - 1, 1)
            R_stage(NH - 1, l)
            if l == 0:
                T_a(NH - 1, 1)
            if l == 0:
                N1(NH - 2)
            elif l == 1:
                N2(NH - 2)
            elif l == 2:
                N3(NH - 2)
            elif l == 3:
                Y_stage(NH - 2)

        def drain_b(l):
            if l == 0:
                N1(NH - 1)
            elif l == 1:
                N2(NH - 1)
            elif l == 2:
                N3(NH - 1)
            elif l == 3:
                Y_stage(NH - 1)

        def drain_c(l):
            pass

        def gstep(slot, j):
            for l in range(2):
                dt_ = 2 * j + l
                pa, pak = next_pf()
                pb_, pbk = next_pf()
                mm_group(pa, [(wr[slot][:, kt, l * 128:(l + 1) * 128], xT[:, kt, :]) for kt in range(KT)],
                         XT_K + wkeys(slot), pak)
                mm_group(pb_, [(wr[slot][:, kt, 256 + l * 128:256 + (l + 1) * 128], xT[:, kt, :]) for kt in range(KT)],
                         XT_K + wkeys(slot), pbk)
                ta, tak = next_tmp()
                tb, tbk = next_tmp()
                S.op("act", (lambda pa, ta, dt_: lambda e: e.activation(out=ta[:], in_=pa, func=AF.Sigmoid,
                                                                         bias=bgate[:, dt_:dt_ + 1], scale=1.0))(pa, ta, dt_),
                     reads=pak + ["bgate"], writes=[tak])
                S.op("act", (lambda pb_, tb, dt_: lambda e: e.activation(out=tb[:], in_=pb_, func=AF.Sigmoid,
                                                                          bias=bgate[:, 16 + dt_:17 + dt_], scale=1.0))(pb_, tb, dt_),
                     reads=pbk + ["bgate"], writes=[tbk])
                S.op("dve", (lambda ta, dt_: lambda e: e.tensor_tensor(out=ta[:], in0=ta[:], in1=yaT[:, dt_, :], op=ALU.mult))(ta, dt_),
                     reads=[tak, "yaT.%d" % dt_], writes=[tak])
                S.op("dve", (lambda tb, dt_: lambda e: e.tensor_tensor(out=tb[:], in0=tb[:], in1=ybT[:, dt_, :], op=ALU.mult))(tb, dt_),
                     reads=[tbk, "ybT.%d" % dt_], writes=[tbk])
                S.op("dve", (lambda ta, tb, dt_: lambda e: e.tensor_tensor(out=yaT[:, dt_, :], in0=ta[:], in1=tb[:], op=ALU.add))(ta, tb, dt_),
                     reads=[tak, tbk], writes=["yaT.%d" % dt_])

        def final_ln_chunk(tile, c):
            m, mk, r, rk = rstd_batch([stv[:, c, :, :].rearrange("p a b -> p (a b)")], ["stv.%d" % c], want_nb=True)
            S.op("act", lambda e: e.activation(out=res[:, c, :], in_=res[:, c, :], func=AF.Identity, bias=r[:, 4:5], scale=r[:, 0:1]),
                 reads=[rk, "res.%d" % c], writes=["res.%d" % c])
            S.op("dve", lambda e: e.tensor_tensor(out=res[:, c, :], in0=res[:, c, :], in1=lncg[:], op=ALU.mult),
                 reads=["res.%d" % c, "lncg"], writes=["res.%d" % c])
            S.op("dve", lambda e: e.tensor_tensor(out=res[:, c, :], in0=res[:, c, :], in1=lncb[:], op=ALU.add),
                 reads=["res.%d" % c, "lncb"], writes=["res.%d" % c])
            r0 = (tile * NCH + c) * 128
            S.op("sp", lambda e: e.dma_start(out=out_d[r0:r0 + 128, :], in_=res[:, c, :]),
                 reads=["res.%d" % c], writes=["out.%d.%d" % (tile, c)], dma_key="out")

        def ostep(slot, nb, tile):
            for c in range(NCH):
                pf, pk = next_pf()
                mm_group(pf, [(yaT[:, kt, c * 128:(c + 1) * 128], wr[slot][:, kt, :]) for kt in range(KT)],
                         YAT_K + wkeys(slot), pk)
                S.op("dve", (lambda pf, c: lambda e: e.scalar_tensor_tensor(
                    out=res[:, c, nb * 512:(nb + 1) * 512], in0=res[:, c, nb * 512:(nb + 1) * 512], scalar=ALPHA, in1=pf,
                    op0=ALU.mult, op1=ALU.add))(pf, c),
                    reads=pk + ["res.%d" % c], writes=["res.%d" % c])
                S.op("dve", (lambda c: lambda e: e.bn_stats(out=stv[:, c, nb, :], in_=res[:, c, nb * 512:(nb + 1) * 512]))(c),
                     reads=["res.%d" % c], writes=["stv.%d" % c])
                if nb == 3:
                    final_ln_chunk(tile, c)

        steps = []
        wmeta = []
        wctx = {"kind": "p", "tile": 0, "nid": 0}

        def W(pieces_list, fn):
            steps.append((pieces_list, fn))
            for _ in pieces_list:
                li = wctx["nid"]
                if wctx["kind"] == "p":
                    ctile = li % 2
                else:
                    ctile = 0 if ((li - NH) % 5) < 3 else 1
                t = wctx["tile"]
                mode = "load" if t < ctile else ("load_store" if t == ctile else "cached")
                wmeta.append((wctx["kind"], li, mode))
                wctx["nid"] += 1

        def Nw(fn):
            steps.append(([], fn))

        for ptile in range(NT):
            wctx.update(kind="p", tile=ptile, nid=0)
            Nw((lambda ptile: lambda s: (build_xT(xp_d, "p", ptile), load_cs(cs_pre_d, ptile)))(ptile))
            for h in range(NH):
                W([[(0, w_in[:, C_K + h * 128:C_K + (h + 1) * 128]), (128, w_in[:, C_RV + h * 256:C_RV + (h + 1) * 256])]],
                  (lambda h, ptile: lambda s: prefix_step(s[0], h, ptile))(h, ptile))
            Nw(lambda s: prefix_state(NH - 1))

        def after_prefix(s):
            for h in range(NH):
                S.op("act", (lambda h: lambda e: e.activation(out=Rg[:, h, :], in_=R[:, h, :], func=AF.Identity, scale=float(gam[h])))(h),
                     reads=["R%d" % h], writes=["Rg%d" % h])
        Nw(after_prefix)

        for tile in range(NT):
            wctx.update(kind="m", tile=tile, nid=NH)
            Nw((lambda tile: lambda s: (build_xT(x_d, "m", tile), load_cs(cs_main_d, tile)))(tile))
            for nb in range(4):
                W([[(0, w_in[:, C_V + nb * 512:C_V + (nb + 1) * 512])]], (lambda nb: lambda s: a1(s[0], nb))(nb))
            Nw(lambda s: a2())
            for j in range(8):
                W([[(0, w_in[:, C_U + j * 256:C_U + (j + 1) * 256]), (256, w_in[:, C_Z + j * 256:C_Z + (j + 1) * 256])]],
                  (lambda j: lambda s: a3(s[0], j))(j))
            Nw(lambda s: alias(VLN_K, YBT_K))
            for j in range(4):
                W([[(0, w_oa[:, j * 512:(j + 1) * 512])]], (lambda j: lambda s: proj_fm(s[0], j, preT, PRE_K, yaT, "yaT"))(j))
            for h in range(NH):
                W([[(0, w_in[:, C_Q + h * 128:C_Q + (h + 1) * 128]), (128, w_in[:, C_K + h * 128:C_K + (h + 1) * 128]),
                    (256, w_in[:, C_RV + h * 256:C_RV + (h + 1) * 256])]],
                  (lambda h, tile: lambda s: bstep(s, h, tile))(h, tile))
            zx = [(drain_a, [0, 1, 2]), (drain_b, [0, 1, 2]), (drain_c, [0, 1, 2]), (None, None)]
            for j in range(4):
                W([[(0, w_in[:, C_RZ + j * 512:C_RZ + (j + 1) * 512])]],
                  (lambda j: lambda s: zstep(s[0], j, extra=zx[j][0], banks=zx[j][1]))(j))
            for j in range(4):
                W([[(0, w_ob[:, j * 512:(j + 1) * 512])]], (lambda j: lambda s: proj_fm(s[0], j, preT, PRE_K, ybT, "ybT"))(j))
            for j in range(8):
                W([[(0, w_in[:, C_GA + j * 256:C_GA + (j + 1) * 256]), (256, w_in[:, C_GB + j * 256:C_GB + (j + 1) * 256])]],
                  (lambda j: lambda s: gstep(s[0], j))(j))

            def pre_o(s, tile=tile):
                alias(YBT_K + PRE_K, RES_K)
                for c in range(NCH):
                    r0 = (tile * NCH + c) * 128
                    S.op("sp", (lambda c, r0: lambda e: e.dma_start(out=res[:, c, :], in_=x_d[r0:r0 + 128, :]))(c, r0),
                         writes=["res.%d" % c], dma_key="res.%d" % c)
                if tile + 1 < NT:
                    for c in range(NCH):
                        x_dma(x_d, "m", tile + 1, c)
            Nw(pre_o)
            for nb in range(4):
                W([[(0, w_out[:, nb * 512:(nb + 1) * 512])]], (lambda nb, tile: lambda s: ostep(s[0], nb, tile))(nb, tile))
            if tile + 1 < NT:
                Nw(lambda s: alias(RES_K, VLN_K + PRE_K))

        wblocks = []
        for (pl, fn) in steps:
            for pieces in pl:
                wblocks.append(pieces)

        NCACHE = NH + 44
        wsc = nc.dram_tensor("wsc", [NCACHE, 128, KT * 512], BF16).ap()

        def issue_load(bi):
            slot = bi % NSLOT
            kind, cid, mode = wmeta[bi]
            assert cid < NCACHE
            flat = wr[slot][:].rearrange("p k n -> p (k n)")
            if mode == "cached":
                S.op("pool", lambda e: e.dma_start(out=flat, in_=wsc[cid]), reads=["wsc.%d" % cid], writes=wkeys(slot),
                     dma_key="wr%d" % slot, dma_group=bi)
                return
            pi = 0
            for (off, src) in wblocks[bi]:
                n = src.shape[1]
                srcv = src.rearrange("(kt p) n -> p kt n", p=128)
                nparts = 4 if n >= 256 else 2
                kper = KT // nparts
                for q in range(nparts):
                    S.op("pool", (lambda slot, off, n, srcv, q, kper: lambda e: e.dma_start(
                        out=wr[slot][:, q * kper:(q + 1) * kper, off:off + n], in_=srcv[:, q * kper:(q + 1) * kper, :]))(slot, off, n, srcv, q, kper),
                        writes=["wr%d.%d" % (slot, pi)], dma_key="wr%d" % slot, dma_group=bi)
                    pi += 1
            assert pi <= MAXP
            if mode != "load_store":
                return
            S.op("sp", lambda e: e.dma_start(out=wsc[cid], in_=flat), reads=wkeys(slot), writes=["wsc.%d" % cid],
                 dma_key="wsc_st%d" % slot)

        issued = 0
        bi = 0
        for (pl, fn) in steps:
            limit = min(len(wblocks), bi + NSLOT)
            while issued < limit:
                issue_load(issued)
                issued += 1
            slots = [(bi + k) % NSLOT for k in range(len(pl))]
            fn(slots)
            bi += len(pl)

        S.op("sp", None, reads=["out.%d.%d" % (t_, c_) for t_ in range(NT) for c_ in range(NCH)])
        build_program.sbuf_left = nc.sbuf_bytes_remaining
        S.emit()
        build_program.stats = S.stats
    return nc


def _tables(hf):
    lg = np.log1p(-np.exp2(-5.0 - np.arange(NH, dtype=np.float64)))
    freqs = (10000.0 ** (-np.arange(0, 128, 2, dtype=np.float32) / np.float32(128))).astype(np.float32)
    t = np.arange(128, dtype=np.float64)

    def cs_table(base):
        pos = (base + np.arange(TOK)).astype(np.float32)
        ang = (pos[:, None] * freqs[None, :]).astype(np.float32)
        c = np.cos(ang.astype(np.float64)).astype(np.float32)
        s = np.sin(ang.astype(np.float64)).astype(np.float32)
        tab = np.concatenate([c, s], axis=1).reshape(16, 128, 128).transpose(1, 0, 2)
        return np.ascontiguousarray(tab, dtype=np.float32)

    cs_main = cs_table(hf * TOK)
    cs_pre = cs_table(0)
    dec = np.zeros((128, 16), np.float32)
    dec[:, 0:8] = np.exp(lg[None, :] * t[:, None])
    dec[:, 8:16] = np.exp(-lg[None, :] * t[:, None]) * (128.0 ** -0.5)
    s_glob = np.arange(TOK, dtype=np.float64)
    dkp = (np.exp(lg[None, :] * (TOK - 1 - s_glob)[:, None]) * (128.0 ** -0.5)).astype(np.float32)
    dkp = np.ascontiguousarray(dkp.reshape(16, 128, 8).transpose(1, 0, 2))
    maskT = (np.arange(128)[None, :] >= np.arange(128)[:, None]).astype(np.float32)
    mask_ts = np.ascontiguousarray(maskT.T)
    return dict(cs_main=cs_main, cs_pre=cs_pre, dec=dec, dkp=dkp, maskT=maskT, mask_ts=mask_ts)


_NC_CACHE = {}


def kernel(x, w_in, b_gate, ln_v_g, ln_v_b, w_s, b_s, w_oa, w_ob, w_out, ln_g, ln_b):
    x = np.asarray(x, dtype=np.float32)
    B, SEQ, _ = x.shape
    if "nc" not in _NC_CACHE:
        _NC_CACHE["nc"] = build_program()
    nc = _NC_CACHE["nc"]
    shared = dict(
        w_in=np.ascontiguousarray(np.asarray(w_in, np.float32)[0]),
        w_oa=np.ascontiguousarray(np.asarray(w_oa, np.float32)[0]),
        w_ob=np.ascontiguousarray(np.asarray(w_ob, np.float32)[0]),
        w_out=np.ascontiguousarray(np.asarray(w_out, np.float32)[0]),
        b_gate=np.ascontiguousarray(np.asarray(b_gate, np.float32)[0]),
        ln_v_g=np.ascontiguousarray(np.asarray(ln_v_g, np.float32)[0]),
        ln_v_b=np.ascontiguousarray(np.asarray(ln_v_b, np.float32)[0]),
        w_s=np.ascontiguousarray(np.asarray(w_s, np.float32)[0]),
        b_s=np.ascontiguousarray(np.asarray(b_s, np.float32)[0]),
        ln_g=np.ascontiguousarray(np.asarray(ln_g, np.float32)[0]),
        ln_b=np.ascontiguousarray(np.asarray(ln_b, np.float32)[0]),
    )
    tabs = [_tables(0), _tables(1)]
    zeros = np.zeros((TOK, D), np.float32)
    in_maps = []
    for c in range(NCORES):
        b, hf = c // 2, c % 2
        m = dict(shared)
        m["x"] = np.ascontiguousarray(x[b, hf * TOK:(hf + 1) * TOK, :])
        m["xp"] = np.ascontiguousarray(x[b, 0:TOK, :]) if hf == 1 else zeros
        m.update(tabs[hf])
        in_maps.append(m)
    res = run_bass_kernel_spmd(nc, in_maps, core_ids=list(range(NCORES)))
    out = np.empty((B, SEQ, D), np.float32)
    for c in range(NCORES):
        b, hf = c // 2, c % 2
        out[b, hf * TOK:(hf + 1) * TOK, :] = res.results[c]["out"]
    return out
```

```python
import math
from contextlib import ExitStack

import numpy as np
import concourse.bass as bass
import concourse.mybir as mybir
from concourse.bass_utils import run_bass_kernel_spmd

F32 = mybir.dt.float32
BF16 = mybir.dt.bfloat16
AF = mybir.ActivationFunctionType
ALU = mybir.AluOpType

D = 2048
NCORES = 8
TOK = 2048
NCH = 4
T = 512
NT = TOK // T
KT = 16
NH = 8
ALPHA = float(2.0 ** 0.25)
EPS = 1e-5
NSLOT = 3
MAXP = 12
C_U, C_V, C_Z, C_Q, C_K, C_RV, C_RZ, C_GA, C_GB = 0, 2048, 4096, 6144, 7168, 8192, 10240, 12288, 14336

ENGS = ("pe", "act", "dve", "pool", "sp")


class _Op:
    __slots__ = ("idx", "eng", "fn", "deps", "dma_key", "signal", "count", "sem", "grp", "gcount")

    def __init__(self, idx, eng, fn, deps, dma_key, grp=None):
        self.idx = idx
        self.eng = eng
        self.fn = fn
        self.deps = deps
        self.dma_key = dma_key
        self.signal = False
        self.count = 0
        self.sem = None
        self.grp = grp if grp is not None else ("_", idx)
        self.gcount = 0


class Sched:
    def __init__(self, nc):
        self.nc = nc
        self.ops = []
        self.lastw = {}
        self.readers = {}

    def op(self, eng, fn, reads=(), writes=(), dma_key=None, dma_group=None):
        deps = set()
        for k in reads:
            w = self.lastw.get(k)
            if w is not None:
                deps.add(w)
            if k.startswith("PF"):
                for r in self.readers.get(k, ()):
                    if self.ops[r].eng != eng:
                        deps.add(r)
        for k in writes:
            w = self.lastw.get(k)
            if w is not None:
                deps.add(w)
            for r in self.readers.get(k, ()):
                deps.add(r)
        idx = len(self.ops)
        o = _Op(idx, eng, fn, deps, dma_key, dma_group)
        self.ops.append(o)
        for k in reads:
            self.readers.setdefault(k, []).append(idx)
        for k in writes:
            self.lastw[k] = idx
            self.readers[k] = []
        return idx

    def emit(self):
        nc = self.nc
        ops = self.ops
        for o in ops:
            if o.eng == "pe" and o.dma_key is None:
                o.deps = {d for d in o.deps
                          if not (ops[d].eng == "pe" and ops[d].dma_key is None)}
        for o in ops:
            for d in o.deps:
                ops[d].signal = True
        with ExitStack() as es:
            esem = {e: es.enter_context(nc.semaphore("s_" + e)) for e in ENGS}
            dsem = {}
            ecount = {e: 0 for e in ENGS}
            dcount = {}
            for o in ops:
                if o.dma_key is not None:
                    if o.dma_key not in dsem:
                        dsem[o.dma_key] = es.enter_context(nc.semaphore("d_%d" % len(dsem)))
                        dcount[o.dma_key] = 0
                    dcount[o.dma_key] += 16
                    o.sem = dsem[o.dma_key]
                    o.count = dcount[o.dma_key]
                    o.signal = True
                elif o.signal:
                    ecount[o.eng] += 1
                    o.sem = esem[o.eng]
                    o.count = ecount[o.eng]
            self.stats = dict(ecount)
            self.stats["n_dma_sems"] = len(dsem)
            self.stats["n_ops"] = len(ops)
            streams = {e: [o for o in ops if o.eng == e] for e in ENGS}
            gend = {}
            for o in ops:
                if o.dma_key is not None:
                    k = (o.dma_key, o.grp)
                    gend[k] = max(gend.get(k, 0), o.count)
            for o in ops:
                if o.dma_key is not None:
                    o.gcount = gend[(o.dma_key, o.grp)]

            def dma_count(p, cons_idx):
                return p.gcount

            def run(eng_name, e):
                waited = {}
                for o in streams[eng_name]:
                    need = {}
                    for d in o.deps:
                        p = ops[d]
                        sid = id(p.sem)
                        pc = p.count if p.dma_key is None else dma_count(p, o.idx)
                        if waited.get(sid, 0) < pc:
                            if sid not in need or need[sid][1] < pc:
                                need[sid] = (p.sem, pc)
                    for sid, (sem, cnt) in need.items():
                        e.wait_ge(sem, cnt)
                        waited[sid] = cnt
                    if o.fn is None:
                        continue
                    ins = o.fn(e)
                    if o.signal:
                        ins.then_inc(o.sem, 16 if o.dma_key is not None else 1)

            with nc.Block() as block:
                @block.tensor
                def _(e):
                    run("pe", e)

                @block.scalar
                def _(e):
                    run("act", e)

                @block.vector
                def _(e):
                    run("dve", e)

                @block.gpsimd
                def _(e):
                    run("pool", e)

                @block.sync
                def _(e):
                    run("sp", e)


def build_program(taps=None):
    nc = bass.Bass("TRN2", target_bir_lowering=False)

    def din(name, shape):
        return nc.dram_tensor(name, list(shape), F32, kind="ExternalInput").ap()

    x_d = din("x", [TOK, D])
    xp_d = din("xp", [TOK, D])
    w_in = din("w_in", [D, 16384])
    w_oa = din("w_oa", [D, D])
    w_ob = din("w_ob", [D, D])
    w_out = din("w_out", [D, D])
    b_gate = din("b_gate", [4096])
    ln_v_g = din("ln_v_g", [D])
    ln_v_b = din("ln_v_b", [D])
    w_s = din("w_s", [8, 128, 128])
    b_s = din("b_s", [8, 128])
    ln_g = din("ln_g", [D])
    ln_b = din("ln_b", [D])
    cs_main_d = din("cs_main", [128, 16, 128])
    cs_pre_d = din("cs_pre", [128, 16, 128])
    dec_d = din("dec", [128, 16])
    dkp_d = din("dkp", [128, 16, 8])
    maskT_d = din("maskT", [128, 128])
    maskts_d = din("mask_ts", [128, 128])
    out_d = nc.dram_tensor("out", [TOK, D], F32, kind="ExternalOutput").ap()

    gam = [1.0 - 2.0 ** (-5.0 - h) for h in range(NH)]
    gC = [g ** 128 for g in gam]
    gC1 = [g ** 127 for g in gam]

    with ExitStack() as es:
        es.enter_context(nc.allow_non_contiguous_dma(reason="small param column loads"))

        def sb(name, shape, dt):
            return es.enter_context(nc.sbuf_tensor("sb_" + name, list(shape), dt))

        xT = sb("xT", [128, KT, T], BF16)
        yaT = sb("yaT", [128, KT, T], BF16)
        X = sb("X", [128, 16384], BF16)
        vln = X[:, 0:8192].rearrange("p (c n) -> p c n", c=NCH)
        ybT = X[:, 0:8192].rearrange("p (k n) -> p k n", k=KT)
        preT = X[:, 8192:16384].rearrange("p (k n) -> p k n", k=KT)
        X32 = X[:].bitcast(F32)
        res = X32.rearrange("p (c n) -> p c n", c=NCH)
        wr = [sb("wr%d" % i, [128, KT, 512], BF16) for i in range(NSLOT)]
        lncg = sb("lncg", [128, D], F32)
        lncb = sb("lncb", [128, D], F32)
        xb = [sb("xb%d" % i, [128, D], BF16) for i in range(4)]
        R = sb("R", [128, NH, 256], F32)
        Rg = sb("Rg", [128, NH, 256], BF16)
        wsT = sb("wsT", [128, 8, 128], BF16)
        bias2 = sb("bias2", [128, 16, 128], F32)
        cs = sb("cs", [128, NCH, 128], F32)
        dec = sb("dec", [128, 16], F32)
        dkp = sb("dkp", [128, 16, 8], F32)
        maskT = sb("maskT", [128, 128], F32)
        ident = sb("ident", [128, 128], BF16)
        ones = sb("ones", [128, 128], BF16)
        lnvg = sb("lnvg", [128, 16], F32)
        lnvb = sb("lnvb", [128, 16], F32)
        bgate = sb("bgate", [128, 32], F32)
        scratch = sb("scratch", [128, 2], F32)
        tmpF = [sb("tmpF%d" % i, [128, 512], F32) for i in range(4)]
        qk32 = [sb("qk32_%d" % i, [128, 512], F32) for i in range(2)]
        rta = [sb("rta%d" % i, [128, 256], F32) for i in range(2)]
        rtb = [sb("rtb%d" % i, [128, 256], F32) for i in range(2)]
        qkt = [sb("qkt%d" % i, [128, NCH, 2, 128], BF16) for i in range(2)]
        qkT = [sb("qkT%d" % i, [128, 2, T], BF16) for i in range(2)]
        vh = [sb("vh%d" % i, [128, NCH, 256], BF16) for i in range(2)]
        ST = [sb("ST%d" % i, [128, NCH, 128], BF16) for i in range(2)]
        on32 = [sb("on32_%d" % i, [128, NCH, 256], BF16) for i in range(2)]
        ybp = [sb("ybp0", [128, NCH, 256], BF16)] * 2
        stv = sb("stv", [128, NCH, 4, 6], F32)
        st4 = [sb("st4_%d" % i, [128, 4, 6], F32) for i in range(2)]
        sm = [sb("sm%d" % i, [128, 8], F32) for i in range(8)]

        PS = es.enter_context(nc.psum_tensor("ps_all", [128, 8, 512], F32))

        def bank(i):
            return PS[:, i, :]

        def bank16(i):
            return PS[:, i, :].bitcast(BF16)

        def bk(i):
            return ["PF%d.a" % i, "PF%d.b" % i]

        def bkh(i, half):
            return ["PF%d.a" % i, "PF%d.b" % i]

        S = Sched(nc)
        cnt = {"tmp": 0, "sm": 0, "st4": 0, "pf": 0, "ev": 0}

        def next_tmp():
            i = cnt["tmp"] % 4
            cnt["tmp"] += 1
            return tmpF[i], "tmpF%d" % i

        def next_sm():
            i = cnt["sm"] % 8
            cnt["sm"] += 1
            return sm[i], "sm%d" % i

        def next_pf():
            i = cnt["pf"] % 6
            cnt["pf"] += 1
            return bank(i), bk(i)

        def ev_eng():
            cnt["ev"] += 1
            return "act" if cnt["ev"] % 2 else "dve"

        def copy_op(eng, out, in_, reads, writes):
            if eng == "act":
                S.op("act", lambda e: e.activation(out=out, in_=in_, func=AF.Identity), reads=reads, writes=writes)
            else:
                S.op("dve", lambda e: e.tensor_copy(out=out, in_=in_), reads=reads, writes=writes)

        def alias(old, new):
            S.op("dve", lambda e: e.memset(scratch[:, 0:1], 0.0), writes=list(old) + list(new) + ["scratch"])

        def wkeys(slot):
            return ["wr%d.%d" % (slot, j) for j in range(MAXP)]

        def mm_group(out, pairs, reads, writes):
            def fn(e):
                n = len(pairs)
                ins = None
                for i, (l, r) in enumerate(pairs):
                    ins = e.matmul(out, lhsT=l, rhs=r, start=(i == 0), stop=(i == n - 1))
                return ins
            S.op("pe", fn, reads=reads, writes=writes)

        def mm_each(items, reads, writes):
            def fn(e):
                ins = None
                for (o, l, r) in items:
                    ins = e.matmul(o, lhsT=l, rhs=r, start=True, stop=True)
                return ins
            S.op("pe", fn, reads=reads, writes=writes)

        def tr_group(items, reads, writes):
            def fn(e):
                ins = None
                for (o, i) in items:
                    ins = e.transpose(out=o, in_=i, identity=ident[:])
                return ins
            S.op("pe", fn, reads=reads + ["ident"], writes=writes)

        bsb = X32[:, 0:1024].rearrange("p (g t) -> p g t", g=8)
        Wb = X32[:, 1024:2048].rearrange("p (g t) -> p g t", g=8)
        ws32 = X32[:, 2048:3072].rearrange("p (g s) -> p g s", g=8)
        mts = X32[:, 3072:3200]
        wsm = X[:, 8192:9216].rearrange("p (g s) -> p g s", g=8)

        S.op("dve", lambda e: e.memset(tmpF[0][:, 0:128], 0.0), writes=["tmpF0"])
        S.op("pool", lambda e: e.affine_select(out=tmpF[0][:, 0:128], in_=tmpF[0][:, 0:128], pattern=[[-1, 128]],
                                                 compare_op=ALU.not_equal, fill=1.0, base=0, channel_multiplier=1),
             reads=["tmpF0"], writes=["tmpF0"])
        S.op("dve", lambda e: e.tensor_copy(out=ident[:], in_=tmpF[0][:, 0:128]), reads=["tmpF0"], writes=["ident"])
        S.op("dve", lambda e: e.memset(ones[:], 1.0), writes=["ones"])
        S.op("dve", lambda e: e.memset(R[:], 0.0), writes=["R%d" % h for h in range(NH)])

        def ld(dst, src, key):
            S.op("sp", lambda e: e.dma_start(out=dst, in_=src), writes=[key], dma_key=key)

        ld(dec[:], dec_d, "dec")
        ld(dkp[:], dkp_d, "dkp")
        ld(maskT[:], maskT_d, "maskT")
        ld(lnvg[:], ln_v_g.rearrange("(c p) -> p c", p=128), "lnvg")
        ld(lnvb[:], ln_v_b.rearrange("(c p) -> p c", p=128), "lnvb")
        ld(bgate[:], b_gate.rearrange("(c p) -> p c", p=128), "bgate")
        ld(lncg[:], ln_g.partition_broadcast(128), "lncg")
        ld(lncb[:], ln_b.partition_broadcast(128), "lncb")
        S.op("sp", lambda e: e.dma_start(out=X32[:, 0:1024], in_=b_s.rearrange("g t -> (g t)").partition_broadcast(128)),
             writes=["X.bsb"], dma_key="X.bsb")
        S.op("sp", lambda e: e.dma_start(out=ws32, in_=w_s.rearrange("g t s -> t g s")), writes=["X.ws32"], dma_key="X.ws32")
        S.op("sp", lambda e: e.dma_start(out=mts, in_=maskts_d), writes=["X.mts"], dma_key="X.mts")
        S.op("dve", lambda e: e.tensor_tensor(out=wsm, in0=ws32, in1=mts.unsqueeze(1).to_broadcast([128, 8, 128]), op=ALU.mult),
             reads=["X.ws32", "X.mts"], writes=["X.wsm"])
        tr_group([(bank16(6)[:, g * 128:(g + 1) * 128], wsm[:, g, :]) for g in range(8)], ["X.wsm"], bk(6))
        S.op("dve", lambda e: e.tensor_copy(out=wsT[:].rearrange("p g t -> p (g t)"), in_=bank16(6)), reads=bk(6), writes=["wsT"])
        for hlf in range(2):
            mm_group(bank(hlf), [(ones[:], wsT[:, hlf * 4:(hlf + 1) * 4, :].rearrange("p g t -> p (g t)"))],
                     ["ones", "wsT"], bk(hlf))
            S.op("dve", (lambda hlf: lambda e: e.tensor_copy(out=X32[:, 1024 + hlf * 512:1024 + (hlf + 1) * 512], in_=bank(hlf)))(hlf),
                 reads=bk(hlf), writes=["X.Wb%d" % hlf])
        for ct in range(16):
            g = ct // 2
            S.op("dve", (lambda ct, g: lambda e: e.scalar_tensor_tensor(
                out=bias2[:, ct, :], in0=Wb[:, g, :], scalar=lnvb[:, ct:ct + 1], in1=bsb[:, g, :],
                op0=ALU.mult, op1=ALU.add))(ct, g),
                reads=["X.Wb%d" % (g // 4), "lnvb", "X.bsb"], writes=["bias2"])
        VLN_K = ["vln.%d" % c for c in range(NCH)]
        PRE_K = ["preT.%d" % k for k in range(KT)]
        YBT_K = ["ybT.%d" % k for k in range(KT)]
        RES_K = ["res.%d" % c for c in range(NCH)]
        XT_K = ["xT.c%d.%d" % (c, hh) for c in range(NCH) for hh in range(2)]
        YAT_K = ["yaT.%d" % k for k in range(KT)]
        alias(["X.bsb", "X.ws32", "X.mts", "X.wsm", "X.Wb0", "X.Wb1"], VLN_K + PRE_K)

        xdone = set()

        def x_dma(src, tag, tile, c):
            if (tag, tile, c) in xdone:
                return
            xdone.add((tag, tile, c))
            b = c
            r0 = (tile * NCH + c) * 128
            S.op("pool", lambda e: e.dma_start(out=xb[b][:], in_=src[r0:r0 + 128, :]), writes=["xb%d" % b], dma_key="xb%d" % b)

        def build_xT(src, tag, tile):
            for c in range(NCH):
                x_dma(src, tag, tile, c)
                b = c
                for hlf in range(2):
                    bi_ = (2 * c + hlf) % 8
                    pb = bank16(bi_)
                    tr_group([(pb[:, j * 128:(j + 1) * 128], xb[b][:, (hlf * 8 + j) * 128:(hlf * 8 + j + 1) * 128])
                              for j in range(8)], ["xb%d" % b], bk(bi_))
                    copy_op("act" if tag == "m" else ev_eng(), xT[:, hlf * 8:(hlf + 1) * 8, c * 128:(c + 1) * 128],
                            pb.rearrange("p (k t) -> p k t", k=8), bk(bi_), ["xT.c%d.%d" % (c, hlf)])

        def load_cs(src_d, tile):
            S.op("sp", lambda e: e.dma_start(out=cs[:], in_=src_d[:, tile * NCH:(tile + 1) * NCH, :]), writes=["cs"], dma_key="cs")

        def rotary(x1, x2, cosv, sinv, d1, d2, ta, tb, srck, dstk, par):
            ka, kb = "rta%d" % par, "rtb%d" % par
            S.op("dve", lambda e: e.tensor_tensor(out=ta, in0=x1, in1=cosv, op=ALU.mult), reads=[srck, "cs"], writes=[ka])
            S.op("dve", lambda e: e.tensor_tensor(out=tb, in0=x2, in1=sinv, op=ALU.mult), reads=[srck, "cs"], writes=[kb])
            S.op("dve", lambda e: e.tensor_tensor(out=d1, in0=ta, in1=tb, op=ALU.subtract), reads=[ka, kb], writes=list(dstk))
            S.op("dve", lambda e: e.tensor_tensor(out=ta, in0=x1, in1=sinv, op=ALU.mult), reads=[srck, "cs"], writes=[ka])
            S.op("dve", lambda e: e.tensor_tensor(out=tb, in0=x2, in1=cosv, op=ALU.mult), reads=[srck, "cs"], writes=[kb])
            S.op("dve", lambda e: e.tensor_tensor(out=d2, in0=ta, in1=tb, op=ALU.add), reads=[ka, kb], writes=list(dstk))

        def prefix_proj(slot, h, ptile):
            hb = h % 2
            kb = 3 * hb
            for c in range(NCH):
                mm_group(bank(kb)[:, c * 128:(c + 1) * 128],
                         [(xT[:, kt, c * 128:(c + 1) * 128], wr[slot][:, kt, 0:128]) for kt in range(KT)],
                         ["xT.c%d.0" % c, "xT.c%d.1" % c] + wkeys(slot), bkh(kb, c // 2))
                vb = kb + 1 + c // 2
                mm_group(bank(vb)[:, (c % 2) * 256:(c % 2 + 1) * 256],
                         [(xT[:, kt, c * 128:(c + 1) * 128], wr[slot][:, kt, 128:384]) for kt in range(KT)],
                         ["xT.c%d.0" % c, "xT.c%d.1" % c] + wkeys(slot), bkh(vb, c % 2))
            k32 = qk32[hb][:].rearrange("p (c d) -> p c d", c=NCH)
            k32k = "qk32_%d" % hb
            for c in range(NCH):
                jj = ptile * NCH + c
                S.op("act", (lambda c, jj: lambda e: e.activation(out=k32[:, c, :], in_=bank(kb)[:, c * 128:(c + 1) * 128],
                                                                   func=AF.Identity, scale=dkp[:, jj, h:h + 1]))(c, jj),
                     reads=bkh(kb, c // 2) + ["dkp"], writes=[k32k])
            S.op("act", lambda e: e.activation(out=vh[hb][:], in_=PS[:, kb + 1:kb + 3, :].rearrange("p b (c e) -> p (b c) e", c=2),
                                               func=AF.Identity),
                 reads=bk(kb + 1) + bk(kb + 2), writes=["vh%d.%d" % (hb, c) for c in range(NCH)])
            ta = rta[hb][:].rearrange("p (c f) -> p c f", c=NCH)
            tb = rtb[hb][:].rearrange("p (c f) -> p c f", c=NCH)
            rotary(k32[:, :, 0:64], k32[:, :, 64:128], cs[:, :, 0:64], cs[:, :, 64:128],
                   qkt[hb][:, :, 0, 0:64], qkt[hb][:, :, 0, 64:128], ta, tb, k32k, ["qkt%d.0" % hb, "qkt%d.1" % hb], hb)

        def prefix_state(h):
            hb = h % 2
            for c in range(NCH):
                def fn(e, c=c):
                    return e.matmul(bank(6)[:, 0:256], lhsT=qkt[hb][:, c, 0, :], rhs=vh[hb][:, c, :],
                                    start=(c == 0), stop=(c == NCH - 1))
                S.op("pe", fn, reads=["qkt%d.0" % hb, "qkt%d.1" % hb, "vh%d.%d" % (hb, c)], writes=bkh(6, 0))
            S.op("dve", lambda e: e.tensor_tensor(out=R[:, h, :], in0=R[:, h, :], in1=bank(6)[:, 0:256], op=ALU.add),
                 reads=bkh(6, 0) + ["R%d" % h], writes=["R%d" % h])

        def prefix_step(slot, h, ptile):
            if h == 2:
                for c in range(NCH):
                    if ptile + 1 < NT:
                        x_dma(xp_d, "p", ptile + 1, c)
                    else:
                        x_dma(x_d, "m", 0, c)
            prefix_proj(slot, h, ptile)
            if h > 0:
                prefix_state(h - 1)

        def rstd_batch(stat_views, stat_keys, want_nb=False):
            n = len(stat_views)
            m, mk = next_sm()
            r, rk = next_sm()
            for c in range(n):
                S.op("dve", (lambda c: lambda e: e.bn_aggr(out=m[:, 2 * c:2 * c + 2], in_=stat_views[c]))(c),
                     reads=[stat_keys[c]], writes=[mk])
            var = m[:, 0:2 * n].rearrange("p (c t) -> p c t", t=2)[:, :, 1]
            mean = m[:, 0:2 * n].rearrange("p (c t) -> p c t", t=2)[:, :, 0]
            S.op("dve", lambda e: e.tensor_scalar(out=r[:, 0:n], in0=var, scalar1=EPS, scalar2=None, op0=ALU.add),
                 reads=[mk], writes=[rk])
            S.op("act", lambda e: e.activation(out=r[:, 0:n], in_=r[:, 0:n], func=AF.Sqrt), reads=[rk], writes=[rk])
            S.op("dve", lambda e: e.reciprocal(out=r[:, 0:n], in_=r[:, 0:n]), reads=[rk], writes=[rk])
            if want_nb:
                S.op("dve", lambda e: e.scalar_tensor_tensor(out=r[:, 4:4 + n], in0=mean, scalar=-1.0, in1=r[:, 0:n],
                                                             op0=ALU.mult, op1=ALU.mult), reads=[mk, rk], writes=[rk])
            return m, mk, r, rk

        def a1(slot, nb):
            for c in range(NCH):
                pf, pk = next_pf()
                mm_group(pf, [(xT[:, kt, c * 128:(c + 1) * 128], wr[slot][:, kt, :]) for kt in range(KT)],
                         ["xT.c%d.0" % c, "xT.c%d.1" % c] + wkeys(slot), pk)
                tf, tk = next_tmp()
                S.op("act", (lambda pf, tf: lambda e: e.activation(out=tf[:], in_=pf, func=AF.Gelu_apprx_tanh))(pf, tf),
                     reads=pk, writes=[tk])
                S.op("dve", (lambda tf, c: lambda e: e.bn_stats(out=stv[:, c, nb, :], in_=tf[:]))(tf, c),
                     reads=[tk], writes=["stv.%d" % c])
                S.op("dve", (lambda tf, c: lambda e: e.tensor_copy(out=vln[:, c, nb * 512:(nb + 1) * 512], in_=tf[:]))(tf, c),
                     reads=[tk], writes=["vln.%d" % c])

        def a2():
            m, mk, r, rk = rstd_batch([stv[:, c, :, :].rearrange("p a b -> p (a b)") for c in range(NCH)],
                                      ["stv.%d" % c for c in range(NCH)])
            for c in range(NCH):
                S.op("dve", (lambda c: lambda e: e.tensor_scalar(out=vln[:, c, :], in0=vln[:, c, :], scalar1=m[:, 2 * c:2 * c + 1],
                                                                  scalar2=r[:, c:c + 1], op0=ALU.subtract, op1=ALU.mult))(c),
                     reads=[mk, rk, "vln.%d" % c], writes=["vln.%d" % c])

        def a3(slot, j):
            for l in range(2):
                ct = 2 * j + l
                g = ct // 2
                pu, puk = next_pf()
                psv, psvk = next_pf()
                pz, pzk = next_pf()
                mm_group(pu, [(wr[slot][:, kt, l * 128:(l + 1) * 128], xT[:, kt, :]) for kt in range(KT)],
                         XT_K + wkeys(slot), puk)
                mm_each([(psv[:, c * 128:(c + 1) * 128], vln[:, c, ct * 128:(ct + 1) * 128], wsT[:, g, :]) for c in range(NCH)],
                        VLN_K + ["wsT"], psvk)
                mm_group(pz, [(wr[slot][:, kt, 256 + l * 128:256 + (l + 1) * 128], xT[:, kt, :]) for kt in range(KT)],
                         XT_K + wkeys(slot), pzk)
                tu, tuk = next_tmp()
                t2, t2k = next_tmp()
                S.op("act", (lambda pu, tu: lambda e: e.activation(out=tu[:], in_=pu, func=AF.Gelu_apprx_tanh))(pu, tu),
                     reads=puk, writes=[tuk])
                S.op("dve", (lambda psv, t2, ct: lambda e: e.scalar_tensor_tensor(
                    out=t2[:].rearrange("p (c t) -> p c t", c=NCH), in0=psv.rearrange("p (c t) -> p c t", c=NCH),
                    scalar=lnvg[:, ct:ct + 1], in1=bias2[:, ct, :].unsqueeze(1).to_broadcast([128, NCH, 128]),
                    op0=ALU.mult, op1=ALU.add))(psv, t2, ct),
                    reads=psvk + ["lnvg", "bias2"], writes=[t2k])
                S.op("dve", (lambda t2, tu: lambda e: e.tensor_tensor(out=t2[:], in0=t2[:], in1=tu[:], op=ALU.mult))(t2, tu),
                     reads=[t2k, tuk], writes=[t2k])
                S.op("act", (lambda pz, tu: lambda e: e.activation(out=tu[:], in_=pz, func=AF.Silu))(pz, tu),
                     reads=pzk, writes=[tuk])
                S.op("dve", (lambda t2, tu, ct: lambda e: e.tensor_tensor(out=preT[:, ct, :], in0=t2[:], in1=tu[:], op=ALU.mult))(t2, tu, ct),
                     reads=[t2k, tuk], writes=["preT.%d" % ct])

        def proj_fm(slot, j, srcT, src_keys, dstT, dst_prefix):
            for l in range(4):
                dt_ = 4 * j + l
                pf, pk = next_pf()
                mm_group(pf, [(wr[slot][:, kt, l * 128:(l + 1) * 128], srcT[:, kt, :]) for kt in range(KT)],
                         src_keys + wkeys(slot), pk)
                copy_op(ev_eng(), dstT[:, dt_, :], pf, pk, ["%s.%d" % (dst_prefix, dt_)])

        def qbank(h, c):
            return (4 * h + c) % 3

        def P_gemm(h, c, sq, mid=None, split=12):
            qb = qbank(h, c)
            pairs = [(xT[:, kt, c * 128:(c + 1) * 128], wr[sq][:, kt, :]) for kt in range(KT)]
            rk_ = ["xT.c%d.0" % c, "xT.c%d.1" % c] + wkeys(sq)
            if mid is None:
                mm_group(bank(qb), pairs, rk_, bk(qb))
                return

            def part(lo, hi):
                def fn(e):
                    ins = None
                    for i in range(lo, hi):
                        ins = e.matmul(bank(qb), lhsT=pairs[i][0], rhs=pairs[i][1], start=(i == 0), stop=(i == KT - 1))
                    return ins
                S.op("pe", fn, reads=rk_, writes=bk(qb))
            part(0, split)
            mid()
            part(split, KT)

        def P_evac(h, c):
            hb = h % 2
            half = c // 2
            qb = qbank(h, c)
            q32 = qk32[half][:].rearrange("p (c j d) -> p c j d", c=2, j=2)
            q32k = "qk32_%d" % half
            S.op("act", lambda e: e.activation(out=q32[:, c % 2, 0, :], in_=bank(qb)[:, 0:128], func=AF.Identity, scale=dec[:, h:h + 1]),
                 reads=bk(qb) + ["dec"], writes=[q32k])
            S.op("act", lambda e: e.activation(out=q32[:, c % 2, 1, :], in_=bank(qb)[:, 128:256], func=AF.Identity, scale=dec[:, 8 + h:9 + h]),
                 reads=bk(qb) + ["dec"], writes=[q32k])
            S.op("act", lambda e: e.activation(out=vh[hb][:, c, :], in_=bank(qb)[:, 256:512], func=AF.Identity),
                 reads=bk(qb), writes=["vh%d.%d" % (hb, c)])

        def P_post(h, half):
            hb = h % 2
            c0 = 2 * half
            q32 = qk32[half][:].rearrange("p (c j d) -> p c j d", c=2, j=2)
            q32k = "qk32_%d" % half
            ta = rta[half][:].rearrange("p (c j f) -> p c j f", c=2, j=2)
            tb = rtb[half][:].rearrange("p (c j f) -> p c j f", c=2, j=2)
            cosv = cs[:, c0:c0 + 2, 0:64].unsqueeze(2).to_broadcast([128, 2, 2, 64])
            sinv = cs[:, c0:c0 + 2, 64:128].unsqueeze(2).to_broadcast([128, 2, 2, 64])
            rotary(q32[:, :, :, 0:64], q32[:, :, :, 64:128], cosv, sinv,
                   qkt[hb][:, c0:c0 + 2, :, 0:64], qkt[hb][:, c0:c0 + 2, :, 64:128], ta, tb, q32k, ["qkt%d.%d" % (hb, half)], half)

        def T_a(h, half):
            hb = h % 2
            c0 = 2 * half
            pb = bank16(6)
            tr_group([(pb[:, half * 512 + (ci * 2 + j) * 128:half * 512 + (ci * 2 + j + 1) * 128], qkt[hb][:, c0 + ci, j, :])
                      for ci in range(2) for j in range(2)], ["qkt%d.%d" % (hb, half)], bkh(6, half))
            copy_op("act", qkT[hb][:, :, c0 * 128:(c0 + 2) * 128].rearrange("p j (c t) -> p c j t", c=2),
                    pb[:, half * 512:(half + 1) * 512].rearrange("p (c j t) -> p c j t", c=2, j=2),
                    bkh(6, half), ["qkT%d.%d" % (hb, half)])

        def T_b(h, half):
            hb = h % 2
            c0 = 2 * half
            mm_each([(bank(4)[:, c * 128:(c + 1) * 128], qkT[hb][:, 1, c * 128:(c + 1) * 128], qkT[hb][:, 0, c * 128:(c + 1) * 128])
                     for c in (c0, c0 + 1)], ["qkT%d.%d" % (hb, half)], bkh(4, half))
            S.op("dve", lambda e: e.tensor_tensor(out=ST[hb][:, c0:c0 + 2, :],
                                                  in0=bank(4)[:, c0 * 128:(c0 + 2) * 128].rearrange("p (c t) -> p c t", c=2),
                                                  in1=maskT[:].unsqueeze(1).to_broadcast([128, 2, 128]), op=ALU.mult),
                 reads=bkh(4, half) + ["maskT"], writes=["ST%d.%d" % (hb, half)])

        def R_stage(h, c):
            hb = h % 2
            half = c // 2

            def fn_o(e):
                e.matmul(bank(5)[:, 0:256], lhsT=ST[hb][:, c, :], rhs=vh[hb][:, c, :], start=True, stop=False)
                return e.matmul(bank(5)[:, 0:256], lhsT=qkT[hb][:, 0, c * 128:(c + 1) * 128], rhs=Rg[:, h, :], start=False, stop=True)
            S.op("pe", fn_o, reads=["ST%d.%d" % (hb, half), "vh%d.%d" % (hb, c), "qkT%d.%d" % (hb, half), "Rg%d" % h], writes=bk(5))
            mm_group(bank(3)[:, 0:256], [(qkt[hb][:, c, 1, :], vh[hb][:, c, :])],
                     ["qkt%d.%d" % (hb, half), "vh%d.%d" % (hb, c)], bk(3))
            s_c = float(gC1[h] / (gC[h] ** (c + 1)))
            S.op("dve", lambda e: e.scalar_tensor_tensor(out=R[:, h, :], in0=bank(3)[:, 0:256], scalar=s_c, in1=R[:, h, :],
                                                         op0=ALU.mult, op1=ALU.add),
                 reads=bk(3) + ["R%d" % h], writes=["R%d" % h])
            S.op("dve", lambda e: e.tensor_scalar(out=Rg[:, h, :], in0=R[:, h, :], scalar1=float(gam[h] * gC[h] ** (c + 1)),
                                                  scalar2=None, op0=ALU.mult),
                 reads=["R%d" % h], writes=["Rg%d" % h])
            if c == NCH - 1:
                S.op("dve", lambda e: e.tensor_scalar(out=R[:, h, :], in0=R[:, h, :], scalar1=float(gC[h] ** NCH),
                                                      scalar2=None, op0=ALU.mult),
                     reads=["R%d" % h], writes=["R%d" % h])
            S.op("act", lambda e: e.activation(out=on32[hb][:, c, :], in_=bank(5)[:, 0:256], func=AF.Identity),
                 reads=bk(5), writes=["on32_%d.%d" % (hb, c)])

        nst = {}

        def N1(h):
            hb = h % 2
            s4, s4k = st4[cnt["st4"] % 2], "st4_%d" % (cnt["st4"] % 2)
            cnt["st4"] += 1
            for c in range(NCH):
                S.op("dve", (lambda c: lambda e: e.bn_stats(out=s4[:, c, :], in_=on32[hb][:, c, :]))(c),
                     reads=["on32_%d.%d" % (hb, c)], writes=[s4k])
            m, mk = next_sm()
            r, rk = next_sm()
            for c in range(NCH):
                S.op("dve", (lambda c: lambda e: e.bn_aggr(out=m[:, 2 * c:2 * c + 2], in_=s4[:, c, :]))(c), reads=[s4k], writes=[mk])
            var = m[:, 0:8].rearrange("p (c t) -> p c t", t=2)[:, :, 1]
            S.op("dve", lambda e: e.tensor_scalar(out=r[:, 0:4], in0=var, scalar1=EPS, scalar2=None, op0=ALU.add), reads=[mk], writes=[rk])
            nst[h] = (m, mk, r, rk)

        def N2(h):
            m, mk, r, rk = nst[h]
            mean = m[:, 0:8].rearrange("p (c t) -> p c t", t=2)[:, :, 0]
            S.op("act", lambda e: e.activation(out=r[:, 0:4], in_=r[:, 0:4], func=AF.Sqrt), reads=[rk], writes=[rk])
            S.op("dve", lambda e: e.reciprocal(out=r[:, 0:4], in_=r[:, 0:4]), reads=[rk], writes=[rk])
            S.op("dve", lambda e: e.scalar_tensor_tensor(out=r[:, 4:8], in0=mean, scalar=-1.0, in1=r[:, 0:4],
                                                         op0=ALU.mult, op1=ALU.mult), reads=[mk, rk], writes=[rk])

        def N3(h):
            hb = h % 2
            m, mk, r, rk = nst[h]
            for c in range(NCH):
                S.op("act", (lambda c: lambda e: e.activation(out=ybp[hb][:, c, :], in_=on32[hb][:, c, :], func=AF.Identity,
                                                               bias=r[:, 4 + c:5 + c], scale=r[:, c:c + 1]))(c),
                     reads=[rk, "on32_%d.%d" % (hb, c)], writes=["ybp.%d" % c])

        def Y_stage(h):
            hb = h % 2
            pb = bank16(7)
            tr_group([(pb[:, (c * 2 + j) * 128:(c * 2 + j + 1) * 128], ybp[hb][:, c, j * 128:(j + 1) * 128])
                      for c in range(NCH) for j in range(2)], ["ybp.%d" % c for c in range(NCH)], bk(7))
            copy_op("act", preT[:, 2 * h:2 * h + 2, :].rearrange("p j (c t) -> p c j t", c=NCH),
                    pb.rearrange("p (c j t) -> p c j t", c=NCH, j=2), bk(7), ["preT.%d" % (2 * h), "preT.%d" % (2 * h + 1)])

        def bstep(slots, h, tile=None):
            sq = slots[0] if h < NH else None
            hr = h - 1
            if h == 1 and tile is not None and tile + 1 < NT:
                for c in range(NCH):
                    x_dma(x_d, "m", tile + 1, c)
            rv = 0 <= hr < NH
            for c in range(NCH):
                if h < NH:
                    if c == 1 and rv:
                        P_gemm(h, c, sq, mid=lambda: T_b(hr, 1), split=8)
                    else:
                        P_gemm(h, c, sq)
                        if c == 3:
                            T_b(h, 0)
                    P_evac(h, c)
                    if c % 2 == 1:
                        P_post(h, c // 2)
                else:
                    if c == 1 and rv:
                        T_b(hr, 1)
                if rv:
                    R_stage(hr, c)
                    if c == 0:
                        T_a(hr, 1)
                if c == 2 and h < NH:
                    T_a(h, 0)
                if h - 2 >= 0:
                    if c == 0:
                        N1(h - 2)
                    elif c == 1:
                        N2(h - 2)
                    elif c == 2:
                        N3(h - 2)
            if 0 <= h - 2 < NH:
                Y_stage(h - 2)

        def zstep(slot, j, extra=None, banks=None):
            for l in range(4):
                ct = 4 * j + l
                if banks is None:
                    pf, pk = next_pf()
                else:
                    bi_ = banks[l % len(banks)]
                    pf, pk = bank(bi_), bk(bi_)
                mm_group(pf, [(wr[slot][:, kt, l * 128:(l + 1) * 128], xT[:, kt, :]) for kt in range(KT)],
                         XT_K + wkeys(slot), pk)
                tz, tzk = next_tmp()
                S.op("act", (lambda pf, tz: lambda e: e.activation(out=tz[:], in_=pf, func=AF.Silu))(pf, tz), reads=pk, writes=[tzk])
                S.op("dve", (lambda tz, ct: lambda e: e.tensor_tensor(out=preT[:, ct, :], in0=preT[:, ct, :], in1=tz[:], op=ALU.mult))(tz, ct),
                     reads=[tzk, "preT.%d" % ct], writes=["preT.%d" % ct])
                if extra is not None:
                    extra(l)

        def drain_a(l):
            if l == 1:
                T_b(NH - 1, 1)
            R_stage(NH - 1, l)
            if l == 0:
                T_a(NH - 1, 1)
            if l == 0:
                N1(NH - 2)
            elif l == 1:
                N2(NH - 2)
            elif l == 2:
                N3(NH - 2)
            elif l == 3:
                Y_stage(NH - 2)

        def drain_b(l):
            if l == 0:
                N1(NH - 1)
            elif l == 1:
                N2(NH - 1)
            elif l == 2:
                N3(NH - 1)
            elif l == 3:
                Y_stage(NH - 1)

        def drain_c(l):
            pass

        def gstep(slot, j):
            for l in range(2):
                dt_ = 2 * j + l
                pa, pak = next_pf()
                pb_, pbk = next_pf()
                mm_group(pa, [(wr[slot][:, kt, l * 128:(l + 1) * 128], xT[:, kt, :]) for kt in range(KT)],
                         XT_K + wkeys(slot), pak)
                mm_group(pb_, [(wr[slot][:, kt, 256 + l * 128:256 + (l + 1) * 128], xT[:, kt, :]) for kt in range(KT)],
                         XT_K + wkeys(slot), pbk)
                ta, tak = next_tmp()
                tb, tbk = next_tmp()
                S.op("act", (lambda pa, ta, dt_: lambda e: e.activation(out=ta[:], in_=pa, func=AF.Sigmoid,
                                                                         bias=bgate[:, dt_:dt_ + 1], scale=1.0))(pa, ta, dt_),
                     reads=pak + ["bgate"], writes=[tak])
                S.op("act", (lambda pb_, tb, dt_: lambda e: e.activation(out=tb[:], in_=pb_, func=AF.Sigmoid,
                                                                          bias=bgate[:, 16 + dt_:17 + dt_], scale=1.0))(pb_, tb, dt_),
                     reads=pbk + ["bgate"], writes=[tbk])
                S.op("dve", (lambda ta, dt_: lambda e: e.tensor_tensor(out=ta[:], in0=ta[:], in1=yaT[:, dt_, :], op=ALU.mult))(ta, dt_),
                     reads=[tak, "yaT.%d" % dt_], writes=[tak])
                S.op("dve", (lambda tb, dt_: lambda e: e.tensor_tensor(out=tb[:], in0=tb[:], in1=ybT[:, dt_, :], op=ALU.mult))(tb, dt_),
                     reads=[tbk, "ybT.%d" % dt_], writes=[tbk])
                S.op("dve", (lambda ta, tb, dt_: lambda e: e.tensor_tensor(out=yaT[:, dt_, :], in0=ta[:], in1=tb[:], op=ALU.add))(ta, tb, dt_),
                     reads=[tak, tbk], writes=["yaT.%d" % dt_])

        def final_ln_chunk(tile, c):
            m, mk, r, rk = rstd_batch([stv[:, c, :, :].rearrange("p a b -> p (a b)")], ["stv.%d" % c], want_nb=True)
            S.op("act", lambda e: e.activation(out=res[:, c, :], in_=res[:, c, :], func=AF.Identity, bias=r[:, 4:5], scale=r[:, 0:1]),
                 reads=[rk, "res.%d" % c], writes=["res.%d" % c])
            S.op("dve", lambda e: e.tensor_tensor(out=res[:, c, :], in0=res[:, c, :], in1=lncg[:], op=ALU.mult),
                 reads=["res.%d" % c, "lncg"], writes=["res.%d" % c])
            S.op("dve", lambda e: e.tensor_tensor(out=res[:, c, :], in0=res[:, c, :], in1=lncb[:], op=ALU.add),
                 reads=["res.%d" % c, "lncb"], writes=["res.%d" % c])
            r0 = (tile * NCH + c) * 128
            S.op("sp", lambda e: e.dma_start(out=out_d[r0:r0 + 128, :], in_=res[:, c, :]),
                 reads=["res.%d" % c], writes=["out.%d.%d" % (tile, c)], dma_key="out")

        def ostep(slot, nb, tile):
            for c in range(NCH):
                pf, pk = next_pf()
                mm_group(pf, [(yaT[:, kt, c * 128:(c + 1) * 128], wr[slot][:, kt, :]) for kt in range(KT)],
                         YAT_K + wkeys(slot), pk)
                S.op("dve", (lambda pf, c: lambda e: e.scalar_tensor_tensor(
                    out=res[:, c, nb * 512:(nb + 1) * 512], in0=res[:, c, nb * 512:(nb + 1) * 512], scalar=ALPHA, in1=pf,
                    op0=ALU.mult, op1=ALU.add))(pf, c),
                    reads=pk + ["res.%d" % c], writes=["res.%d" % c])
                S.op("dve", (lambda c: lambda e: e.bn_stats(out=stv[:, c, nb, :], in_=res[:, c, nb * 512:(nb + 1) * 512]))(c),
                     reads=["res.%d" % c], writes=["stv.%d" % c])
                if nb == 3:
                    final_ln_chunk(tile, c)

        steps = []
        wmeta = []
        wctx = {"kind": "p", "tile": 0, "nid": 0}

        def W(pieces_list, fn):
            steps.append((pieces_list, fn))
            for _ in pieces_list:
                li = wctx["nid"]
                if wctx["kind"] == "p":
                    ctile = li % 2
                else:
                    ctile = 0 if ((li - NH) % 5) < 2 else 1
                t = wctx["tile"]
                mode = "load" if t < ctile else ("load_store" if t == ctile else "cached")
                wmeta.append((wctx["kind"], li, mode))
                wctx["nid"] += 1

        def Nw(fn):
            steps.append(([], fn))

        for ptile in range(NT):
            wctx.update(kind="p", tile=ptile, nid=0)
            Nw((lambda ptile: lambda s: (build_xT(xp_d, "p", ptile), load_cs(cs_pre_d, ptile)))(ptile))
            for h in range(NH):
                W([[(0, w_in[:, C_K + h * 128:C_K + (h + 1) * 128]), (128, w_in[:, C_RV + h * 256:C_RV + (h + 1) * 256])]],
                  (lambda h, ptile: lambda s: prefix_step(s[0], h, ptile))(h, ptile))
            Nw(lambda s: prefix_state(NH - 1))

        def after_prefix(s):
            for h in range(NH):
                S.op("act", (lambda h: lambda e: e.activation(out=Rg[:, h, :], in_=R[:, h, :], func=AF.Identity, scale=float(gam[h])))(h),
                     reads=["R%d" % h], writes=["Rg%d" % h])
        Nw(after_prefix)

        for tile in range(NT):
            wctx.update(kind="m", tile=tile, nid=NH)
            Nw((lambda tile: lambda s: (build_xT(x_d, "m", tile), load_cs(cs_main_d, tile)))(tile))
            for nb in range(4):
                W([[(0, w_in[:, C_V + nb * 512:C_V + (nb + 1) * 512])]], (lambda nb: lambda s: a1(s[0], nb))(nb))
            Nw(lambda s: a2())
            for j in range(8):
                W([[(0, w_in[:, C_U + j * 256:C_U + (j + 1) * 256]), (256, w_in[:, C_Z + j * 256:C_Z + (j + 1) * 256])]],
                  (lambda j: lambda s: a3(s[0], j))(j))
            Nw(lambda s: alias(VLN_K, YBT_K))
            for j in range(4):
                W([[(0, w_oa[:, j * 512:(j + 1) * 512])]], (lambda j: lambda s: proj_fm(s[0], j, preT, PRE_K, yaT, "yaT"))(j))
            for h in range(NH):
                W([[(0, w_in[:, C_Q + h * 128:C_Q + (h + 1) * 128]), (128, w_in[:, C_K + h * 128:C_K + (h + 1) * 128]),
                    (256, w_in[:, C_RV + h * 256:C_RV + (h + 1) * 256])]],
                  (lambda h, tile: lambda s: bstep(s, h, tile))(h, tile))
            zx = [(drain_a, [0, 1, 2]), (drain_b, [0, 1, 2]), (drain_c, [0, 1, 2]), (None, None)]
            for j in range(4):
                W([[(0, w_in[:, C_RZ + j * 512:C_RZ + (j + 1) * 512])]],
                  (lambda j: lambda s: zstep(s[0], j, extra=zx[j][0], banks=zx[j][1]))(j))
            for j in range(4):
                W([[(0, w_ob[:, j * 512:(j + 1) * 512])]], (lambda j: lambda s: proj_fm(s[0], j, preT, PRE_K, ybT, "ybT"))(j))
            for j in range(8):
                W([[(0, w_in[:, C_GA + j * 256:C_GA + (j + 1) * 256]), (256, w_in[:, C_GB + j * 256:C_GB + (j + 1) * 256])]],
                  (lambda j: lambda s: gstep(s[0], j))(j))

            def pre_o(s, tile=tile):
                alias(YBT_K + PRE_K, RES_K)
                for c in range(NCH):
                    r0 = (tile * NCH + c) * 128
                    S.op("sp", (lambda c, r0: lambda e: e.dma_start(out=res[:, c, :], in_=x_d[r0:r0 + 128, :]))(c, r0),
                         writes=["res.%d" % c], dma_key="res.%d" % c)
                if tile + 1 < NT:
                    for c in range(NCH):
                        x_dma(x_d, "m", tile + 1, c)
            Nw(pre_o)
            for nb in range(4):
                W([[(0, w_out[:, nb * 512:(nb + 1) * 512])]], (lambda nb, tile: lambda s: ostep(s[0], nb, tile))(nb, tile))
            if tile + 1 < NT:
                Nw(lambda s: alias(RES_K, VLN_K + PRE_K))

        wblocks = []
        for (pl, fn) in steps:
            for pieces in pl:
                wblocks.append(pieces)

        NCACHE = NH + 44
        wsc = nc.dram_tensor("wsc", [NCACHE, 128, KT * 512], BF16).ap()

        def issue_load(bi):
            slot = bi % NSLOT
            kind, cid, mode = wmeta[bi]
            assert cid < NCACHE
            flat = wr[slot][:].rearrange("p k n -> p (k n)")
            if mode == "cached":
                S.op("pool", lambda e: e.dma_start(out=flat, in_=wsc[cid]), reads=["wsc.%d" % cid], writes=wkeys(slot),
                     dma_key="wr%d" % slot, dma_group=bi)
                return
            pi = 0
            for (off, src) in wblocks[bi]:
                n = src.shape[1]
                srcv = src.rearrange("(kt p) n -> p kt n", p=128)
                nparts = 4 if n >= 256 else 2
                kper = KT // nparts
                for q in range(nparts):
                    S.op("pool", (lambda slot, off, n, srcv, q, kper: lambda e: e.dma_start(
                        out=wr[slot][:, q * kper:(q + 1) * kper, off:off + n], in_=srcv[:, q * kper:(q + 1) * kper, :]))(slot, off, n, srcv, q, kper),
                        writes=["wr%d.%d" % (slot, pi)], dma_key="wr%d" % slot, dma_group=bi)
                    pi += 1
            assert pi <= MAXP
            if mode != "load_store":
                return
            S.op("sp", lambda e: e.dma_start(out=wsc[cid], in_=flat), reads=wkeys(slot), writes=["wsc.%d" % cid],
                 dma_key="wsc_st%d" % slot)

        issued = 0
        bi = 0
        for (pl, fn) in steps:
            limit = min(len(wblocks), bi + NSLOT)
            while issued < limit:
                issue_load(issued)
                issued += 1
            slots = [(bi + k) % NSLOT for k in range(len(pl))]
            fn(slots)
            bi += len(pl)

        S.op("sp", None, reads=["out.%d.%d" % (t_, c_) for t_ in range(NT) for c_ in range(NCH)])
        build_program.sbuf_left = nc.sbuf_bytes_remaining
        S.emit()
        build_program.stats = S.stats
    return nc


def _tables(hf):
    lg = np.log1p(-np.exp2(-5.0 - np.arange(NH, dtype=np.float64)))
    freqs = (10000.0 ** (-np.arange(0, 128, 2, dtype=np.float32) / np.float32(128))).astype(np.float32)
    t = np.arange(128, dtype=np.float64)

    def cs_table(base):
        pos = (base + np.arange(TOK)).astype(np.float32)
        ang = (pos[:, None] * freqs[None, :]).astype(np.float32)
        c = np.cos(ang.astype(np.float64)).astype(np.float32)
        s = np.sin(ang.astype(np.float64)).astype(np.float32)
        tab = np.concatenate([c, s], axis=1).reshape(16, 128, 128).transpose(1, 0, 2)
        return np.ascontiguousarray(tab, dtype=np.float32)

    cs_main = cs_table(hf * TOK)
    cs_pre = cs_table(0)
    dec = np.zeros((128, 16), np.float32)
    dec[:, 0:8] = np.exp(lg[None, :] * t[:, None])
    dec[:, 8:16] = np.exp(-lg[None, :] * t[:, None]) * (128.0 ** -0.5)
    s_glob = np.arange(TOK, dtype=np.float64)
    dkp = (np.exp(lg[None, :] * (TOK - 1 - s_glob)[:, None]) * (128.0 ** -0.5)).astype(np.float32)
    dkp = np.ascontiguousarray(dkp.reshape(16, 128, 8).transpose(1, 0, 2))
    maskT = (np.arange(128)[None, :] >= np.arange(128)[:, None]).astype(np.float32)
    mask_ts = np.ascontiguousarray(maskT.T)
    return dict(cs_main=cs_main, cs_pre=cs_pre, dec=dec, dkp=dkp, maskT=maskT, mask_ts=mask_ts)


_NC_CACHE = {}


def kernel(x, w_in, b_gate, ln_v_g, ln_v_b, w_s, b_s, w_oa, w_ob, w_out, ln_g, ln_b):
    x = np.asarray(x, dtype=np.float32)
    B, SEQ, _ = x.shape
    if "nc" not in _NC_CACHE:
        _NC_CACHE["nc"] = build_program()
    nc = _NC_CACHE["nc"]
    shared = dict(
        w_in=np.ascontiguousarray(np.asarray(w_in, np.float32)[0]),
        w_oa=np.ascontiguousarray(np.asarray(w_oa, np.float32)[0]),
        w_ob=np.ascontiguousarray(np.asarray(w_ob, np.float32)[0]),
        w_out=np.ascontiguousarray(np.asarray(w_out, np.float32)[0]),
        b_gate=np.ascontiguousarray(np.asarray(b_gate, np.float32)[0]),
        ln_v_g=np.ascontiguousarray(np.asarray(ln_v_g, np.float32)[0]),
        ln_v_b=np.ascontiguousarray(np.asarray(ln_v_b, np.float32)[0]),
        w_s=np.ascontiguousarray(np.asarray(w_s, np.float32)[0]),
        b_s=np.ascontiguousarray(np.asarray(b_s, np.float32)[0]),
        ln_g=np.ascontiguousarray(np.asarray(ln_g, np.float32)[0]),
        ln_b=np.ascontiguousarray(np.asarray(ln_b, np.float32)[0]),
    )
    tabs = [_tables(0), _tables(1)]
    zeros = np.zeros((TOK, D), np.float32)
    in_maps = []
    for c in range(NCORES):
        b, hf = c // 2, c % 2
        m = dict(shared)
        m["x"] = np.ascontiguousarray(x[b, hf * TOK:(hf + 1) * TOK, :])
        m["xp"] = np.ascontiguousarray(x[b, 0:TOK, :]) if hf == 1 else zeros
        m.update(tabs[hf])
        in_maps.append(m)
    res = run_bass_kernel_spmd(nc, in_maps, core_ids=list(range(NCORES)))
    out = np.empty((B, SEQ, D), np.float32)
    for c in range(NCORES):
        b, hf = c // 2, c % 2
        out[b, hf * TOK:(hf + 1) * TOK, :] = res.results[c]["out"]
    return out
```

```python
import math
from contextlib import ExitStack

import numpy as np
import concourse.bass as bass
import concourse.mybir as mybir
from concourse.bass_utils import run_bass_kernel_spmd

F32 = mybir.dt.float32
BF16 = mybir.dt.bfloat16
AF = mybir.ActivationFunctionType
ALU = mybir.AluOpType

D = 2048
NCORES = 8
TOK = 2048
NCH = 4
T = 512
NT = TOK // T
KT = 16
NH = 8
ALPHA = float(2.0 ** 0.25)
EPS = 1e-5
NSLOT = 3
MAXP = 12
C_U, C_V, C_Z, C_Q, C_K, C_RV, C_RZ, C_GA, C_GB = 0, 2048, 4096, 6144, 7168, 8192, 10240, 12288, 14336

ENGS = ("pe", "act", "dve", "pool", "sp")


class _Op:
    __slots__ = ("idx", "eng", "fn", "deps", "dma_key", "signal", "count", "sem", "grp", "gcount")

    def __init__(self, idx, eng, fn, deps, dma_key, grp=None):
        self.idx = idx
        self.eng = eng
        self.fn = fn
        self.deps = deps
        self.dma_key = dma_key
        self.signal = False
        self.count = 0
        self.sem = None
        self.grp = grp if grp is not None else ("_", idx)
        self.gcount = 0


class Sched:
    def __init__(self, nc):
        self.nc = nc
        self.ops = []
        self.lastw = {}
        self.readers = {}

    def op(self, eng, fn, reads=(), writes=(), dma_key=None, dma_group=None):
        deps = set()
        for k in reads:
            w = self.lastw.get(k)
            if w is not None:
                deps.add(w)
            if k.startswith("PF"):
                for r in self.readers.get(k, ()):
                    if self.ops[r].eng != eng:
                        deps.add(r)
        for k in writes:
            w = self.lastw.get(k)
            if w is not None:
                deps.add(w)
            for r in self.readers.get(k, ()):
                deps.add(r)
        idx = len(self.ops)
        o = _Op(idx, eng, fn, deps, dma_key, dma_group)
        self.ops.append(o)
        for k in reads:
            self.readers.setdefault(k, []).append(idx)
        for k in writes:
            self.lastw[k] = idx
            self.readers[k] = []
        return idx

    def emit(self):
        nc = self.nc
        ops = self.ops
        for o in ops:
            if o.eng == "pe" and o.dma_key is None:
                o.deps = {d for d in o.deps
                          if not (ops[d].eng == "pe" and ops[d].dma_key is None)}
        for o in ops:
            for d in o.deps:
                ops[d].signal = True
        with ExitStack() as es:
            esem = {e: es.enter_context(nc.semaphore("s_" + e)) for e in ENGS}
            dsem = {}
            ecount = {e: 0 for e in ENGS}
            dcount = {}
            for o in ops:
                if o.dma_key is not None:
                    if o.dma_key not in dsem:
                        dsem[o.dma_key] = es.enter_context(nc.semaphore("d_%d" % len(dsem)))
                        dcount[o.dma_key] = 0
                    dcount[o.dma_key] += 16
                    o.sem = dsem[o.dma_key]
                    o.count = dcount[o.dma_key]
                    o.signal = True
                elif o.signal:
                    ecount[o.eng] += 1
                    o.sem = esem[o.eng]
                    o.count = ecount[o.eng]
            self.stats = dict(ecount)
            self.stats["n_dma_sems"] = len(dsem)
            self.stats["n_ops"] = len(ops)
            streams = {e: [o for o in ops if o.eng == e] for e in ENGS}
            gend = {}
            for o in ops:
                if o.dma_key is not None:
                    k = (o.dma_key, o.grp)
                    gend[k] = max(gend.get(k, 0), o.count)
            for o in ops:
                if o.dma_key is not None:
                    o.gcount = gend[(o.dma_key, o.grp)]

            def dma_count(p, cons_idx):
                return p.gcount

            def run(eng_name, e):
                waited = {}
                for o in streams[eng_name]:
                    need = {}
                    for d in o.deps:
                        p = ops[d]
                        sid = id(p.sem)
                        pc = p.count if p.dma_key is None else dma_count(p, o.idx)
                        if waited.get(sid, 0) < pc:
                            if sid not in need or need[sid][1] < pc:
                                need[sid] = (p.sem, pc)
                    for sid, (sem, cnt) in need.items():
                        e.wait_ge(sem, cnt)
                        waited[sid] = cnt
                    if o.fn is None:
                        continue
                    ins = o.fn(e)
                    if o.signal:
                        ins.then_inc(o.sem, 16 if o.dma_key is not None else 1)

            with nc.Block() as block:
                @block.tensor
                def _(e):
                    run("pe", e)

                @block.scalar
                def _(e):
                    run("act", e)

                @block.vector
                def _(e):
                    run("dve", e)

                @block.gpsimd
                def _(e):
                    run("pool", e)

                @block.sync
                def _(e):
                    run("sp", e)


def build_program(taps=None):
    nc = bass.Bass("TRN2", target_bir_lowering=False)

    def din(name, shape):
        return nc.dram_tensor(name, list(shape), F32, kind="ExternalInput").ap()

    x_d = din("x", [TOK, D])
    xp_d = din("xp", [TOK, D])
    w_in = din("w_in", [D, 16384])
    w_oa = din("w_oa", [D, D])
    w_ob = din("w_ob", [D, D])
    w_out = din("w_out", [D, D])
    b_gate = din("b_gate", [4096])
    ln_v_g = din("ln_v_g", [D])
    ln_v_b = din("ln_v_b", [D])
    w_s = din("w_s", [8, 128, 128])
    b_s = din("b_s", [8, 128])
    ln_g = din("ln_g", [D])
    ln_b = din("ln_b", [D])
    cs_main_d = din("cs_main", [128, 16, 128])
    cs_pre_d = din("cs_pre", [128, 16, 128])
    dec_d = din("dec", [128, 16])
    dkp_d = din("dkp", [128, 16, 8])
    maskT_d = din("maskT", [128, 128])
    maskts_d = din("mask_ts", [128, 128])
    out_d = nc.dram_tensor("out", [TOK, D], F32, kind="ExternalOutput").ap()

    gam = [1.0 - 2.0 ** (-5.0 - h) for h in range(NH)]
    gC = [g ** 128 for g in gam]
    gC1 = [g ** 127 for g in gam]

    with ExitStack() as es:
        es.enter_context(nc.allow_non_contiguous_dma(reason="small param column loads"))

        def sb(name, shape, dt):
            return es.enter_context(nc.sbuf_tensor("sb_" + name, list(shape), dt))

        xT = sb("xT", [128, KT, T], BF16)
        yaT = sb("yaT", [128, KT, T], BF16)
        X = sb("X", [128, 16384], BF16)
        vln = X[:, 0:8192].rearrange("p (c n) -> p c n", c=NCH)
        ybT = X[:, 0:8192].rearrange("p (k n) -> p k n", k=KT)
        preT = X[:, 8192:16384].rearrange("p (k n) -> p k n", k=KT)
        X32 = X[:].bitcast(F32)
        res = X32.rearrange("p (c n) -> p c n", c=NCH)
        wr = [sb("wr%d" % i, [128, KT, 512], BF16) for i in range(NSLOT)]
        lncg = sb("lncg", [128, D], F32)
        lncb = sb("lncb", [128, D], F32)
        xb = [sb("xb%d" % i, [128, D], BF16) for i in range(4)]
        R = sb("R", [128, NH, 256], F32)
        Rg = sb("Rg", [128, NH, 256], BF16)
        wsT = sb("wsT", [128, 8, 128], BF16)
        bias2 = sb("bias2", [128, 16, 128], F32)
        cs = sb("cs", [128, NCH, 128], F32)
        dec = sb("dec", [128, 16], F32)
        dkp = sb("dkp", [128, 16, 8], F32)
        maskT = sb("maskT", [128, 128], F32)
        ident = sb("ident", [128, 128], BF16)
        ones = sb("ones", [128, 128], BF16)
        lnvg = sb("lnvg", [128, 16], F32)
        lnvb = sb("lnvb", [128, 16], F32)
        bgate = sb("bgate", [128, 32], F32)
        scratch = sb("scratch", [128, 2], F32)
        tmpF = [sb("tmpF%d" % i, [128, 512], F32) for i in range(4)]
        qk32 = [sb("qk32_%d" % i, [128, 512], F32) for i in range(2)]
        rta = [sb("rta%d" % i, [128, 256], F32) for i in range(2)]
        rtb = [sb("rtb%d" % i, [128, 256], F32) for i in range(2)]
        qkt = [sb("qkt%d" % i, [128, NCH, 2, 128], BF16) for i in range(2)]
        qkT = [sb("qkT%d" % i, [128, 2, T], BF16) for i in range(2)]
        vh = [sb("vh%d" % i, [128, NCH, 256], BF16) for i in range(2)]
        ST = [sb("ST%d" % i, [128, NCH, 128], BF16) for i in range(2)]
        on32 = [sb("on32_%d" % i, [128, NCH, 256], BF16) for i in range(2)]
        ybp = [sb("ybp0", [128, NCH, 256], BF16)] * 2
        stv = sb("stv", [128, NCH, 4, 6], F32)
        st4 = [sb("st4_%d" % i, [128, 4, 6], F32) for i in range(2)]
        sm = [sb("sm%d" % i, [128, 8], F32) for i in range(8)]

        PS = es.enter_context(nc.psum_tensor("ps_all", [128, 8, 512], F32))

        def bank(i):
            return PS[:, i, :]

        def bank16(i):
            return PS[:, i, :].bitcast(BF16)

        def bk(i):
            return ["PF%d.a" % i, "PF%d.b" % i]

        def bkh(i, half):
            return ["PF%d.a" % i, "PF%d.b" % i]

        S = Sched(nc)
        cnt = {"tmp": 0, "sm": 0, "st4": 0, "pf": 0, "ev": 0}

        def next_tmp():
            i = cnt["tmp"] % 4
            cnt["tmp"] += 1
            return tmpF[i], "tmpF%d" % i

        def next_sm():
            i = cnt["sm"] % 8
            cnt["sm"] += 1
            return sm[i], "sm%d" % i

        def next_pf():
            i = cnt["pf"] % 6
            cnt["pf"] += 1
            return bank(i), bk(i)

        def ev_eng():
            cnt["ev"] += 1
            return "act" if cnt["ev"] % 2 else "dve"

        def copy_op(eng, out, in_, reads, writes):
            if eng == "act":
                S.op("act", lambda e: e.activation(out=out, in_=in_, func=AF.Identity), reads=reads, writes=writes)
            else:
                S.op("dve", lambda e: e.tensor_copy(out=out, in_=in_), reads=reads, writes=writes)

        def alias(old, new):
            S.op("dve", lambda e: e.memset(scratch[:, 0:1], 0.0), writes=list(old) + list(new) + ["scratch"])

        def wkeys(slot):
            return ["wr%d.%d" % (slot, j) for j in range(MAXP)]

        def mm_group(out, pairs, reads, writes):
            def fn(e):
                n = len(pairs)
                ins = None
                for i, (l, r) in enumerate(pairs):
                    ins = e.matmul(out, lhsT=l, rhs=r, start=(i == 0), stop=(i == n - 1))
                return ins
            S.op("pe", fn, reads=reads, writes=writes)

        def mm_each(items, reads, writes):
            def fn(e):
                ins = None
                for (o, l, r) in items:
                    ins = e.matmul(o, lhsT=l, rhs=r, start=True, stop=True)
                return ins
            S.op("pe", fn, reads=reads, writes=writes)

        def tr_group(items, reads, writes):
            def fn(e):
                ins = None
                for (o, i) in items:
                    ins = e.transpose(out=o, in_=i, identity=ident[:])
                return ins
            S.op("pe", fn, reads=reads + ["ident"], writes=writes)

        bsb = X32[:, 0:1024].rearrange("p (g t) -> p g t", g=8)
        Wb = X32[:, 1024:2048].rearrange("p (g t) -> p g t", g=8)
        ws32 = X32[:, 2048:3072].rearrange("p (g s) -> p g s", g=8)
        mts = X32[:, 3072:3200]
        wsm = X[:, 8192:9216].rearrange("p (g s) -> p g s", g=8)

        S.op("dve", lambda e: e.memset(tmpF[0][:, 0:128], 0.0), writes=["tmpF0"])
        S.op("pool", lambda e: e.affine_select(out=tmpF[0][:, 0:128], in_=tmpF[0][:, 0:128], pattern=[[-1, 128]],
                                                 compare_op=ALU.not_equal, fill=1.0, base=0, channel_multiplier=1),
             reads=["tmpF0"], writes=["tmpF0"])
        S.op("dve", lambda e: e.tensor_copy(out=ident[:], in_=tmpF[0][:, 0:128]), reads=["tmpF0"], writes=["ident"])
        S.op("dve", lambda e: e.memset(ones[:], 1.0), writes=["ones"])
        S.op("dve", lambda e: e.memset(R[:], 0.0), writes=["R%d" % h for h in range(NH)])

        def ld(dst, src, key):
            S.op("sp", lambda e: e.dma_start(out=dst, in_=src), writes=[key], dma_key=key)

        ld(dec[:], dec_d, "dec")
        ld(dkp[:], dkp_d, "dkp")
        ld(maskT[:], maskT_d, "maskT")
        ld(lnvg[:], ln_v_g.rearrange("(c p) -> p c", p=128), "lnvg")
        ld(lnvb[:], ln_v_b.rearrange("(c p) -> p c", p=128), "lnvb")
        ld(bgate[:], b_gate.rearrange("(c p) -> p c", p=128), "bgate")
        ld(lncg[:], ln_g.partition_broadcast(128), "lncg")
        ld(lncb[:], ln_b.partition_broadcast(128), "lncb")
        S.op("sp", lambda e: e.dma_start(out=X32[:, 0:1024], in_=b_s.rearrange("g t -> (g t)").partition_broadcast(128)),
             writes=["X.bsb"], dma_key="X.bsb")
        S.op("sp", lambda e: e.dma_start(out=ws32, in_=w_s.rearrange("g t s -> t g s")), writes=["X.ws32"], dma_key="X.ws32")
        S.op("sp", lambda e: e.dma_start(out=mts, in_=maskts_d), writes=["X.mts"], dma_key="X.mts")
        S.op("dve", lambda e: e.tensor_tensor(out=wsm, in0=ws32, in1=mts.unsqueeze(1).to_broadcast([128, 8, 128]), op=ALU.mult),
             reads=["X.ws32", "X.mts"], writes=["X.wsm"])
        tr_group([(bank16(6)[:, g * 128:(g + 1) * 128], wsm[:, g, :]) for g in range(8)], ["X.wsm"], bk(6))
        S.op("dve", lambda e: e.tensor_copy(out=wsT[:].rearrange("p g t -> p (g t)"), in_=bank16(6)), reads=bk(6), writes=["wsT"])
        for hlf in range(2):
            mm_group(bank(hlf), [(ones[:], wsT[:, hlf * 4:(hlf + 1) * 4, :].rearrange("p g t -> p (g t)"))],
                     ["ones", "wsT"], bk(hlf))
            S.op("dve", (lambda hlf: lambda e: e.tensor_copy(out=X32[:, 1024 + hlf * 512:1024 + (hlf + 1) * 512], in_=bank(hlf)))(hlf),
                 reads=bk(hlf), writes=["X.Wb%d" % hlf])
        for ct in range(16):
            g = ct // 2
            S.op("dve", (lambda ct, g: lambda e: e.scalar_tensor_tensor(
                out=bias2[:, ct, :], in0=Wb[:, g, :], scalar=lnvb[:, ct:ct + 1], in1=bsb[:, g, :],
                op0=ALU.mult, op1=ALU.add))(ct, g),
                reads=["X.Wb%d" % (g // 4), "lnvb", "X.bsb"], writes=["bias2"])
        VLN_K = ["vln.%d" % c for c in range(NCH)]
        PRE_K = ["preT.%d" % k for k in range(KT)]
        YBT_K = ["ybT.%d" % k for k in range(KT)]
        RES_K = ["res.%d" % c for c in range(NCH)]
        XT_K = ["xT.c%d.%d" % (c, hh) for c in range(NCH) for hh in range(2)]
        YAT_K = ["yaT.%d" % k for k in range(KT)]
        alias(["X.bsb", "X.ws32", "X.mts", "X.wsm", "X.Wb0", "X.Wb1"], VLN_K + PRE_K)

        xdone = set()

        def x_dma(src, tag, tile, c):
            if (tag, tile, c) in xdone:
                return
            xdone.add((tag, tile, c))
            b = c
            r0 = (tile * NCH + c) * 128
            S.op("pool", lambda e: e.dma_start(out=xb[b][:], in_=src[r0:r0 + 128, :]), writes=["xb%d" % b], dma_key="xb%d" % b)

        def build_xT(src, tag, tile):
            for c in range(NCH):
                x_dma(src, tag, tile, c)
                b = c
                for hlf in range(2):
                    bi_ = (2 * c + hlf) % 8
                    pb = bank16(bi_)
                    tr_group([(pb[:, j * 128:(j + 1) * 128], xb[b][:, (hlf * 8 + j) * 128:(hlf * 8 + j + 1) * 128])
                              for j in range(8)], ["xb%d" % b], bk(bi_))
                    copy_op("act" if tag == "m" else ev_eng(), xT[:, hlf * 8:(hlf + 1) * 8, c * 128:(c + 1) * 128],
                            pb.rearrange("p (k t) -> p k t", k=8), bk(bi_), ["xT.c%d.%d" % (c, hlf)])

        def load_cs(src_d, tile):
            S.op("sp", lambda e: e.dma_start(out=cs[:], in_=src_d[:, tile * NCH:(tile + 1) * NCH, :]), writes=["cs"], dma_key="cs")

        def rotary(x1, x2, cosv, sinv, d1, d2, ta, tb, srck, dstk, par):
            ka, kb = "rta%d" % par, "rtb%d" % par
            S.op("dve", lambda e: e.tensor_tensor(out=ta, in0=x1, in1=cosv, op=ALU.mult), reads=[srck, "cs"], writes=[ka])
            S.op("dve", lambda e: e.tensor_tensor(out=tb, in0=x2, in1=sinv, op=ALU.mult), reads=[srck, "cs"], writes=[kb])
            S.op("dve", lambda e: e.tensor_tensor(out=d1, in0=ta, in1=tb, op=ALU.subtract), reads=[ka, kb], writes=list(dstk))
            S.op("dve", lambda e: e.tensor_tensor(out=ta, in0=x1, in1=sinv, op=ALU.mult), reads=[srck, "cs"], writes=[ka])
            S.op("dve", lambda e: e.tensor_tensor(out=tb, in0=x2, in1=cosv, op=ALU.mult), reads=[srck, "cs"], writes=[kb])
            S.op("dve", lambda e: e.tensor_tensor(out=d2, in0=ta, in1=tb, op=ALU.add), reads=[ka, kb], writes=list(dstk))

        def prefix_proj(slot, h, ptile):
            hb = h % 2
            kb = 3 * hb
            for c in range(NCH):
                mm_group(bank(kb)[:, c * 128:(c + 1) * 128],
                         [(xT[:, kt, c * 128:(c + 1) * 128], wr[slot][:, kt, 0:128]) for kt in range(KT)],
                         ["xT.c%d.0" % c, "xT.c%d.1" % c] + wkeys(slot), bkh(kb, c // 2))
                vb = kb + 1 + c // 2
                mm_group(bank(vb)[:, (c % 2) * 256:(c % 2 + 1) * 256],
                         [(xT[:, kt, c * 128:(c + 1) * 128], wr[slot][:, kt, 128:384]) for kt in range(KT)],
                         ["xT.c%d.0" % c, "xT.c%d.1" % c] + wkeys(slot), bkh(vb, c % 2))
            k32 = qk32[hb][:].rearrange("p (c d) -> p c d", c=NCH)
            k32k = "qk32_%d" % hb
            for c in range(NCH):
                jj = ptile * NCH + c
                S.op("act", (lambda c, jj: lambda e: e.activation(out=k32[:, c, :], in_=bank(kb)[:, c * 128:(c + 1) * 128],
                                                                   func=AF.Identity, scale=dkp[:, jj, h:h + 1]))(c, jj),
                     reads=bkh(kb, c // 2) + ["dkp"], writes=[k32k])
            S.op("act", lambda e: e.activation(out=vh[hb][:], in_=PS[:, kb + 1:kb + 3, :].rearrange("p b (c e) -> p (b c) e", c=2),
                                               func=AF.Identity),
                 reads=bk(kb + 1) + bk(kb + 2), writes=["vh%d.%d" % (hb, c) for c in range(NCH)])
            ta = rta[hb][:].rearrange("p (c f) -> p c f", c=NCH)
            tb = rtb[hb][:].rearrange("p (c f) -> p c f", c=NCH)
            rotary(k32[:, :, 0:64], k32[:, :, 64:128], cs[:, :, 0:64], cs[:, :, 64:128],
                   qkt[hb][:, :, 0, 0:64], qkt[hb][:, :, 0, 64:128], ta, tb, k32k, ["qkt%d.0" % hb, "qkt%d.1" % hb], hb)

        def prefix_state(h):
            hb = h % 2
            for c in range(NCH):
                def fn(e, c=c):
                    return e.matmul(bank(6)[:, 0:256], lhsT=qkt[hb][:, c, 0, :], rhs=vh[hb][:, c, :],
                                    start=(c == 0), stop=(c == NCH - 1))
                S.op("pe", fn, reads=["qkt%d.0" % hb, "qkt%d.1" % hb, "vh%d.%d" % (hb, c)], writes=bkh(6, 0))
            S.op("dve", lambda e: e.tensor_tensor(out=R[:, h, :], in0=R[:, h, :], in1=bank(6)[:, 0:256], op=ALU.add),
                 reads=bkh(6, 0) + ["R%d" % h], writes=["R%d" % h])

        def prefix_step(slot, h, ptile):
            if h == 2:
                for c in range(NCH):
                    if ptile + 1 < NT:
                        x_dma(xp_d, "p", ptile + 1, c)
                    else:
                        x_dma(x_d, "m", 0, c)
            prefix_proj(slot, h, ptile)
            if h > 0:
                prefix_state(h - 1)

        def rstd_batch(stat_views, stat_keys, want_nb=False):
            n = len(stat_views)
            m, mk = next_sm()
            r, rk = next_sm()
            for c in range(n):
                S.op("dve", (lambda c: lambda e: e.bn_aggr(out=m[:, 2 * c:2 * c + 2], in_=stat_views[c]))(c),
                     reads=[stat_keys[c]], writes=[mk])
            var = m[:, 0:2 * n].rearrange("p (c t) -> p c t", t=2)[:, :, 1]
            mean = m[:, 0:2 * n].rearrange("p (c t) -> p c t", t=2)[:, :, 0]
            S.op("dve", lambda e: e.tensor_scalar(out=r[:, 0:n], in0=var, scalar1=EPS, scalar2=None, op0=ALU.add),
                 reads=[mk], writes=[rk])
            S.op("act", lambda e: e.activation(out=r[:, 0:n], in_=r[:, 0:n], func=AF.Sqrt), reads=[rk], writes=[rk])
            S.op("dve", lambda e: e.reciprocal(out=r[:, 0:n], in_=r[:, 0:n]), reads=[rk], writes=[rk])
            if want_nb:
                S.op("dve", lambda e: e.scalar_tensor_tensor(out=r[:, 4:4 + n], in0=mean, scalar=-1.0, in1=r[:, 0:n],
                                                             op0=ALU.mult, op1=ALU.mult), reads=[mk, rk], writes=[rk])
            return m, mk, r, rk

        def a1(slot, nb):
            for c in range(NCH):
                pf, pk = next_pf()
                mm_group(pf, [(xT[:, kt, c * 128:(c + 1) * 128], wr[slot][:, kt, :]) for kt in range(KT)],
                         ["xT.c%d.0" % c, "xT.c%d.1" % c] + wkeys(slot), pk)
                tf, tk = next_tmp()
                S.op("act", (lambda pf, tf: lambda e: e.activation(out=tf[:], in_=pf, func=AF.Gelu_apprx_tanh))(pf, tf),
                     reads=pk, writes=[tk])
                S.op("dve", (lambda tf, c: lambda e: e.bn_stats(out=stv[:, c, nb, :], in_=tf[:]))(tf, c),
                     reads=[tk], writes=["stv.%d" % c])
                S.op("dve", (lambda tf, c: lambda e: e.tensor_copy(out=vln[:, c, nb * 512:(nb + 1) * 512], in_=tf[:]))(tf, c),
                     reads=[tk], writes=["vln.%d" % c])

        def a2():
            m, mk, r, rk = rstd_batch([stv[:, c, :, :].rearrange("p a b -> p (a b)") for c in range(NCH)],
                                      ["stv.%d" % c for c in range(NCH)])
            for c in range(NCH):
                S.op("dve", (lambda c: lambda e: e.tensor_scalar(out=vln[:, c, :], in0=vln[:, c, :], scalar1=m[:, 2 * c:2 * c + 1],
                                                                  scalar2=r[:, c:c + 1], op0=ALU.subtract, op1=ALU.mult))(c),
                     reads=[mk, rk, "vln.%d" % c], writes=["vln.%d" % c])

        def a3(slot, j):
            for l in range(2):
                ct = 2 * j + l
                g = ct // 2
                pu, puk = next_pf()
                psv, psvk = next_pf()
                pz, pzk = next_pf()
                mm_group(pu, [(wr[slot][:, kt, l * 128:(l + 1) * 128], xT[:, kt, :]) for kt in range(KT)],
                         XT_K + wkeys(slot), puk)
                mm_each([(psv[:, c * 128:(c + 1) * 128], vln[:, c, ct * 128:(ct + 1) * 128], wsT[:, g, :]) for c in range(NCH)],
                        VLN_K + ["wsT"], psvk)
                mm_group(pz, [(wr[slot][:, kt, 256 + l * 128:256 + (l + 1) * 128], xT[:, kt, :]) for kt in range(KT)],
                         XT_K + wkeys(slot), pzk)
                tu, tuk = next_tmp()
                t2, t2k = next_tmp()
                S.op("act", (lambda pu, tu: lambda e: e.activation(out=tu[:], in_=pu, func=AF.Gelu_apprx_tanh))(pu, tu),
                     reads=puk, writes=[tuk])
                S.op("dve", (lambda psv, t2, ct: lambda e: e.scalar_tensor_tensor(
                    out=t2[:].rearrange("p (c t) -> p c t", c=NCH), in0=psv.rearrange("p (c t) -> p c t", c=NCH),
                    scalar=lnvg[:, ct:ct + 1], in1=bias2[:, ct, :].unsqueeze(1).to_broadcast([128, NCH, 128]),
                    op0=ALU.mult, op1=ALU.add))(psv, t2, ct),
                    reads=psvk + ["lnvg", "bias2"], writes=[t2k])
                S.op("dve", (lambda t2, tu: lambda e: e.tensor_tensor(out=t2[:], in0=t2[:], in1=tu[:], op=ALU.mult))(t2, tu),
                     reads=[t2k, tuk], writes=[t2k])
                S.op("act", (lambda pz, tu: lambda e: e.activation(out=tu[:], in_=pz, func=AF.Silu))(pz, tu),
                     reads=pzk, writes=[tuk])
                S.op("dve", (lambda t2, tu, ct: lambda e: e.tensor_tensor(out=preT[:, ct, :], in0=t2[:], in1=tu[:], op=ALU.mult))(t2, tu, ct),
                     reads=[t2k, tuk], writes=["preT.%d" % ct])

        def proj_fm(slot, j, srcT, src_keys, dstT, dst_prefix):
            for l in range(4):
                dt_ = 4 * j + l
                pf, pk = next_pf()
                mm_group(pf, [(wr[slot][:, kt, l * 128:(l + 1) * 128], srcT[:, kt, :]) for kt in range(KT)],
                         src_keys + wkeys(slot), pk)
                copy_op(ev_eng(), dstT[:, dt_, :], pf, pk, ["%s.%d" % (dst_prefix, dt_)])

        def qbank(h, c):
            return (4 * h + c) % 3

        def P_gemm(h, c, sq, mid=None, split=12):
            qb = qbank(h, c)
            pairs = [(xT[:, kt, c * 128:(c + 1) * 128], wr[sq][:, kt, :]) for kt in range(KT)]
            rk_ = ["xT.c%d.0" % c, "xT.c%d.1" % c] + wkeys(sq)
            if mid is None:
                mm_group(bank(qb), pairs, rk_, bk(qb))
                return

            def part(lo, hi):
                def fn(e):
                    ins = None
                    for i in range(lo, hi):
                        ins = e.matmul(bank(qb), lhsT=pairs[i][0], rhs=pairs[i][1], start=(i == 0), stop=(i == KT - 1))
                    return ins
                S.op("pe", fn, reads=rk_, writes=bk(qb))
            part(0, split)
            mid()
            part(split, KT)

        def P_evac(h, c):
            hb = h % 2
            half = c // 2
            qb = qbank(h, c)
            q32 = qk32[half][:].rearrange("p (c j d) -> p c j d", c=2, j=2)
            q32k = "qk32_%d" % half
            S.op("act", lambda e: e.activation(out=q32[:, c % 2, 0, :], in_=bank(qb)[:, 0:128], func=AF.Identity, scale=dec[:, h:h + 1]),
                 reads=bk(qb) + ["dec"], writes=[q32k])
            S.op("act", lambda e: e.activation(out=q32[:, c % 2, 1, :], in_=bank(qb)[:, 128:256], func=AF.Identity, scale=dec[:, 8 + h:9 + h]),
                 reads=bk(qb) + ["dec"], writes=[q32k])
            S.op("act", lambda e: e.activation(out=vh[hb][:, c, :], in_=bank(qb)[:, 256:512], func=AF.Identity),
                 reads=bk(qb), writes=["vh%d.%d" % (hb, c)])

        def P_post(h, half):
            hb = h % 2
            c0 = 2 * half
            q32 = qk32[half][:].rearrange("p (c j d) -> p c j d", c=2, j=2)
            q32k = "qk32_%d" % half
            ta = rta[half][:].rearrange("p (c j f) -> p c j f", c=2, j=2)
            tb = rtb[half][:].rearrange("p (c j f) -> p c j f", c=2, j=2)
            cosv = cs[:, c0:c0 + 2, 0:64].unsqueeze(2).to_broadcast([128, 2, 2, 64])
            sinv = cs[:, c0:c0 + 2, 64:128].unsqueeze(2).to_broadcast([128, 2, 2, 64])
            rotary(q32[:, :, :, 0:64], q32[:, :, :, 64:128], cosv, sinv,
                   qkt[hb][:, c0:c0 + 2, :, 0:64], qkt[hb][:, c0:c0 + 2, :, 64:128], ta, tb, q32k, ["qkt%d.%d" % (hb, half)], half)

        def T_a(h, half):
            hb = h % 2
            c0 = 2 * half
            pb = bank16(6)
            tr_group([(pb[:, half * 512 + (ci * 2 + j) * 128:half * 512 + (ci * 2 + j + 1) * 128], qkt[hb][:, c0 + ci, j, :])
                      for ci in range(2) for j in range(2)], ["qkt%d.%d" % (hb, half)], bkh(6, half))
            copy_op("act", qkT[hb][:, :, c0 * 128:(c0 + 2) * 128].rearrange("p j (c t) -> p c j t", c=2),
                    pb[:, half * 512:(half + 1) * 512].rearrange("p (c j t) -> p c j t", c=2, j=2),
                    bkh(6, half), ["qkT%d.%d" % (hb, half)])

        def T_b(h, half):
            hb = h % 2
            c0 = 2 * half
            mm_each([(bank(4)[:, c * 128:(c + 1) * 128], qkT[hb][:, 1, c * 128:(c + 1) * 128], qkT[hb][:, 0, c * 128:(c + 1) * 128])
                     for c in (c0, c0 + 1)], ["qkT%d.%d" % (hb, half)], bkh(4, half))
            S.op("dve", lambda e: e.tensor_tensor(out=ST[hb][:, c0:c0 + 2, :],
                                                  in0=bank(4)[:, c0 * 128:(c0 + 2) * 128].rearrange("p (c t) -> p c t", c=2),
                                                  in1=maskT[:].unsqueeze(1).to_broadcast([128, 2, 128]), op=ALU.mult),
                 reads=bkh(4, half) + ["maskT"], writes=["ST%d.%d" % (hb, half)])

        def R_stage(h, c):
            hb = h % 2
            half = c // 2

            def fn_o(e):
                e.matmul(bank(5)[:, 0:256], lhsT=ST[hb][:, c, :], rhs=vh[hb][:, c, :], start=True, stop=False)
                return e.matmul(bank(5)[:, 0:256], lhsT=qkT[hb][:, 0, c * 128:(c + 1) * 128], rhs=Rg[:, h, :], start=False, stop=True)
            S.op("pe", fn_o, reads=["ST%d.%d" % (hb, half), "vh%d.%d" % (hb, c), "qkT%d.%d" % (hb, half), "Rg%d" % h], writes=bk(5))
            mm_group(bank(3)[:, 0:256], [(qkt[hb][:, c, 1, :], vh[hb][:, c, :])],
                     ["qkt%d.%d" % (hb, half), "vh%d.%d" % (hb, c)], bk(3))
            s_c = float(gC1[h] / (gC[h] ** (c + 1)))
            S.op("dve", lambda e: e.scalar_tensor_tensor(out=R[:, h, :], in0=bank(3)[:, 0:256], scalar=s_c, in1=R[:, h, :],
                                                         op0=ALU.mult, op1=ALU.add),
                 reads=bk(3) + ["R%d" % h], writes=["R%d" % h])
            S.op("dve", lambda e: e.tensor_scalar(out=Rg[:, h, :], in0=R[:, h, :], scalar1=float(gam[h] * gC[h] ** (c + 1)),
                                                  scalar2=None, op0=ALU.mult),
                 reads=["R%d" % h], writes=["Rg%d" % h])
            if c == NCH - 1:
                S.op("dve", lambda e: e.tensor_scalar(out=R[:, h, :], in0=R[:, h, :], scalar1=float(gC[h] ** NCH),
                                                      scalar2=None, op0=ALU.mult),
                     reads=["R%d" % h], writes=["R%d" % h])
            S.op("act", lambda e: e.activation(out=on32[hb][:, c, :], in_=bank(5)[:, 0:256], func=AF.Identity),
                 reads=bk(5), writes=["on32_%d.%d" % (hb, c)])

        nst = {}

        def N1(h):
            hb = h % 2
            s4, s4k = st4[cnt["st4"] % 2], "st4_%d" % (cnt["st4"] % 2)
            cnt["st4"] += 1
            for c in range(NCH):
                S.op("dve", (lambda c: lambda e: e.bn_stats(out=s4[:, c, :], in_=on32[hb][:, c, :]))(c),
                     reads=["on32_%d.%d" % (hb, c)], writes=[s4k])
            m, mk = next_sm()
            r, rk = next_sm()
            for c in range(NCH):
                S.op("dve", (lambda c: lambda e: e.bn_aggr(out=m[:, 2 * c:2 * c + 2], in_=s4[:, c, :]))(c), reads=[s4k], writes=[mk])
            var = m[:, 0:8].rearrange("p (c t) -> p c t", t=2)[:, :, 1]
            S.op("dve", lambda e: e.tensor_scalar(out=r[:, 0:4], in0=var, scalar1=EPS, scalar2=None, op0=ALU.add), reads=[mk], writes=[rk])
            nst[h] = (m, mk, r, rk)

        def N2(h):
            m, mk, r, rk = nst[h]
            mean = m[:, 0:8].rearrange("p (c t) -> p c t", t=2)[:, :, 0]
            S.op("act", lambda e: e.activation(out=r[:, 0:4], in_=r[:, 0:4], func=AF.Sqrt), reads=[rk], writes=[rk])
            S.op("dve", lambda e: e.reciprocal(out=r[:, 0:4], in_=r[:, 0:4]), reads=[rk], writes=[rk])
            S.op("dve", lambda e: e.scalar_tensor_tensor(out=r[:, 4:8], in0=mean, scalar=-1.0, in1=r[:, 0:4],
                                                         op0=ALU.mult, op1=ALU.mult), reads=[mk, rk], writes=[rk])

        def N3(h):
            hb = h % 2
            m, mk, r, rk = nst[h]
            for c in range(NCH):
                S.op("act", (lambda c: lambda e: e.activation(out=ybp[hb][:, c, :], in_=on32[hb][:, c, :], func=AF.Identity,
                                                               bias=r[:, 4 + c:5 + c], scale=r[:, c:c + 1]))(c),
                     reads=[rk, "on32_%d.%d" % (hb, c)], writes=["ybp.%d" % c])

        def Y_stage(h):
            hb = h % 2
            pb = bank16(7)
            tr_group([(pb[:, (c * 2 + j) * 128:(c * 2 + j + 1) * 128], ybp[hb][:, c, j * 128:(j + 1) * 128])
                      for c in range(NCH) for j in range(2)], ["ybp.%d" % c for c in range(NCH)], bk(7))
            copy_op("act", preT[:, 2 * h:2 * h + 2, :].rearrange("p j (c t) -> p c j t", c=NCH),
                    pb.rearrange("p (c j t) -> p c j t", c=NCH, j=2), bk(7), ["preT.%d" % (2 * h), "preT.%d" % (2 * h + 1)])

        def bstep(slots, h, tile=None):
            sq = slots[0] if h < NH else None
            hr = h - 1
            if h == 1 and tile is not None and tile + 1 < NT:
                for c in range(NCH):
                    x_dma(x_d, "m", tile + 1, c)
            rv = 0 <= hr < NH
            for c in range(NCH):
                if h < NH:
                    if c == 1 and rv:
                        P_gemm(h, c, sq, mid=lambda: T_b(hr, 1), split=8)
                    else:
                        P_gemm(h, c, sq)
                        if c == 3:
                            T_b(h, 0)
                    P_evac(h, c)
                    if c % 2 == 1:
                        P_post(h, c // 2)
                else:
                    if c == 1 and rv:
                        T_b(hr, 1)
                if rv:
                    R_stage(hr, c)
                    if c == 0:
                        T_a(hr, 1)
                if c == 2 and h < NH:
                    T_a(h, 0)
                if h - 2 >= 0:
                    if c == 0:
                        N1(h - 2)
                    elif c == 1:
                        N2(h - 2)
                    elif c == 2:
                        N3(h - 2)
            if 0 <= h - 2 < NH:
                Y_stage(h - 2)

        def zstep(slot, j, extra=None, banks=None):
            for l in range(4):
                ct = 4 * j + l
                if banks is None:
                    pf, pk = next_pf()
                else:
                    bi_ = banks[l % len(banks)]
                    pf, pk = bank(bi_), bk(bi_)
                mm_group(pf, [(wr[slot][:, kt, l * 128:(l + 1) * 128], xT[:, kt, :]) for kt in range(KT)],
                         XT_K + wkeys(slot), pk)
                tz, tzk = next_tmp()
                S.op("act", (lambda pf, tz: lambda e: e.activation(out=tz[:], in_=pf, func=AF.Silu))(pf, tz), reads=pk, writes=[tzk])
                S.op("dve", (lambda tz, ct: lambda e: e.tensor_tensor(out=preT[:, ct, :], in0=preT[:, ct, :], in1=tz[:], op=ALU.mult))(tz, ct),
                     reads=[tzk, "preT.%d" % ct], writes=["preT.%d" % ct])
                if extra is not None:
                    extra(l)

        def drain_a(l):
            if l == 1:
                T_b(NH - 1, 1)
            R_stage(NH - 1, l)
            if l == 0:
                T_a(NH - 1, 1)
            if l == 0:
                N1(NH - 2)
            elif l == 1:
                N2(NH - 2)
            elif l == 2:
                N3(NH - 2)
            elif l == 3:
                Y_stage(NH - 2)

        def drain_b(l):
            if l == 0:
                N1(NH - 1)
            elif l == 1:
                N2(NH - 1)
            elif l == 2:
                N3(NH - 1)
            elif l == 3:
                Y_stage(NH - 1)

        def drain_c(l):
            pass

        def gstep(slot, j):
            for l in range(2):
                dt_ = 2 * j + l
                pa, pak = next_pf()
                pb_, pbk = next_pf()
                mm_group(pa, [(wr[slot][:, kt, l * 128:(l + 1) * 128], xT[:, kt, :]) for kt in range(KT)],
                         XT_K + wkeys(slot), pak)
                mm_group(pb_, [(wr[slot][:, kt, 256 + l * 128:256 + (l + 1) * 128], xT[:, kt, :]) for kt in range(KT)],
                         XT_K + wkeys(slot), pbk)
                ta, tak = next_tmp()
                tb, tbk = next_tmp()
                S.op("act", (lambda pa, ta, dt_: lambda e: e.activation(out=ta[:], in_=pa, func=AF.Sigmoid,
                                                                         bias=bgate[:, dt_:dt_ + 1], scale=1.0))(pa, ta, dt_),
                     reads=pak + ["bgate"], writes=[tak])
                S.op("act", (lambda pb_, tb, dt_: lambda e: e.activation(out=tb[:], in_=pb_, func=AF.Sigmoid,
                                                                          bias=bgate[:, 16 + dt_:17 + dt_], scale=1.0))(pb_, tb, dt_),
                     reads=pbk + ["bgate"], writes=[tbk])
                S.op("dve", (lambda ta, dt_: lambda e: e.tensor_tensor(out=ta[:], in0=ta[:], in1=yaT[:, dt_, :], op=ALU.mult))(ta, dt_),
                     reads=[tak, "yaT.%d" % dt_], writes=[tak])
                S.op("dve", (lambda tb, dt_: lambda e: e.tensor_tensor(out=tb[:], in0=tb[:], in1=ybT[:, dt_, :], op=ALU.mult))(tb, dt_),
                     reads=[tbk, "ybT.%d" % dt_], writes=[tbk])
                S.op("dve", (lambda ta, tb, dt_: lambda e: e.tensor_tensor(out=yaT[:, dt_, :], in0=ta[:], in1=tb[:], op=ALU.add))(ta, tb, dt_),
                     reads=[tak, tbk], writes=["yaT.%d" % dt_])

        def final_ln_all(tile):
            m, mk, r, rk = rstd_batch([stv[:, c, :, :].rearrange("p a b -> p (a b)") for c in range(NCH)],
                                      ["stv.%d" % c for c in range(NCH)], want_nb=True)
            for c in range(NCH):
                S.op("act", (lambda c: lambda e: e.activation(out=res[:, c, :], in_=res[:, c, :], func=AF.Identity,
                                                               bias=r[:, 4 + c:5 + c], scale=r[:, c:c + 1]))(c),
                     reads=[rk, "res.%d" % c], writes=["res.%d" % c])
            for c in range(NCH):
                S.op("dve", (lambda c: lambda e: e.tensor_tensor(out=res[:, c, :], in0=res[:, c, :], in1=lncg[:], op=ALU.mult))(c),
                     reads=["res.%d" % c, "lncg"], writes=["res.%d" % c])
                S.op("dve", (lambda c: lambda e: e.tensor_tensor(out=res[:, c, :], in0=res[:, c, :], in1=lncb[:], op=ALU.add))(c),
                     reads=["res.%d" % c, "lncb"], writes=["res.%d" % c])
                r0 = (tile * NCH + c) * 128
                S.op("sp", (lambda c, r0: lambda e: e.dma_start(out=out_d[r0:r0 + 128, :], in_=res[:, c, :]))(c, r0),
                     reads=["res.%d" % c], writes=["out.%d.%d" % (tile, c)], dma_key="out")

        def ostep(slot, nb, tile):
            for c in range(NCH):
                pf, pk = next_pf()
                mm_group(pf, [(yaT[:, kt, c * 128:(c + 1) * 128], wr[slot][:, kt, :]) for kt in range(KT)],
                         YAT_K + wkeys(slot), pk)
                S.op("dve", (lambda pf, c: lambda e: e.scalar_tensor_tensor(
                    out=res[:, c, nb * 512:(nb + 1) * 512], in0=res[:, c, nb * 512:(nb + 1) * 512], scalar=ALPHA, in1=pf,
                    op0=ALU.mult, op1=ALU.add))(pf, c),
                    reads=pk + ["res.%d" % c], writes=["res.%d" % c])
                S.op("dve", (lambda c: lambda e: e.bn_stats(out=stv[:, c, nb, :], in_=res[:, c, nb * 512:(nb + 1) * 512]))(c),
                     reads=["res.%d" % c], writes=["stv.%d" % c])
            if nb == 3:
                final_ln_all(tile)

        steps = []
        wmeta = []
        wctx = {"kind": "p", "tile": 0, "nid": 0}

        def W(pieces_list, fn):
            steps.append((pieces_list, fn))
            for _ in pieces_list:
                li = wctx["nid"]
                if wctx["kind"] == "p":
                    ctile = li % 2
                else:
                    ctile = 0 if ((li - NH) % 5) < 2 else 1
                t = wctx["tile"]
                mode = "load" if t < ctile else ("load_store" if t == ctile else "cached")
                wmeta.append((wctx["kind"], li, mode))
                wctx["nid"] += 1

        def Nw(fn):
            steps.append(([], fn))

        for ptile in range(NT):
            wctx.update(kind="p", tile=ptile, nid=0)
            Nw((lambda ptile: lambda s: (build_xT(xp_d, "p", ptile), load_cs(cs_pre_d, ptile)))(ptile))
            for h in range(NH):
                W([[(0, w_in[:, C_K + h * 128:C_K + (h + 1) * 128]), (128, w_in[:, C_RV + h * 256:C_RV + (h + 1) * 256])]],
                  (lambda h, ptile: lambda s: prefix_step(s[0], h, ptile))(h, ptile))
            Nw(lambda s: prefix_state(NH - 1))

        def after_prefix(s):
            for h in range(NH):
                S.op("act", (lambda h: lambda e: e.activation(out=Rg[:, h, :], in_=R[:, h, :], func=AF.Identity, scale=float(gam[h])))(h),
                     reads=["R%d" % h], writes=["Rg%d" % h])
        Nw(after_prefix)

        for tile in range(NT):
            wctx.update(kind="m", tile=tile, nid=NH)
            Nw((lambda tile: lambda s: (build_xT(x_d, "m", tile), load_cs(cs_main_d, tile)))(tile))
            for nb in range(4):
                W([[(0, w_in[:, C_V + nb * 512:C_V + (nb + 1) * 512])]], (lambda nb: lambda s: a1(s[0], nb))(nb))
            Nw(lambda s: a2())
            for j in range(8):
                W([[(0, w_in[:, C_U + j * 256:C_U + (j + 1) * 256]), (256, w_in[:, C_Z + j * 256:C_Z + (j + 1) * 256])]],
                  (lambda j: lambda s: a3(s[0], j))(j))
            Nw(lambda s: alias(VLN_K, YBT_K))
            for j in range(4):
                W([[(0, w_oa[:, j * 512:(j + 1) * 512])]], (lambda j: lambda s: proj_fm(s[0], j, preT, PRE_K, yaT, "yaT"))(j))
            for h in range(NH):
                W([[(0, w_in[:, C_Q + h * 128:C_Q + (h + 1) * 128]), (128, w_in[:, C_K + h * 128:C_K + (h + 1) * 128]),
                    (256, w_in[:, C_RV + h * 256:C_RV + (h + 1) * 256])]],
                  (lambda h, tile: lambda s: bstep(s, h, tile))(h, tile))
            zx = [(drain_a, [0, 1, 2]), (drain_b, [0, 1, 2]), (drain_c, [0, 1, 2]), (None, None)]
            for j in range(4):
                W([[(0, w_in[:, C_RZ + j * 512:C_RZ + (j + 1) * 512])]],
                  (lambda j: lambda s: zstep(s[0], j, extra=zx[j][0], banks=zx[j][1]))(j))
            for j in range(4):
                W([[(0, w_ob[:, j * 512:(j + 1) * 512])]], (lambda j: lambda s: proj_fm(s[0], j, preT, PRE_K, ybT, "ybT"))(j))
            for j in range(8):
                W([[(0, w_in[:, C_GA + j * 256:C_GA + (j + 1) * 256]), (256, w_in[:, C_GB + j * 256:C_GB + (j + 1) * 256])]],
                  (lambda j: lambda s: gstep(s[0], j))(j))

            def pre_o(s, tile=tile):
                alias(YBT_K + PRE_K, RES_K)
                for c in range(NCH):
                    r0 = (tile * NCH + c) * 128
                    S.op("sp", (lambda c, r0: lambda e: e.dma_start(out=res[:, c, :], in_=x_d[r0:r0 + 128, :]))(c, r0),
                         writes=["res.%d" % c], dma_key="res.%d" % c)
                if tile + 1 < NT:
                    for c in range(NCH):
                        x_dma(x_d, "m", tile + 1, c)
            Nw(pre_o)
            for nb in range(4):
                W([[(0, w_out[:, nb * 512:(nb + 1) * 512])]], (lambda nb, tile: lambda s: ostep(s[0], nb, tile))(nb, tile))
            if tile + 1 < NT:
                Nw(lambda s: alias(RES_K, VLN_K + PRE_K))

        wblocks = []
        for (pl, fn) in steps:
            for pieces in pl:
                wblocks.append(pieces)

        NCACHE = NH + 44
        wsc = nc.dram_tensor("wsc", [NCACHE, 128, KT * 512], BF16).ap()

        def issue_load(bi):
            slot = bi % NSLOT
            kind, cid, mode = wmeta[bi]
            assert cid < NCACHE
            flat = wr[slot][:].rearrange("p k n -> p (k n)")
            if mode == "cached":
                S.op("pool", lambda e: e.dma_start(out=flat, in_=wsc[cid]), reads=["wsc.%d" % cid], writes=wkeys(slot),
                     dma_key="wr%d" % slot, dma_group=bi)
                return
            pi = 0
            for (off, src) in wblocks[bi]:
                n = src.shape[1]
                srcv = src.rearrange("(kt p) n -> p kt n", p=128)
                nparts = 4 if n >= 256 else 2
                kper = KT // nparts
                for q in range(nparts):
                    S.op("pool", (lambda slot, off, n, srcv, q, kper: lambda e: e.dma_start(
                        out=wr[slot][:, q * kper:(q + 1) * kper, off:off + n], in_=srcv[:, q * kper:(q + 1) * kper, :]))(slot, off, n, srcv, q, kper),
                        writes=["wr%d.%d" % (slot, pi)], dma_key="wr%d" % slot, dma_group=bi)
                    pi += 1
            assert pi <= MAXP
            if mode != "load_store":
                return
            S.op("sp", lambda e: e.dma_start(out=wsc[cid], in_=flat), reads=wkeys(slot), writes=["wsc.%d" % cid],
                 dma_key="wsc_st%d" % slot)

        issued = 0
        bi = 0
        for (pl, fn) in steps:
            limit = min(len(wblocks), bi + NSLOT)
            while issued < limit:
                issue_load(issued)
                issued += 1
            slots = [(bi + k) % NSLOT for k in range(len(pl))]
            fn(slots)
            bi += len(pl)

        S.op("sp", None, reads=["out.%d.%d" % (t_, c_) for t_ in range(NT) for c_ in range(NCH)])
        build_program.sbuf_left = nc.sbuf_bytes_remaining
        S.emit()
        build_program.stats = S.stats
    return nc


def _tables(hf):
    lg = np.log1p(-np.exp2(-5.0 - np.arange(NH, dtype=np.float64)))
    freqs = (10000.0 ** (-np.arange(0, 128, 2, dtype=np.float32) / np.float32(128))).astype(np.float32)
    t = np.arange(128, dtype=np.float64)

    def cs_table(base):
        pos = (base + np.arange(TOK)).astype(np.float32)
        ang = (pos[:, None] * freqs[None, :]).astype(np.float32)
        c = np.cos(ang.astype(np.float64)).astype(np.float32)
        s = np.sin(ang.astype(np.float64)).astype(np.float32)
        tab = np.concatenate([c, s], axis=1).reshape(16, 128, 128).transpose(1, 0, 2)
        return np.ascontiguousarray(tab, dtype=np.float32)

    cs_main = cs_table(hf * TOK)
    cs_pre = cs_table(0)
    dec = np.zeros((128, 16), np.float32)
    dec[:, 0:8] = np.exp(lg[None, :] * t[:, None])
    dec[:, 8:16] = np.exp(-lg[None, :] * t[:, None]) * (128.0 ** -0.5)
    s_glob = np.arange(TOK, dtype=np.float64)
    dkp = (np.exp(lg[None, :] * (TOK - 1 - s_glob)[:, None]) * (128.0 ** -0.5)).astype(np.float32)
    dkp = np.ascontiguousarray(dkp.reshape(16, 128, 8).transpose(1, 0, 2))
    maskT = (np.arange(128)[None, :] >= np.arange(128)[:, None]).astype(np.float32)
    mask_ts = np.ascontiguousarray(maskT.T)
    return dict(cs_main=cs_main, cs_pre=cs_pre, dec=dec, dkp=dkp, maskT=maskT, mask_ts=mask_ts)


_NC_CACHE = {}


def kernel(x, w_in, b_gate, ln_v_g, ln_v_b, w_s, b_s, w_oa, w_ob, w_out, ln_g, ln_b):
    x = np.asarray(x, dtype=np.float32)
    B, SEQ, _ = x.shape
    if "nc" not in _NC_CACHE:
        _NC_CACHE["nc"] = build_program()
    nc = _NC_CACHE["nc"]
    shared = dict(
        w_in=np.ascontiguousarray(np.asarray(w_in, np.float32)[0]),
        w_oa=np.ascontiguousarray(np.asarray(w_oa, np.float32)[0]),
        w_ob=np.ascontiguousarray(np.asarray(w_ob, np.float32)[0]),
        w_out=np.ascontiguousarray(np.asarray(w_out, np.float32)[0]),
        b_gate=np.ascontiguousarray(np.asarray(b_gate, np.float32)[0]),
        ln_v_g=np.ascontiguousarray(np.asarray(ln_v_g, np.float32)[0]),
        ln_v_b=np.ascontiguousarray(np.asarray(ln_v_b, np.float32)[0]),
        w_s=np.ascontiguousarray(np.asarray(w_s, np.float32)[0]),
        b_s=np.ascontiguousarray(np.asarray(b_s, np.float32)[0]),
        ln_g=np.ascontiguousarray(np.asarray(ln_g, np.float32)[0]),
        ln_b=np.ascontiguousarray(np.asarray(ln_b, np.float32)[0]),
    )
    tabs = [_tables(0), _tables(1)]
    zeros = np.zeros((TOK, D), np.float32)
    in_maps = []
    for c in range(NCORES):
        b, hf = c // 2, c % 2
        m = dict(shared)
        m["x"] = np.ascontiguousarray(x[b, hf * TOK:(hf + 1) * TOK, :])
        m["xp"] = np.ascontiguousarray(x[b, 0:TOK, :]) if hf == 1 else zeros
        m.update(tabs[hf])
        in_maps.append(m)
    res = run_bass_kernel_spmd(nc, in_maps, core_ids=list(range(NCORES)))
    out = np.empty((B, SEQ, D), np.float32)
    for c in range(NCORES):
        b, hf = c // 2, c % 2
        out[b, hf * TOK:(hf + 1) * TOK, :] = res.results[c]["out"]
    return out
```
